# Optimizing a Trainium2 kernel written in Bass

```python
import jax
import jax.numpy as jnp
from jax import lax
import numpy as np

D_MODEL = 1024
BATCH = 8
SEQ = 4096
DEPTH = 2

N_MIXERS = 2
HEAD_DIM = 64
MIX_HEADS = 12
MEM_HEADS = 4
MIX_WIDTH = MIX_HEADS * HEAD_DIM
MEM_WIDTH = MEM_HEADS * HEAD_DIM
MEM_TOKENS = 256
MLA_Q_RANK = 384
MLA_KV_RANK = 256
MLA_NOPE_DIM = 64
MLA_ROPE_DIM = 32
MLA_V_DIM = 64
MLA_QK_DIM = MLA_NOPE_DIM + MLA_ROPE_DIM
ROPE_BASE = 10000.0
Q_BLOCK = 128
RWKV_DECAY_RANK = 64
RWKV_A_RANK = 64
RWKV_GATE_RANK = 160
RWKV_GN_EPS = 64e-5
D_FF = 4 * D_MODEL
LN_EPS = 1e-5
RMS_EPS = 1e-6
ALPHA = (2.0 * DEPTH) ** 0.25
BETA = (8.0 * DEPTH) ** -0.25
N_MLA_LAYERS = (DEPTH + 1) // 2
N_RWKV_LAYERS = DEPTH // 2
MLA_IN_COLS = MLA_Q_RANK + MLA_KV_RANK + MLA_ROPE_DIM + MEM_WIDTH
RWKV_TM_COLS = 3 * MIX_WIDTH + RWKV_DECAY_RANK + RWKV_A_RANK + RWKV_GATE_RANK
RWKV_IN_COLS = RWKV_TM_COLS + MEM_WIDTH

kernel_name = "mla_rwkv7_interleaved_deepnorm_memory"


def _split(t, sizes):
    out, start = [], 0
    for s in sizes:
        out.append(t[..., start:start + s])
        start += s
    return out


def _layer_norm(t, g, b):
    tf = t.astype(jnp.float32)
    mu = jnp.mean(tf, axis=-1, keepdims=True)
    var = jnp.mean(jnp.square(tf - mu), axis=-1, keepdims=True)
    y = (tf - mu) * lax.rsqrt(var + LN_EPS)
    return (y * g.astype(jnp.float32) + b.astype(jnp.float32)).astype(t.dtype)


def _rms_norm(t, g):
    tf = t.astype(jnp.float32)
    y = tf * lax.rsqrt(jnp.mean(jnp.square(tf), axis=-1, keepdims=True) + RMS_EPS)
    return (y * g.astype(jnp.float32)).astype(t.dtype)


def _apply_rope(t, cos, sin):
    half = t.shape[-1] // 2
    t1, t2 = t[..., :half], t[..., half:]
    cos = cos.astype(t.dtype)
    sin = sin.astype(t.dtype)
    return jnp.concatenate([t1 * cos - t2 * sin, t1 * sin + t2 * cos], axis=-1)


def _token_shift(t):
    return jnp.pad(t, ((0, 0), (1, 0), (0, 0)))[:, :-1]


def _memory_attention(q, mem_k, mem_v):
    s = jnp.einsum("bthd,bmhd->bhtm", q, mem_k).astype(jnp.float32) * (HEAD_DIM ** -0.5)
    p = jax.nn.softmax(s, axis=-1).astype(mem_v.dtype)
    return jnp.einsum("bhtm,bmhd->bthd", p, mem_v)


def _causal_mla_attention(q_nope, q_rope, k_nope, k_rope, v):
    B, T, H, _ = q_nope.shape
    n_blocks = T // Q_BLOCK
    scale = MLA_QK_DIM ** -0.5
    key_pos = jnp.arange(T)

    def to_blocks(t):
        return jnp.moveaxis(t.reshape((B, n_blocks, Q_BLOCK) + t.shape[2:]), 1, 0)

    def one_block(args):
        qn, qr, start = args
        s = (jnp.einsum("bqhd,bkhd->bhqk", qn, k_nope)
             + jnp.einsum("bqhd,bkd->bhqk", qr, k_rope)).astype(jnp.float32) * scale
        q_pos = start + jnp.arange(Q_BLOCK)
        s = jnp.where(key_pos[None, :] <= q_pos[:, None], s, -1e30)
        p = jax.nn.softmax(s, axis=-1).astype(v.dtype)
        return jnp.einsum("bhqk,bkhd->bqhd", p, v)

    out = lax.map(one_block, (to_blocks(q_nope), to_blocks(q_rope), jnp.arange(n_blocks) * Q_BLOCK))
    return jnp.moveaxis(out, 0, 1).reshape(B, T, H, v.shape[-1])


def _rwkv7_scan(r, w, k, v, a, b):
    B, T, H, N = r.shape

    def step(S, inp):
        r_t, w_t, k_t, v_t, a_t, b_t = inp
        sa = jnp.einsum("bhij,bhj->bhi", S, a_t)
        S = S * w_t[:, :, None, :] + sa[..., None] * b_t[:, :, None, :] + v_t[..., None] * k_t[:, :, None, :]
        y = jnp.einsum("bhij,bhj->bhi", S, r_t)
        return S, y

    xs = tuple(jnp.moveaxis(t, 1, 0) for t in (r, w, k, v, a, b))
    S0 = jnp.zeros((B, H, N, N), jnp.float32)
    _, ys = lax.scan(step, S0, xs)
    return jnp.moveaxis(ys, 0, 1)


def _mla_mixer(h, cos, sin, mem_k, mem_v, w_in, q_norm, w_q_up, kv_norm, w_kv_up, w_out):
    B, T, _ = h.shape
    c_q, c_kv, k_rope, q_mem = _split(h @ w_in, (MLA_Q_RANK, MLA_KV_RANK, MLA_ROPE_DIM, MEM_WIDTH))
    q = (_rms_norm(c_q, q_norm) @ w_q_up).reshape(B, T, MIX_HEADS, MLA_QK_DIM)
    q_nope, q_rope = q[..., :MLA_NOPE_DIM], q[..., MLA_NOPE_DIM:]
    kv = (_rms_norm(c_kv, kv_norm) @ w_kv_up).reshape(B, T, MIX_HEADS, MLA_NOPE_DIM + MLA_V_DIM)
    k_nope, v = kv[..., :MLA_NOPE_DIM], kv[..., MLA_NOPE_DIM:]
    q_rope = _apply_rope(q_rope, cos[:, :, None, :], sin[:, :, None, :])
    k_rope = _apply_rope(k_rope, cos, sin)
    o_mix = _causal_mla_attention(q_nope, q_rope, k_nope, k_rope, v)
    o_mem = _memory_attention(q_mem.reshape(B, T, MEM_HEADS, HEAD_DIM), mem_k, mem_v)
    o = jnp.concatenate([o_mix.reshape(B, T, MIX_WIDTH), o_mem.reshape(B, T, MEM_WIDTH)], axis=-1)
    return o @ w_out


def _rwkv7_mixer(h, mem_k, mem_v, w_in, mu, w0, w2, a0, a2, g2, k_k, k_a, r_k, gn_g, gn_b, w_out):
    B, T, _ = h.shape
    f32 = jnp.float32
    proj = h @ w_in
    p_tm, q_mem = proj[..., :RWKV_TM_COLS], proj[..., RWKV_TM_COLS:]
    p_tm = p_tm + (_token_shift(p_tm) - p_tm) * mu
    r, k, v, hw, ha, hg = _split(p_tm, (MIX_WIDTH, MIX_WIDTH, MIX_WIDTH, RWKV_DECAY_RANK, RWKV_A_RANK, RWKV_GATE_RANK))
    w_log = -jax.nn.softplus(-(w0 + jnp.tanh(hw) @ w2).astype(f32)) - 0.5
    decay = jnp.exp(-jnp.exp(w_log))
    a = jax.nn.sigmoid((a0 + ha @ a2).astype(f32))
    g = jax.nn.sigmoid(hg) @ g2

    def heads(t):
        return t.reshape(B, T, MIX_HEADS, HEAD_DIM)

    k = k.astype(f32)
    kk = heads(k * k_k.astype(f32))
    kk = kk / jnp.maximum(jnp.sqrt(jnp.sum(jnp.square(kk), axis=-1, keepdims=True)), 1e-12)
    k = heads(k * (1.0 + (a - 1.0) * k_a.astype(f32)))
    r = heads(r.astype(f32))
    v = heads(v.astype(f32))
    a = heads(a)
    y = _rwkv7_scan(r, heads(decay), k, v, -kk, kk * a)
    mean = jnp.mean(y, axis=-1, keepdims=True)
    var = jnp.mean(jnp.square(y - mean), axis=-1, keepdims=True)
    y = (y - mean) * lax.rsqrt(var + RWKV_GN_EPS)
    y = y * gn_g.astype(f32).reshape(MIX_HEADS, HEAD_DIM) + gn_b.astype(f32).reshape(MIX_HEADS, HEAD_DIM)
    y = y + jnp.sum(r * k * r_k.astype(f32), axis=-1, keepdims=True) * v
    y = y.reshape(B, T, MIX_WIDTH).astype(h.dtype) * g
    o_mem = _memory_attention(q_mem.reshape(B, T, MEM_HEADS, HEAD_DIM), mem_k, mem_v)
    o = jnp.concatenate([y, o_mem.reshape(B, T, MEM_WIDTH)], axis=-1)
    return o @ w_out


def setup_inputs(seed: int = 0) -> dict:
    key = jax.random.key(seed)
    ks = iter(jax.random.split(key, 48))

    def nrm(shape, scale):
        return scale * jax.random.normal(next(ks), shape, jnp.float32)

    NA, NB, D = N_MLA_LAYERS, N_RWKV_LAYERS, D_MODEL
    x = nrm((BATCH, SEQ, D), 1.0)
    mem = nrm((BATCH, MEM_TOKENS, D), 1.0)
    positions = jnp.broadcast_to(jnp.arange(SEQ, dtype=jnp.int32)[None, :], (BATCH, SEQ))
    mem_ln_g = 1.0 + nrm((D,), 0.02)
    mem_ln_b = nrm((D,), 0.02)
    w_mem_kv = jnp.concatenate([nrm((D, MEM_WIDTH), D ** -0.5), nrm((D, MEM_WIDTH), BETA * D ** -0.5)], axis=-1)
    mla_w_in = nrm((NA, D, MLA_IN_COLS), D ** -0.5)
    mla_q_norm = 1.0 + nrm((NA, MLA_Q_RANK), 0.02)
    mla_w_q_up = nrm((NA, MLA_Q_RANK, MIX_HEADS * MLA_QK_DIM), MLA_Q_RANK ** -0.5)
    mla_kv_norm = 1.0 + nrm((NA, MLA_KV_RANK), 0.02)
    kv_k = nrm((NA, MLA_KV_RANK, MIX_HEADS, MLA_NOPE_DIM), MLA_KV_RANK ** -0.5)
    kv_v = nrm((NA, MLA_KV_RANK, MIX_HEADS, MLA_V_DIM), BETA * MLA_KV_RANK ** -0.5)
    mla_w_kv_up = jnp.concatenate([kv_k, kv_v], axis=-1).reshape(NA, MLA_KV_RANK, MIX_HEADS * (MLA_NOPE_DIM + MLA_V_DIM))
    rwkv_w_in = jnp.concatenate([
        nrm((NB, D, 2 * MIX_WIDTH), D ** -0.5),
        nrm((NB, D, MIX_WIDTH), BETA * D ** -0.5),
        nrm((NB, D, RWKV_DECAY_RANK + RWKV_A_RANK + RWKV_GATE_RANK + MEM_WIDTH), D ** -0.5),
    ], axis=-1)
    rwkv_mu = jax.random.uniform(next(ks), (NB, RWKV_TM_COLS), jnp.float32)
    rwkv_w0 = jnp.broadcast_to(jnp.linspace(-6.0, -1.0, MIX_WIDTH, dtype=jnp.float32), (NB, MIX_WIDTH)) + nrm((NB, MIX_WIDTH), 0.1)
    rwkv_w2 = nrm((NB, RWKV_DECAY_RANK, MIX_WIDTH), 0.5 * RWKV_DECAY_RANK ** -0.5)
    rwkv_a0 = nrm((NB, MIX_WIDTH), 0.1)
    rwkv_a2 = nrm((NB, RWKV_A_RANK, MIX_WIDTH), RWKV_A_RANK ** -0.5)
    rwkv_g2 = nrm((NB, RWKV_GATE_RANK, MIX_WIDTH), RWKV_GATE_RANK ** -0.5)
    rwkv_k_k = 0.85 + nrm((NB, MIX_WIDTH), 0.05)
    rwkv_k_a = 1.0 + nrm((NB, MIX_WIDTH), 0.05)
    rwkv_r_k = nrm((NB, MIX_HEADS, HEAD_DIM), 0.1)
    rwkv_gn_g = 1.0 + nrm((NB, MIX_WIDTH), 0.02)
    rwkv_gn_b = nrm((NB, MIX_WIDTH), 0.02)
    w_out = nrm((DEPTH, MIX_WIDTH + MEM_WIDTH, D), BETA * (MIX_WIDTH + MEM_WIDTH) ** -0.5)
    ln1_g = 1.0 + nrm((DEPTH, D), 0.02)
    ln1_b = nrm((DEPTH, D), 0.02)
    w_ff1 = nrm((DEPTH, D, D_FF), D ** -0.5)
    w_ff2 = nrm((DEPTH, D_FF, D), BETA * D_FF ** -0.5)
    ln2_g = 1.0 + nrm((DEPTH, D), 0.02)
    ln2_b = nrm((DEPTH, D), 0.02)
    return {
        "x": x, "mem": mem, "positions": positions,
        "mem_ln_g": mem_ln_g, "mem_ln_b": mem_ln_b, "w_mem_kv": w_mem_kv,
        "mla_w_in": mla_w_in, "mla_q_norm": mla_q_norm, "mla_w_q_up": mla_w_q_up,
        "mla_kv_norm": mla_kv_norm, "mla_w_kv_up": mla_w_kv_up,
        "rwkv_w_in": rwkv_w_in, "rwkv_mu": rwkv_mu, "rwkv_w0": rwkv_w0, "rwkv_w2": rwkv_w2,
        "rwkv_a0": rwkv_a0, "rwkv_a2": rwkv_a2, "rwkv_g2": rwkv_g2, "rwkv_k_k": rwkv_k_k,
        "rwkv_k_a": rwkv_k_a, "rwkv_r_k": rwkv_r_k, "rwkv_gn_g": rwkv_gn_g, "rwkv_gn_b": rwkv_gn_b,
        "w_out": w_out, "ln1_g": ln1_g, "ln1_b": ln1_b,
        "w_ff1": w_ff1, "w_ff2": w_ff2, "ln2_g": ln2_g, "ln2_b": ln2_b,
    }


def reference(x, mem, positions, mem_ln_g, mem_ln_b, w_mem_kv,
              mla_w_in, mla_q_norm, mla_w_q_up, mla_kv_norm, mla_w_kv_up,
              rwkv_w_in, rwkv_mu, rwkv_w0, rwkv_w2, rwkv_a0, rwkv_a2, rwkv_g2,
              rwkv_k_k, rwkv_k_a, rwkv_r_k, rwkv_gn_g, rwkv_gn_b,
              w_out, ln1_g, ln1_b, w_ff1, w_ff2, ln2_g, ln2_b):
    B, M = mem.shape[0], mem.shape[1]
    mem_kv = _layer_norm(mem, mem_ln_g, mem_ln_b) @ w_mem_kv
    mem_k = mem_kv[..., :MEM_WIDTH].reshape(B, M, MEM_HEADS, HEAD_DIM)
    mem_v = mem_kv[..., MEM_WIDTH:].reshape(B, M, MEM_HEADS, HEAD_DIM)
    half = MLA_ROPE_DIM // 2
    inv_freq = ROPE_BASE ** (-jnp.arange(half, dtype=jnp.float32) * 2.0 / MLA_ROPE_DIM)
    ang = positions.astype(jnp.float32)[..., None] * inv_freq
    cos, sin = jnp.cos(ang), jnp.sin(ang)

    for i in range(DEPTH):
        j = i // N_MIXERS
        if i % N_MIXERS == 0:
            mix = _mla_mixer(x, cos, sin, mem_k, mem_v, mla_w_in[j], mla_q_norm[j], mla_w_q_up[j],
                             mla_kv_norm[j], mla_w_kv_up[j], w_out[i])
        else:
            mix = _rwkv7_mixer(x, mem_k, mem_v, rwkv_w_in[j], rwkv_mu[j], rwkv_w0[j], rwkv_w2[j],
                               rwkv_a0[j], rwkv_a2[j], rwkv_g2[j], rwkv_k_k[j], rwkv_k_a[j],
                               rwkv_r_k[j], rwkv_gn_g[j], rwkv_gn_b[j], w_out[i])
        x = _layer_norm(ALPHA * x + mix, ln1_g[i], ln1_b[i])
        ff = jnp.square(jax.nn.relu(x @ w_ff1[i])) @ w_ff2[i]
        x = _layer_norm(ALPHA * x + ff, ln2_g[i], ln2_b[i])
    return x
```

```python
import contextlib
import numpy as np
import concourse.bass as bass
import concourse.mybir as mybir
from concourse.bass_utils import run_bass_kernel_spmd

F32 = mybir.dt.float32
BF16 = mybir.dt.bfloat16
I32 = mybir.dt.int32
AF = mybir.ActivationFunctionType
ALU = mybir.AluOpType
AX = mybir.AxisListType

T = 4096
D = 1024
NT = T // 128
ALPHA = 4.0 ** 0.25
LN_EPS = 1e-5
RMS_EPS = 1e-6
GN_EPS = 64e-5
COMPUTE = ("pe", "act", "dve", "pool")
SKIP = set()


class Buf:
    __slots__ = ("name", "writers", "readers", "dsem", "excl", "base")

    def __init__(self, name, excl=False):
        self.name = name
        self.excl = excl
        self.base = []
        self.writers = []
        self.readers = []
        self.dsem = None


class Op:
    __slots__ = ("eng", "fn", "deps", "ddeps", "signal", "val", "is_dma", "dtok", "epoch")

    def __init__(self, eng, fn, epoch):
        self.eng = eng
        self.fn = fn
        self.deps = []
        self.ddeps = {}
        self.signal = False
        self.val = None
        self.is_dma = False
        self.dtok = None
        self.epoch = epoch


class Sched:
    def __init__(self, nc, es, n_dma_sems=40):
        self.nc = nc
        self.sem = {e: es.enter_context(nc.semaphore("s_" + e)) for e in COMPUTE}
        self.count = {e: 0 for e in COMPUTE}
        self.dsems = [es.enter_context(nc.semaphore("d%d" % i)) for i in range(n_dma_sems)]
        self.dtotal = [0] * n_dma_sems
        self.dnext = 0
        self.epoch = 0
        self.ops = {e: [] for e in ("pe", "act", "dve", "pool", "sp")}
        self.nops = 0
        self.phase_log = []

    def _dep(self, op, tok):
        if isinstance(tok, Op):
            if tok.epoch != self.epoch:
                return
            if tok.eng == "pe" and op.eng == "pe" and not op.is_dma:
                return
            tok.signal = True
            op.deps.append(tok)
        else:
            s, _, ep = tok
            if ep != self.epoch:
                return
            op.ddeps[s] = self.dtotal[s]

    @staticmethod
    def _push(lst, tok):
        if lst:
            last = lst[-1]
            if isinstance(tok, Op) and isinstance(last, Op) and last.eng == tok.eng:
                lst[-1] = tok
                return
            if (not isinstance(tok, Op)) and (not isinstance(last, Op)) and last[0] == tok[0]:
                lst[-1] = tok
                return
        lst.append(tok)

    def _record(self, op, reads, writes, partial):
        for b in reads:
            for t in b.writers:
                self._dep(op, t)
            if b.excl:
                for t in b.readers:
                    self._dep(op, t)
        newgen = {}
        for b in writes:
            if b.readers or not partial:
                base = list(b.readers) + list(b.writers)
                newgen[id(b)] = base
                for t in base:
                    self._dep(op, t)
            else:
                for t in b.base:
                    self._dep(op, t)
        if op.is_dma:
            s_ = op.dtok
            self.dtotal[s_] += 16
            op.dtok = (s_, self.dtotal[s_], self.epoch)
        tok = op.dtok if op.is_dma else op
        for b in writes:
            if id(b) in newgen:
                b.base = newgen[id(b)]
                b.readers = []
                b.writers = [tok]
            else:
                self._push(b.writers, tok)
        for b in reads:
            if b not in writes:
                if b.excl:
                    b.readers = [tok]
                else:
                    self._push(b.readers, tok)
        self.ops[op.eng].append(op)
        self.nops += 1

    def op(self, eng, fn, reads=(), writes=(), partial=False):
        o = Op(eng, fn, self.epoch)
        self._record(o, list(reads), list(writes), partial)
        return o

    def dma(self, fn, sb, reads=(), writes=(), queue="sp", partial=False):
        if sb.dsem is None:
            sb.dsem = self.dnext % len(self.dsems)
            self.dnext += 1
        o = Op(queue, fn, self.epoch)
        o.is_dma = True
        o.dtok = sb.dsem
        self._record(o, list(reads), list(writes), partial)
        return o

    def _simulate(self, ops):
        cnt = dict(getattr(self, "_sim_cnt", {e: 0 for e in COMPUTE}))
        dval = list(getattr(self, "_sim_d", [0] * len(self.dsems)))
        pos = {e: 0 for e in ops}
        progress = True
        while progress:
            progress = False
            for e in ops:
                while pos[e] < len(ops[e]):
                    o = ops[e][pos[e]]
                    ok = all(cnt[dd.eng] >= dd.val for dd in o.deps) and all(dval[s_] >= v for s_, v in o.ddeps.items())
                    if not ok:
                        break
                    if o.is_dma:
                        dval[o.dtok[0]] += 16
                    elif o.signal:
                        cnt[o.eng] += 1
                        assert cnt[o.eng] == o.val, (cnt[o.eng], o.val)
                    pos[e] += 1
                    progress = True
        stuck = {e: (pos[e], len(ops[e])) for e in ops if pos[e] < len(ops[e])}
        if stuck:
            for e in stuck:
                o = ops[e][pos[e]]
                print("STUCK", e, pos[e], [(dd.eng, dd.val, cnt[dd.eng]) for dd in o.deps], [(s_, v, dval[s_]) for s_, v in o.ddeps.items()])
            raise RuntimeError("schedule deadlock: %s" % stuck)
        assert dval == self.dtotal, "dma totals mismatch"
        self._sim_cnt, self._sim_d = cnt, dval

    def flush(self):
        nc = self.nc
        for e in COMPUTE:
            for o in self.ops[e]:
                if o.is_dma:
                    continue
                if o.signal:
                    self.count[e] += 1
                    o.val = self.count[e]
        final_d = {i: v for i, v in enumerate(self.dtotal) if v > 0}
        ops, sem, dsems = self.ops, self.sem, self.dsems

        def emit(engname, eng, final=False):
            waited = {}
            for o in ops[engname]:
                need = {}
                for d in o.deps:
                    k = ("c", d.eng)
                    if need.get(k, 0) < d.val:
                        need[k] = d.val
                for s, v in o.ddeps.items():
                    k = ("d", s)
                    if need.get(k, 0) < v:
                        need[k] = v
                for k, v in need.items():
                    if waited.get(k, 0) >= v:
                        continue
                    waited[k] = v
                    eng.wait_ge(sem[k[1]] if k[0] == "c" else dsems[k[1]], v)
                ins = o.fn(eng)
                if o.is_dma:
                    ins.then_inc(dsems[o.dtok[0]], 16)
                elif o.signal:
                    ins.then_inc(sem[o.eng], 1)
            if final:
                for s, v in final_d.items():
                    if waited.get(("d", s), 0) < v:
                        eng.wait_ge(dsems[s], v)

        if getattr(self, "check", True):
            self._simulate(ops)
        with nc.Block() as block:
            if ops["pe"]:
                @block.tensor
                def _(e):
                    emit("pe", e)
            if ops["act"]:
                @block.scalar
                def _(e):
                    emit("act", e)
            if ops["dve"]:
                @block.vector
                def _(e):
                    emit("dve", e)
            if ops["pool"]:
                @block.gpsimd
                def _(e):
                    emit("pool", e)

            @block.sync
            def _(e):
                emit("sp", e, final=True)
        self.phase_log.append((dict(self.count), {e: len(ops[e]) for e in ops}))
        self.ops = {e: [] for e in ("pe", "act", "dve", "pool", "sp")}
        self.epoch += 1


class K:
    def __init__(self, nc, S, es):
        self.nc, self.S, self.es = nc, S, es
        self.ps = []
        for i in range(8):
            t = es.enter_context(nc.psum_tensor("ps%d" % i, [128, 512], F32))
            self.ps.append((t, Buf("ps%d" % i, excl=True)))
        self.psi = 0
        self.rr = 0
        self.reserved = set()

    def nps(self):
        while True:
            i = self.psi % 8
            self.psi += 1
            if i not in self.reserved:
                return self.ps[i]

    def reserve(self):
        while True:
            i = self.psi % 8
            self.psi += 1
            if i not in self.reserved:
                self.reserved.add(i)
                return i, self.ps[i]

    def release(self, i):
        self.reserved.discard(i)

    def sb(self, es, name, shape, dt):
        self.nsb = getattr(self, "nsb", 0) + 1
        nm = "sb%d_%s" % (self.nsb, name)
        return es.enter_context(self.nc.sbuf_tensor(nm, list(shape), dt)), Buf(nm)

    def mm(self, out, lhsT, rhs, start, stop, reads, writes):
        self.S.op("pe", lambda e: e.matmul(out, lhsT=lhsT, rhs=rhs, start=start, stop=stop),
                  reads, writes, partial=not start)

    def tp(self, out, in_, ident, reads, writes, first=False):
        self.S.op("pe", lambda e: e.transpose(out=out, in_=in_, identity=ident), reads, writes,
                  partial=not first)

    def act(self, out, in_, func, reads, writes, bias=None, scale=None, accum=None, partial=False):
        kw = {}
        if bias is not None:
            kw["bias"] = bias
        if scale is not None:
            kw["scale"] = scale
        if accum is not None:
            kw["accum_out"] = accum
        self.S.op("act", lambda e: e.activation(out=out, in_=in_, func=func, **kw), reads, writes,
                  partial=partial)

    def tt(self, eng, out, in0, in1, op, reads, writes, partial=False):
        self.S.op(eng, lambda e: e.tensor_tensor(out=out, in0=in0, in1=in1, op=op), reads, writes,
                  partial=partial)

    def ts(self, eng, out, in0, s1, s2, op0, op1, reads, writes, partial=False):
        if op1 is None:
            self.S.op(eng, lambda e: e.tensor_scalar(out=out, in0=in0, scalar1=s1, scalar2=None, op0=op0),
                      reads, writes, partial=partial)
        else:
            self.S.op(eng, lambda e: e.tensor_scalar(out=out, in0=in0, scalar1=s1, scalar2=s2, op0=op0, op1=op1),
                      reads, writes, partial=partial)

    def stt(self, out, in0, scalar, in1, op0, op1, reads, writes, partial=False):
        self.S.op("dve", lambda e: e.scalar_tensor_tensor(out=out, in0=in0, scalar=scalar, in1=in1, op0=op0, op1=op1),
                  reads, writes, partial=partial)

    def cp(self, eng, out, in_, reads, writes, partial=False):
        if eng == "act":
            self.S.op("act", lambda e: e.activation(out=out, in_=in_, func=AF.Copy), reads, writes, partial=partial)
        else:
            self.S.op(eng, lambda e: e.tensor_copy(out=out, in_=in_), reads, writes, partial=partial)

    def recip(self, out, in_, reads, writes, partial=False):
        self.S.op("dve", lambda e: e.reciprocal(out=out, in_=in_), reads, writes, partial=partial)

    def recip_fast(self, out, in_, reads, writes, partial=False):
        self.S.op("dve", lambda e: e.reciprocal_approx_fast(out=out, in_=in_), reads, writes, partial=partial)

    def memset(self, eng, ap, val, writes, partial=False):
        self.S.op(eng, lambda e: e.memset(ap, val), (), writes, partial=partial)

    def dma(self, out, in_, sb, reads=(), writes=(), partial=False):
        self.S.dma(lambda e: e.dma_start(out=out, in_=in_), sb, reads, writes, partial=partial)

    def alt(self):
        self.rr += 1
        return "act" if self.rr % 2 else "dve"


def interleave_gen(gens, width):
    gens = iter(gens)
    active = []
    for _ in range(width):
        g_ = next(gens, None)
        if g_ is not None:
            active.append(g_)
    while active:
        for g_ in list(active):
            try:
                next(g_)
            except StopIteration:
                i = active.index(g_)
                n_ = next(gens, None)
                if n_ is not None:
                    active[i] = n_
                else:
                    active.pop(i)
        yield


def interleave(gens, width):
    gens = iter(gens)
    active = []
    for _ in range(width):
        g_ = next(gens, None)
        if g_ is not None:
            active.append(g_)
    while active:
        for g_ in list(active):
            try:
                next(g_)
            except StopIteration:
                i = active.index(g_)
                n_ = next(gens, None)
                if n_ is not None:
                    active[i] = n_
                else:
                    active.pop(i)


SC = 2048


class Stage:
    def __init__(self, k, es, n=2, cols=SC):
        self.k = k
        self.cols = cols
        self.t = [k.sb(es, "stg%d" % i, [128, cols], F32) for i in range(n)]
        self.i = 0

    def nxt(self):
        r = self.t[self.i % len(self.t)]
        self.i += 1
        return r


def load_w(k, es, stg, name, src, R, C, scale=None, dst=None):
    nch = (R + 127) // 128
    w, wb = dst if dst is not None else k.sb(es, name, [128, nch, C], BF16)
    for c in range(nch):
        rows = min(128, R - c * 128)
        for j0 in range(0, C, stg.cols):
            cols = min(stg.cols, C - j0)
            st, stb = stg.nxt()
            k.dma(st[0:rows, 0:cols], src[c * 128:c * 128 + rows, j0:j0 + cols], stb, writes=[stb])
            if scale is not None:
                k.act(w[0:rows, c, j0:j0 + cols], st[0:rows, 0:cols], AF.Identity, [stb, scale[1]], [wb],
                      scale=scale[0][0:rows, c:c + 1], partial=True)
            else:
                eng = ("act", "dve", "pool")[k.rr % 3]
                k.rr += 1
                k.cp(eng, w[0:rows, c, j0:j0 + cols], st[0:rows, 0:cols], [stb], [wb], partial=True)
    return w, wb


def bcast_load(k, es, name, src_row, n):
    t, b = k.sb(es, name, [128, n], F32)
    k.dma(t[:], src_row[0, :].partition_broadcast(128), b, writes=[b])
    return t, b


def layernorm(k, x, xb, out, outb, G, B, tmp, eps=LN_EPS):
    st, stb = tmp["stats"]
    mv, mvb = tmp["mv"]
    for h in range(2):
        k.S.op("dve", lambda e, h=h: e.bn_stats(out=st[:, h, :], in_=x[:, h * 512:(h + 1) * 512]), [xb], [stb],
               partial=(h > 0))
    k.S.op("dve", lambda e: e.bn_aggr(out=mv[:, 0:2], in_=st[:].rearrange("p a b -> p (a b)")), [stb], [mvb])
    k.ts("dve", mv[:, 2:3], mv[:, 1:2], float(eps), None, ALU.add, None, [mvb], [mvb])
    k.act(mv[:, 2:3], mv[:, 2:3], AF.Sqrt, [mvb], [mvb])
    k.recip(mv[:, 3:4], mv[:, 2:3], [mvb], [mvb])
    k.ts("dve", out, x, mv[:, 0:1], mv[:, 3:4], ALU.subtract, ALU.mult, [xb, mvb], [outb])
    k.tt("pool", out, out, G[0][:], ALU.mult, [outb, G[1]], [outb])
    k.tt("pool", out, out, B[0][:], ALU.add, [outb, B[1]], [outb])


def transpose_tile(k, x, xb, identF, dst, dstb, col0, ncols=128):
    for half in range(2):
        pt, ptb = k.nps()
        for j in range(4):
            c = half * 4 + j
            k.tp(pt[:, j * 128:(j + 1) * 128], x[:, c * 128:(c + 1) * 128], identF[0][:], [xb, identF[1]], [ptb],
                 first=(j == 0))
        k.cp(k.alt(), dst[:, half * 4:half * 4 + 4, col0:col0 + 128],
             pt[:].rearrange("p (c t) -> p c t", c=4), [ptb], [dstb], partial=True)


def phase0(k, d, g):
    S = k.S
    with contextlib.ExitStack() as es:
        identF = g["identF"]
        k.dma(identF[0][:], d["identF"], identF[1], writes=[identF[1]])
        trif, trifb = k.sb(es, "trif", [128, 128], F32)
        k.dma(trif[:], d["tri"], trifb, writes=[trifb])
        k.cp("dve", g["tri"][0][:], trif[:], [trifb], [g["tri"][1]])
        k.memset("pool", g["ones"][0][:], 1.0, [g["ones"][1]])
        k.memset("pool", g["mem_va"][0][:], 1.0, [g["mem_va"][1]])
        stg = Stage(k, es)
        tmp = {"stats": k.sb(es, "lnst", [128, 2, 6], F32), "mv": k.sb(es, "lnmv", [128, 4], F32)}
        G = bcast_load(k, es, "memG", d["mem_ln_g"], D)
        B = bcast_load(k, es, "memB", d["mem_ln_b"], D)
        wm, wmb = load_w(k, es, stg, "wm", d["w_mem_kv"], D, 512)
        mT, mTb = k.sb(es, "memT", [128, 8, 256], BF16)
        for mt in range(2):
            xm, xmb = k.sb(es, "xm%d" % mt, [128, D], F32)
            k.dma(xm[:], d["mem"][mt * 128:(mt + 1) * 128, :], xmb, writes=[xmb])
            layernorm(k, xm[:], xmb, xm[:], xmb, G, B, tmp)
            transpose_tile(k, xm, xmb, identF, mT, mTb, mt * 128)
        for p in range(2):
            pt, ptb = k.nps()
            for c in range(8):
                k.mm(pt[:, 0:256], wm[:, c, p * 128:(p + 1) * 128], mT[:, c, :], c == 0, c == 7, [wmb, mTb], [ptb])
            k.cp("act", g["mem_kT"][0][:, p, :], pt[:, 0:256], [ptb], [g["mem_kT"][1]], partial=True)
        for mt in range(2):
            pt, ptb = k.nps()
            for c in range(8):
                k.mm(pt[:, 0:256], mT[:, c, mt * 128:(mt + 1) * 128], wm[:, c, 256:512], c == 0, c == 7, [wmb, mTb], [ptb])
            k.cp("dve", g["mem_va"][0][:, mt, :, 0:64], pt[:, 0:256].rearrange("p (h v) -> p h v", h=4),
                 [ptb], [g["mem_va"][1]], partial=True)
        invf, invfb = k.sb(es, "invf", [128, 1], F32)
        sgn, sgnb = k.sb(es, "sgn", [128, 1], F32)
        k.dma(invf[:], d["invf"], invfb, writes=[invfb])
        k.dma(sgn[:], d["sgn"], sgnb, writes=[sgnb])
        W = 1024
        pi, pib = k.sb(es, "pos_i", [128, W], I32)
        r_, rb = k.sb(es, "rr", [128, W], F32)
        ki, kib = k.sb(es, "ki", [128, W], I32)
        kf, kfb = k.sb(es, "kf", [128, W], F32)
        f_, fb = k.sb(es, "ff", [128, W], F32)
        o_, ob = k.sb(es, "oo", [128, W], F32)
        def rot():
            for blk in range(T // W):
                k.dma(pi[:], d["pos"][0, blk * W:(blk + 1) * W].partition_broadcast(128), pib, writes=[pib])
                yield
                k.cp("dve", r_[:], pi[:], [pib], [rb])
                yield
                k.ts("dve", r_[:], r_[:], invf[:, 0:1], None, ALU.mult, None, [rb, invfb], [rb])
                yield
                for which in range(2):
                    if which == 1:
                        k.ts("dve", r_[:], r_[:], 0.25, None, ALU.add, None, [rb], [rb])
                        yield
                    k.cp("dve", ki[:], r_[:], [rb], [kib])
                    yield
                    k.cp("dve", kf[:], ki[:], [kib], [kfb])
                    yield
                    k.tt("dve", f_[:], r_[:], kf[:], ALU.subtract, [rb, kfb], [fb])
                    yield
                    k.ts("dve", kf[:], f_[:], 0.5, None, ALU.is_gt, None, [fb], [kfb])
                    yield
                    k.tt("dve", f_[:], f_[:], kf[:], ALU.subtract, [fb, kfb], [fb])
                    yield
                    k.ts("dve", kf[:], f_[:], -0.5, None, ALU.is_lt, None, [fb], [kfb])
                    yield
                    k.tt("dve", f_[:], f_[:], kf[:], ALU.add, [fb, kfb], [fb])
                    yield
                    k.act(o_[:], f_[:], AF.Sin, [fb], [ob], scale=float(2.0 * np.pi))
                    yield
                    if which == 0:
                        k.ts("dve", o_[:], o_[:], sgn[:, 0:1], None, ALU.mult, None, [ob, sgnb], [ob])
                        yield
                        k.dma(d["SSd"][:, blk * W:(blk + 1) * W], o_[:], ob, reads=[ob])
                        yield
                    else:
                        k.dma(d["CCd"][:, blk * W:(blk + 1) * W], o_[:], ob, reads=[ob])
                        yield
        NBX = 4
        xin = [k.sb(es, "xin%d" % i, [128, D], F32) for i in range(NBX)]
        xt = [k.sb(es, "xt%d" % i, [128, 8, 128], BF16) for i in range(NBX)]
        XTv = d["XT"].rearrange("(c p) t -> p c t", p=128)

        def xtile(t):
            xi, xib = xin[t % NBX]
            xo, xob = xt[t % NBX]
            k.dma(xi[:], d["x"][t * 128:(t + 1) * 128, :], xib, writes=[xib])
            yield
            for half in range(2):
                pt, ptb = k.nps()
                for j in range(4):
                    c = half * 4 + j
                    k.tp(pt[:, j * 128:(j + 1) * 128], xi[:, c * 128:(c + 1) * 128], identF[0][:], [xib, identF[1]], [ptb],
                         first=(j == 0))
                k.cp(k.alt(), xo[:, half * 4:half * 4 + 4, :], pt[:].rearrange("p (c t) -> p c t", c=4), [ptb], [xob],
                     partial=(half == 1))
                yield
            k.dma(XTv[:, :, t * 128:(t + 1) * 128], xo[:], xob, reads=[xob])
            yield

        interleave([rot(), interleave_gen((xtile(t) for t in range(NT)), NBX)], 2)
    S.flush()


def phaseA(k, d, g):
    S = k.S
    ones = g["ones"]
    with contextlib.ExitStack() as es:
        stg = Stage(k, es)
        qn, qnb = k.sb(es, "qn", [128, 3], F32)
        kvn, kvnb = k.sb(es, "kvn", [128, 2], F32)
        k.dma(qn[:], d["qn"], qnb, writes=[qnb])
        k.dma(kvn[:], d["kvn"], kvnb, writes=[kvnb])
        WA, WAb = load_w(k, es, stg, "WA", d["WA"], D, 960)
        WQ, WQb = load_w(k, es, stg, "WQ", d["WQ"], 384, 1536, scale=(None if 'scale' in SKIP else (qn, qnb)))
        WKV, WKVb = load_w(k, es, stg, "WKV", d["WKV"], 256, 1536, scale=(None if 'scale' in SKIP else (kvn, kvnb)))
        xTs = [k.sb(es, "xT%d" % i, [128, 8, 512], BF16) for i in range(2)]
        CCs = [k.sb(es, "CC%d" % i, [128, 512], F32) for i in range(2)]
        SSs = [k.sb(es, "SS%d" % i, [128, 512], F32) for i in range(2)]
        cTs = [k.sb(es, "cT%d" % i, [128, 5, 512], BF16) for i in range(2)]
        sqs_ = [k.sb(es, "sq%d" % i, [128, 5, 512], BF16) for i in range(2)]
        cns = [k.sb(es, "cn%d" % i, [128, 5, 512], BF16) for i in range(2)]
        rqs = [k.sb(es, "rq%d" % i, [128, 512], F32) for i in range(2)]
        rkvs = [k.sb(es, "rkv%d" % i, [128, 512], F32) for i in range(2)]
        t1s = [k.sb(es, "t1_%d" % i, [128, 512], F32) for i in range(2)]
        t2s = [k.sb(es, "t2_%d" % i, [128, 512], F32) for i in range(2)]
        obs = [k.sb(es, "ob%d" % i, [128, 512], BF16) for i in range(6)]
        vts = [k.sb(es, "vt%d" % i, [128, 12, 128], BF16) for i in range(2)]
        for vt, vtb in vts:
            k.memset("pool", vt[:], 1.0, [vtb])
        XTv = d["XT"].rearrange("(c p) t -> p c t", p=128)
        VAv = d["VA"].rearrange("h p kt c -> p h kt c")
        oi = [0]

        def nob():
            r = obs[oi[0] % len(obs)]
            oi[0] += 1
            return r

        def rope_out(psA, psAb, psBt, psBb, rows, CC, CCb, SS, SSb, idx):
            t1, t1b = t1s[idx % 2]
            t2, t2b = t2s[idx % 2]
            o, ob = nob()
            k.tt("dve", t1[0:rows, :], psA[0:rows, :], CC[0:rows, :], ALU.mult, [psAb, CCb], [t1b])
            k.tt("dve", t2[0:rows, :], psBt[0:rows, :], SS[0:rows, :], ALU.mult, [psBb, SSb], [t2b])
            k.tt("pool", o[0:rows, :], t1[0:rows, :], t2[0:rows, :], ALU.add, [t1b, t2b], [ob])
            return o, ob

        def part1(s):
                tk = slice(s * 512, (s + 1) * 512)
                xT, xTb = xTs[s % 2]
                CC, CCb = CCs[s % 2]
                SS, SSb = SSs[s % 2]
                cT, cTb = cTs[s % 2]
                sq, sqb = sqs_[s % 2]
                cn, cnb = cns[s % 2]
                rq, rqb = rqs[s % 2]
                rkv, rkvb = rkvs[s % 2]
                k.dma(xT[:], XTv[:, :, tk], xTb, writes=[xTb])
                yield
                k.dma(CC[:], d["CCd"][:, tk], CCb, writes=[CCb])
                yield
                k.dma(SS[:], d["SSd"][:, tk], SSb, writes=[SSb])
                yield
                for ci in range(5):
                    pt, ptb = k.nps()
                    for c in range(8):
                        k.mm(pt[:], WA[:, c, ci * 128:(ci + 1) * 128], xT[:, c, :], c == 0, c == 7, [WAb, xTb], [ptb])
                    k.cp("dve", cT[:, ci, :], pt[:], [ptb], [cTb], partial=True)
                    yield
                    if 'sq' not in SKIP:
                        k.act(sq[:, ci, :], pt[:], AF.Square, [ptb], [sqb], partial=True)
                        yield
                for (lo, hi, n, r_, rb_) in (() if 'rms' in SKIP else ((0, 3, 384.0, rq, rqb), (3, 5, 256.0, rkv, rkvb))):
                    pt, ptb = k.nps()
                    for ci in range(lo, hi):
                        k.mm(pt[:], ones[0][:], sq[:, ci, :], ci == lo, ci == hi - 1, [ones[1], sqb], [ptb])
                    k.ts("dve", r_[:], pt[:], 1.0 / n, RMS_EPS, ALU.mult, ALU.add, [ptb], [rb_])
                    yield
                    k.act(r_[:], r_[:], AF.Ln, [rb_], [rb_])
                    yield
                    k.act(r_[:], r_[:], AF.Exp, [rb_], [rb_], scale=-0.5)
                    yield
                    for ci in range(lo, hi):
                        k.tt("dve", cn[:, ci, :], cT[:, ci, :], r_[:], ALU.mult, [cTb, rb_], [cnb], partial=True)
                        yield

        def part2(s):
                tk = slice(s * 512, (s + 1) * 512)
                xT, xTb = xTs[s % 2]
                CC, CCb = CCs[s % 2]
                SS, SSb = SSs[s % 2]
                cT, cTb = cTs[s % 2]
                sq, sqb = sqs_[s % 2]
                cn, cnb = cns[s % 2]
                rq, rqb = rqs[s % 2]
                rkv, rkvb = rkvs[s % 2]
                for j in range(0 if 'qm' in SKIP else 2):
                    pt, ptb = k.nps()
                    for c in range(8):
                        k.mm(pt[:], WA[:, c, 640 + j * 128:640 + (j + 1) * 128], xT[:, c, :], c == 0, c == 7, [WAb, xTb], [ptb])
                    o, ob = nob()
                    k.cp("act", o[:], pt[:], [ptb], [ob])
                    yield
                    k.dma(d["QM"][j * 128:(j + 1) * 128, tk], o[:], ob, reads=[ob])
                    yield
                if 'kr' not in SKIP:
                  pA, pAb = k.nps()
                  pB, pBb = k.nps()
                  for c in range(8):
                    k.mm(pA[0:32, :], WA[:, c, 896:928], xT[:, c, :], c == 0, c == 7, [WAb, xTb], [pAb])
                  for c in range(8):
                    k.mm(pB[0:32, :], WA[:, c, 928:960], xT[:, c, :], c == 0, c == 7, [WAb, xTb], [pBb])
                  o, ob = rope_out(pA, pAb, pB, pBb, 32, CC, CCb, SS, SSb, 0)
                  k.dma(d["KR"][0:32, tk], o[0:32, :], ob, reads=[ob])
                  yield
                for p in range(0 if 'qn' in SKIP else 6):
                    pt, ptb = k.nps()
                    for r in range(3):
                        k.mm(pt[:], WQ[:, r, p * 128:(p + 1) * 128], cn[:, r, :], r == 0, r == 2, [WQb, cnb], [ptb])
                    o, ob = nob()
                    k.cp(k.alt(), o[:], pt[:], [ptb], [ob])
                    yield
                    for j in range(2):
                        k.dma(d["QT"][2 * p + j, 0:64, tk], o[j * 64:(j + 1) * 64, :], ob, reads=[ob])
                        yield
                for c4 in range(0 if 'qr' in SKIP else 3):
                    pA, pAb = k.nps()
                    pB, pBb = k.nps()
                    for r in range(3):
                        k.mm(pA[:], WQ[:, r, 768 + c4 * 128:768 + (c4 + 1) * 128], cn[:, r, :], r == 0, r == 2, [WQb, cnb], [pAb])
                    for r in range(3):
                        k.mm(pB[:], WQ[:, r, 1152 + c4 * 128:1152 + (c4 + 1) * 128], cn[:, r, :], r == 0, r == 2, [WQb, cnb], [pBb])
                    o, ob = rope_out(pA, pAb, pB, pBb, 128, CC, CCb, SS, SSb, c4 + 1)
                    for j in range(4):
                        k.dma(d["QT"][4 * c4 + j, 64:96, tk], o[j * 32:(j + 1) * 32, :], ob, reads=[ob])
                        yield
                for p in range(0 if 'kn' in SKIP else 6):
                    pt, ptb = k.nps()
                    for r in range(2):
                        k.mm(pt[:], WKV[:, r, p * 128:(p + 1) * 128], cn[:, 3 + r, :], r == 0, r == 1, [WKVb, cnb], [ptb])
                    o, ob = nob()
                    k.cp(k.alt(), o[:], pt[:], [ptb], [ob])
                    yield
                    for j in range(2):
                        k.dma(d["KT"][2 * p + j, 0:64, tk], o[j * 64:(j + 1) * 64, :], ob, reads=[ob])
                        yield
                for t4 in range(0 if 'v' in SKIP else 4):
                    p1, p1b = k.nps()
                    p2, p2b = k.nps()
                    for r in range(2):
                        k.mm(p1[:], cn[:, 3 + r, t4 * 128:(t4 + 1) * 128], WKV[:, r, 768:1280], r == 0, r == 1, [WKVb, cnb], [p1b])
                    for r in range(2):
                        k.mm(p2[:, 0:256], cn[:, 3 + r, t4 * 128:(t4 + 1) * 128], WKV[:, r, 1280:1536], r == 0, r == 1, [WKVb, cnb], [p2b])
                    vt, vtb = vts[t4 % 2]
                    k.cp("act", vt[:, 0:8, 0:64], p1[:].rearrange("p (h v) -> p h v", h=8), [p1b], [vtb])
                    yield
                    k.cp("dve", vt[:, 8:12, 0:64], p2[:, 0:256].rearrange("p (h v) -> p h v", h=4), [p2b], [vtb], partial=True)
                    yield
                    k.dma(VAv[:, :, s * 4 + t4, :], vt[:], vtb, reads=[vtb])
                    yield

        for _ in part1(0):
            pass
        for s in range(T // 512):
            interleave([part2(s)] + ([part1(s + 1)] if s + 1 < T // 512 else []), 2)
    S.flush()


def attn_head(k, g, bufs, Kap, Kb, Qap, Qb, Vap, Vb, causal, scale, out_dram, st):
    tri = g["tri"]
    tiles = []
    for qb in range(8):
        nkt = 4 * qb + 4 if causal else 2
        for kt in range(nkt):
            n0 = (kt - 4 * qb) * 128 if (causal and kt >= 4 * qb) else 0
            tiles.append((qb, kt, nkt, n0))
    pend = []

    def qk(i):
        qb, kt, nkt, n0 = tiles[i]
        cols = 512 - n0
        ps, psb = k.ps[st["s"] % 4]
        st["s"] += 1
        pt, ptb = bufs["pt"][st["p"] % len(bufs["pt"])]
        st["p"] += 1
        k.mm(ps[:, 0:cols], Kap(kt), Qap(qb * 512 + n0, (qb + 1) * 512), True, True, [Kb, Qb], [psb])
        if "sc" in bufs and i % 10 in (2, 5, 8):
            sc, scb = bufs["sc"][st["c"] % len(bufs["sc"])]
            st["c"] += 1
            k.cp("dve", sc[:, 0:cols], ps[:, 0:cols], [psb], [scb])
            k.act(pt[:, 0:cols], sc[:, 0:cols], AF.Exp, [scb], [ptb], scale=scale)
        else:
            k.act(pt[:, 0:cols], ps[:, 0:cols], AF.Exp, [psb], [ptb], scale=scale)
        if causal and kt >= 4 * qb:
            k.tt("pool", pt[:, 0:128], pt[:, 0:128], tri[0][:], ALU.mult, [ptb, tri[1]], [ptb])
        return pt, ptb

    def pv(i, pt, ptb):
        qb, kt, nkt, n0 = tiles[i]
        cols = 512 - n0
        if kt == 0:
            st["po"] = k.ps[4 + st["o"] % 2]
            st["o"] += 1
        po, pob = st["po"]
        k.mm(po[:, n0:512], Vap(kt), pt[:, 0:cols], kt == 0, kt == nkt - 1, [Vb, ptb], [pob])
        if kt == nkt - 1:
            rec, recb = bufs["rec"][st["r"] % 2]
            osb, osbb = bufs["osb"][st["r"] % 2]
            st["r"] += 1
            k.recip(rec[0:64, :], po[64:128, :], [pob], [recb])
            k.tt("dve", osb[0:64, :], po[0:64, :], rec[0:64, :], ALU.mult, [pob, recb], [osbb])
            k.dma(out_dram[:, qb * 512:(qb + 1) * 512], osb[0:64, :], osbb, reads=[osbb])

    LAG = 3
    q = []
    for i in range(len(tiles)):
        q.append((i,) + qk(i))
        if len(q) > LAG:
            pv(*q.pop(0))
        if i == 8 and st.get("prefetch") is not None:
            st["prefetch"]()
            st["prefetch"] = None
        if st.get("bg") is not None and i % 5 == 4:
            try:
                next(st["bg"])
            except StopIteration:
                st["bg"] = None
    while q:
        pv(*q.pop(0))


def precast_gen(k, es, d):
    stg = [k.sb(es, "pcs%d" % i, [128, 2048], F32) for i in range(2)]
    outb = [k.sb(es, "pcb%d" % i, [128, 2048], BF16) for i in range(2)]
    i = 0
    for l in range(2):
        for (src, dst, R, C) in ((d["w_out"][l], d["WOb"][l], D, D), (d["w_ff1"][l], d["W1b"][l], D, 4096),
                                 (d["w_ff2"][l], d["W2b"][l], 4096, D)):
            for c in range(R // 128):
                for j0 in range(0, C, 2048):
                    cols = min(2048, C - j0)
                    st, stb = stg[i % 2]
                    ob, obb = outb[i % 2]
                    i += 1
                    k.dma(st[:, 0:cols], src[c * 128:(c + 1) * 128, j0:j0 + cols], stb, writes=[stb])
                    yield
                    k.cp("dve", ob[:, 0:cols], st[:, 0:cols], [stb], [obb])
                    yield
                    k.dma(dst[c * 128:(c + 1) * 128, j0:j0 + cols], ob[:, 0:cols], obb, reads=[obb])
                    yield


def phaseB(k, d, g, with_mla):
    S = k.S
    with contextlib.ExitStack() as es:
        bufs = {
            "pt": [k.sb(es, "pt%d" % i, [128, 512], BF16) for i in range(6)],
            "rec": [k.sb(es, "rec%d" % i, [64, 512], F32) for i in range(2)],
            "osb": [k.sb(es, "osb%d" % i, [64, 512], BF16) for i in range(2)],
        }
        st = {"s": 0, "p": 0, "o": 0, "r": 0, "c": 0, "bg": (precast_gen(k, es, d) if with_mla else None)}
        if with_mla:
            KTs = [k.sb(es, "KTh%d" % i, [96, T], BF16) for i in range(2)]
            QTs = [k.sb(es, "QTh%d" % i, [96, T], BF16) for i in range(2)]
            VAs = [k.sb(es, "VAh%d" % i, [128, NT, 128], BF16) for i in range(2)]
        if with_mla:
            k.dma(KTs[0][0][0:64, :], d["KT"][0], KTs[0][1], writes=[KTs[0][1]])
            k.dma(KTs[0][0][64:96, :], d["KR"], KTs[0][1], writes=[KTs[0][1]], partial=True)
            k.dma(QTs[0][0][:], d["QT"][0], QTs[0][1], writes=[QTs[0][1]])
            k.dma(VAs[0][0][:], d["VA"][0], VAs[0][1], writes=[VAs[0][1]])
        mk, mkb = g["mem_kT"]
        mv, mvb = g["mem_va"]
        for pair in range(2):
            qm, qmb = k.sb(es, "qm%d" % pair, [128, T], BF16)
            k.dma(qm[:], d["QM"][pair * 128:(pair + 1) * 128, :], qmb, writes=[qmb])
            for j2 in range(2):
                j = pair * 2 + j2
                b0 = j2 * 64
                attn_head(k, g, bufs,
                          lambda kt, b0=b0, pair=pair: mk[b0:b0 + 64, pair, kt * 128:(kt + 1) * 128], mkb,
                          lambda c0, c1, b0=b0, qm=qm: qm[b0:b0 + 64, c0:c1], qmb,
                          lambda kt, j=j: mv[:, kt, j, :], mvb,
                          False, 64.0 ** -0.5, d["OT"][768 + j * 64:768 + (j + 1) * 64, :], st)
        if with_mla:
            def head_loads(h):
                kt_, ktb = KTs[h % 2]
                qt_, qtb = QTs[h % 2]
                va_, vab = VAs[h % 2]
                k.dma(kt_[0:64, :], d["KT"][h], ktb, writes=[ktb])
                k.dma(kt_[64:96, :], d["KR"], ktb, writes=[ktb], partial=True)
                k.dma(qt_[:], d["QT"][h], qtb, writes=[qtb])
                k.dma(va_[:], d["VA"][h], vab, writes=[vab])

            for h in range(12):
                kt_, ktb = KTs[h % 2]
                qt_, qtb = QTs[h % 2]
                va_, vab = VAs[h % 2]
                st["prefetch"] = (lambda h=h: head_loads(h + 1)) if h + 1 < 12 else None
                attn_head(k, g, bufs,
                          lambda kt, kt_=kt_: kt_[:, kt * 128:(kt + 1) * 128], ktb,
                          lambda c0, c1, qt_=qt_: qt_[:, c0:c1], qtb,
                          lambda kt, va_=va_: va_[:, kt, :], vab,
                          True, 96.0 ** -0.5, d["OT"][h * 64:(h + 1) * 64, :], st)
        if st.get("bg") is not None:
            for _ in st["bg"]:
                pass
    S.flush()


def phaseC(k, d, g, l, Xin, Xout, write_xt):
    S = k.S
    identF = g["identF"]
    with contextlib.ExitStack() as es:
        wo, wob = k.sb(es, "wo", [128, 8, D], BF16)
        w1, w1b = k.sb(es, "w1", [128, 8, 4096], BF16)
        w2, w2b = k.sb(es, "w2", [128, 32, D], BF16)
        G1 = bcast_load(k, es, "G1", d["ln1_g"][l:l + 1, :], D)
        B1 = bcast_load(k, es, "B1", d["ln1_b"][l:l + 1, :], D)
        G2 = bcast_load(k, es, "G2", d["ln2_g"][l:l + 1, :], D)
        B2 = bcast_load(k, es, "B2", d["ln2_b"][l:l + 1, :], D)
        w1bs = [Buf("w1blk%d" % i) for i in range(4)]
        tmp1 = {"stats": k.sb(es, "lnst1", [128, 2, 6], F32), "mv": k.sb(es, "lnmv1", [128, 4], F32)}
        tmp2 = {"stats": k.sb(es, "lnst2", [128, 2, 6], F32), "mv": k.sb(es, "lnmv2", [128, 4], F32)}
        ots = [k.sb(es, "ot%d" % i, [128, 8, 128], BF16) for i in range(2)]
        xins = [k.sb(es, "xi%d" % i, [128, D], F32) for i in range(2)]
        r1, r1b = k.sb(es, "r1", [128, D], F32)
        r2, r2b = r1, r1b
        x1s = [k.sb(es, "x1_%d" % i, [128, D], F32) for i in range(2)]
        x1T, x1Tb = k.sb(es, "x1T", [128, 8, 128], BF16)
        hT, hTb = k.sb(es, "hT", [128, 32, 128], BF16)
        rl = [k.sb(es, "rl%d" % i, [128, 512], F32) for i in range(2)]
        x2T, x2Tb = k.sb(es, "x2T", [128, 8, 128], BF16)
        OTv = d["OT"].rearrange("(c p) t -> p c t", p=128)
        XTv = d["XT"].rearrange("(c p) t -> p c t", p=128)

        def loads(t):
            tk = slice(t * 128, (t + 1) * 128)
            ot, otb = ots[t % 2]
            xi, xib = xins[t % 2]
            k.dma(ot[:], OTv[:, :, tk], otb, writes=[otb])
            k.dma(xi[:], Xin[tk, :], xib, writes=[xib])

        def s1a(t):
            ot, otb = ots[t % 2]
            xi, xib = xins[t % 2]
            x1, x1b = x1s[t % 2]
            for half in range(2):
                pt, ptb = k.nps()
                for c in range(8):
                    k.mm(pt[:], ot[:, c, :], wo[:, c, half * 512:(half + 1) * 512], c == 0, c == 7, [otb, wob], [ptb])
                k.stt(r1[:, half * 512:(half + 1) * 512], xi[:, half * 512:(half + 1) * 512], float(ALPHA), pt[:],
                      ALU.mult, ALU.add, [xib, ptb], [r1b], partial=(half == 1))
            layernorm(k, r1[:], r1b, x1[:], x1b, G1, B1, tmp1)

        def s1bT(t):
            x1, x1b = x1s[t % 2]
            transpose_tile(k, x1, x1b, identF, x1T, x1Tb, 0)

        def s1bF(t):
            for f4 in range(8):
                pt, ptb = k.nps()
                for j in range(4):
                    f = f4 * 4 + j
                    for c in range(8):
                        k.mm(pt[:, j * 128:(j + 1) * 128], w1[:, c, f * 128:(f + 1) * 128], x1T[:, c, :], c == 0, c == 7,
                             [w1bs[f // 8], x1Tb], [ptb])
                rt, rtb = rl[f4 % 2]
                k.act(rt[:], pt[:], AF.Relu, [ptb], [rtb])
                k.tt("pool", hT[:, f4 * 4:(f4 + 1) * 4, :], rt[:].rearrange("p (j t) -> p j t", j=4),
                     rt[:].rearrange("p (j t) -> p j t", j=4), ALU.mult, [rtb], [hTb], partial=(f4 > 0))

        s2st = {}

        def s2(t, part):
            tk = slice(t * 128, (t + 1) * 128)
            x1, x1b = x1s[t % 2]
            if part == 0:
                pt, ptb = k.nps()
                for f in range(32):
                    k.mm(pt[:], hT[:, f, :], w2[:, f, 0:512], f == 0, f == 31, [hTb, w2b], [ptb])
                k.stt(r2[:, 0:512], x1[:, 0:512], float(ALPHA), pt[:], ALU.mult, ALU.add, [x1b, ptb], [r2b])
                return
            if part == 1:
                s2st["i"], s2st["pt"] = k.reserve()
                pt, ptb = s2st["pt"]
                for f in range(16):
                    k.mm(pt[:], hT[:, f, :], w2[:, f, 512:1024], f == 0, False, [hTb, w2b], [ptb])
                return
            pt, ptb = s2st["pt"]
            for f in range(16, 32):
                k.mm(pt[:], hT[:, f, :], w2[:, f, 512:1024], False, f == 31, [hTb, w2b], [ptb])
            k.release(s2st["i"])
            k.stt(r2[:, 512:1024], x1[:, 512:1024], float(ALPHA), pt[:], ALU.mult, ALU.add, [x1b, ptb], [r2b], partial=True)
            layernorm(k, r2[:], r2b, r2[:], r2b, G2, B2, tmp2)
            k.dma(Xout[tk, :], r2[:], r2b, reads=[r2b])

        def s3(t):
            tk = slice(t * 128, (t + 1) * 128)
            if write_xt:
                transpose_tile(k, r2, r2b, identF, x2T, x2Tb, 0)
                k.dma(XTv[:, :, tk], x2T[:], x2Tb, reads=[x2Tb])

        loads(0)
        k.dma(wo[:], d["WOb"][l].rearrange("(c p) n -> p c n", p=128), wob, writes=[wob])
        for c4 in range(4):
            k.dma(w1[:, :, 1024 * c4:1024 * (c4 + 1)], d["W1b"][l].rearrange("(c p) n -> p c n", p=128)[:, :, 1024 * c4:1024 * (c4 + 1)],
                  w1bs[c4], writes=[w1bs[c4]])
        for c4 in range(4):
            k.dma(w2[:, 8 * c4:8 * c4 + 8, :], d["W2b"][l].rearrange("(c p) n -> p c n", p=128)[:, 8 * c4:8 * c4 + 8, :], w2b,
                  writes=[w2b], partial=(c4 > 0))
        for t in range(NT + 1):
            if t + 1 < NT:
                loads(t + 1)
            if t < NT:
                s1a(t)
            if t > 0:
                s2(t - 1, 0)
                s2(t - 1, 1)
            if t < NT:
                s1bT(t)
            if t > 0:
                s2(t - 1, 2)
            if t < NT:
                s1bF(t)
            if t > 0:
                s3(t - 1)
    S.flush()


_CACHE = {}
IN_SPECS = [
    ("x", [T, D], F32), ("mem", [256, D], F32), ("pos", [1, T], I32),
    ("mem_ln_g", [1, D], F32), ("mem_ln_b", [1, D], F32), ("w_mem_kv", [D, 512], F32),
    ("WA", [D, 960], F32), ("qn", [128, 3], F32), ("WQ", [384, 1536], F32), ("kvn", [128, 2], F32),
    ("WKV", [256, 1536], F32),
    ("w_out", [2, D, D], F32), ("ln1_g", [2, D], F32), ("ln1_b", [2, D], F32),
    ("w_ff1", [2, D, 4096], F32), ("w_ff2", [2, 4096, D], F32), ("ln2_g", [2, D], F32), ("ln2_b", [2, D], F32),
    ("identF", [128, 128], F32), ("tri", [128, 128], F32), ("invf", [128, 1], F32), ("sgn", [128, 1], F32),
]
SCRATCH = [
    ("XT", [D, T], BF16), ("CCd", [128, T], F32), ("SSd", [128, T], F32),
    ("QT", [12, 96, T], BF16), ("KT", [12, 64, T], BF16), ("KR", [32, T], BF16),
    ("VA", [12, 128, NT, 128], BF16), ("QM", [256, T], BF16), ("OT", [D, T], BF16),
    ("X1", [T, D], F32),
    ("WOb", [2, D, D], BF16), ("W1b", [2, D, 4096], BF16), ("W2b", [2, 4096, D], BF16),
]


def build(stage="full", dbg=()):
    nc = bass.Bass("TRN2", target_bir_lowering=False)
    d = {}
    for name, shape, dt in IN_SPECS + RW_IN_SPECS:
        d[name] = nc.dram_tensor(name, list(shape), dt, kind="ExternalInput").ap()
    for name, shape, dt in SCRATCH + RW_SCRATCH:
        d[name] = nc.dram_tensor(name, list(shape), dt, kind=("ExternalOutput" if name in dbg else "Internal")).ap()
    d["out"] = nc.dram_tensor("out", [T, D], F32, kind="ExternalOutput").ap()
    with contextlib.ExitStack() as es:
        S = Sched(nc, es)
        _CACHE["S"] = S
        k = K(nc, S, es)
        g = {
            "identF": k.sb(es, "identF", [128, 128], F32),
            "tri": k.sb(es, "tri", [128, 128], BF16),
            "ones": k.sb(es, "ones", [128, 128], BF16),
            "mem_kT": k.sb(es, "mem_kT", [128, 2, 256], BF16),
            "mem_va": k.sb(es, "mem_va", [128, 2, 4, 128], BF16),
        }
        phase0(k, d, g)
        if stage == "p0":
            return nc
        phaseA(k, d, g)
        if stage == "pA":
            return nc
        phaseB(k, d, g, True)
        if stage == "pB":
            return nc
        phaseC(k, d, g, 0, d["x"], d["out"] if stage == "L0" else d["X1"], stage != "L0")
        if stage == "L0":
            return nc
        rwkv_layer(k, d, g, stage)
    return nc


def host_inputs(inp):
    f = lambda a: np.ascontiguousarray(np.asarray(a), dtype=np.float32)
    w_in = f(inp["mla_w_in"])[0]
    sw = (np.arange(32) + 16) % 32
    kr_cols = 640 + np.arange(32)
    WA = np.concatenate([w_in[:, 0:384], w_in[:, 384:640], w_in[:, 672:928], w_in[:, kr_cols], w_in[:, kr_cols[sw]]], axis=1)
    wq = f(inp["mla_w_q_up"])[0].reshape(384, 12, 96)
    WQ = np.concatenate([wq[:, :, 0:64].reshape(384, 768), wq[:, :, 64:96].reshape(384, 384),
                         wq[:, :, 64:96][:, :, sw].reshape(384, 384)], axis=1)
    wkv = f(inp["mla_w_kv_up"])[0].reshape(256, 12, 128)
    WKV = np.concatenate([wkv[:, :, 0:64].reshape(256, 768), wkv[:, :, 64:128].reshape(256, 768)], axis=1)
    p = np.arange(128)
    invf = (10000.0 ** (-(np.arange(16, dtype=np.float32)) * 2.0 / 32.0)).astype(np.float32)
    common = {
        "mem_ln_g": f(inp["mem_ln_g"]).reshape(1, D), "mem_ln_b": f(inp["mem_ln_b"]).reshape(1, D),
        "w_mem_kv": f(inp["w_mem_kv"]),
        "WA": f(WA), "qn": f(f(inp["mla_q_norm"])[0].reshape(3, 128).T), "WQ": f(WQ),
        "kvn": f(f(inp["mla_kv_norm"])[0].reshape(2, 128).T), "WKV": f(WKV),
        "w_out": f(inp["w_out"]), "ln1_g": f(inp["ln1_g"]), "ln1_b": f(inp["ln1_b"]),
        "w_ff1": f(inp["w_ff1"]), "w_ff2": f(inp["w_ff2"]), "ln2_g": f(inp["ln2_g"]), "ln2_b": f(inp["ln2_b"]),
        "identF": np.eye(128, dtype=np.float32),
        "tri": f(p[:, None] <= p[None, :]),
        "invf": f((invf[p % 16] / np.float32(2.0 * np.pi)).reshape(128, 1)),
        "sgn": f(np.where((p % 32) < 16, -1.0, 1.0).reshape(128, 1)),
    }
    common.update(rwkv_host_inputs(inp))
    maps = []
    x = np.asarray(inp["x"])
    mem = np.asarray(inp["mem"])
    pos = np.asarray(inp["positions"])
    for b in range(8):
        m = dict(common)
        m["x"] = f(x[b])
        m["mem"] = f(mem[b])
        m["pos"] = np.ascontiguousarray(pos[b].reshape(1, T).astype(np.int32))
        maps.append(m)
    return maps


def kernel(**inputs):
    if "nc" not in _CACHE:
        _CACHE["nc"] = build("full")
    maps = host_inputs(inputs)
    res = run_bass_kernel_spmd(_CACHE["nc"], maps, core_ids=list(range(8)))
    return np.stack([np.asarray(r["out"], dtype=np.float32) for r in res.results], axis=0)


NCH = T // 64
WF_COLS = 1824
RW_IN_SPECS = [
    ("WF", [D, WF_COLS], F32), ("MUF", [1, WF_COLS], F32), ("WV", [D, 768], F32), ("MUV", [1, 768], F32),
    ("WQM", [D, 256], F32), ("w2a2", [128, 768], F32), ("g2a", [128, 768], F32), ("g2b", [32, 768], F32),
    ("PRM", [128, 6, 5], F32), ("gn_g", [1, 768], F32), ("gn_b", [1, 768], F32),
    ("RST", [128, 512], F32), ("SEL", [128, 2], F32), ("BONES", [128, 128], F32),
    ("MK1", [128, 512], F32), ("ML", [128, 512], F32), ("IDN", [128, 512], F32),
]
RW_SCRATCH = [
    ("ARd", [6, 128, NCH, 128], BF16), ("KBd", [6, 128, NCH, 128], BF16),
    ("KBWd", [12, 64, NCH, 2, 64], BF16), ("Vcd", [12, 64, NCH, 64], BF16),
    ("Vtd", [T, 768], BF16), ("Gtd", [T, 768], F32), ("BONd", [T, 12], F32),
    ("WCd", [6, 128, NCH], F32), ("Ysd", [T, 768], F32),
]


def rwkv_host_inputs(inp):
    f = lambda a: np.ascontiguousarray(np.asarray(a), dtype=np.float32)
    w = f(inp["rwkv_w_in"])[0]
    mu = f(inp["rwkv_mu"])[0]
    fc = np.concatenate([np.arange(0, 1536), np.arange(2304, 2592)])
    col = lambda v: f(f(v).reshape(6, 128).T)
    PRM = np.stack([col(inp["rwkv_w0"][0]), col(inp["rwkv_a0"][0]), col(inp["rwkv_k_k"][0]),
                    col(inp["rwkv_k_a"][0]), col(np.asarray(inp["rwkv_r_k"])[0].reshape(768))], axis=-1)
    p = np.arange(128)
    c = np.arange(512)
    s_ = (p % 64)[:, None]
    t_ = (c % 64)[None, :]
    is_a = ((c // 64) % 2 == 1)[None, :]
    return {
        "WF": f(w[:, fc]), "MUF": f(mu[fc].reshape(1, -1)), "WV": f(w[:, 1536:2304]), "MUV": f(mu[1536:2304].reshape(1, -1)),
        "WQM": f(w[:, 2592:2848]),
        "w2a2": f(np.concatenate([f(inp["rwkv_w2"])[0], f(inp["rwkv_a2"])[0]], axis=0)),
        "g2a": f(f(inp["rwkv_g2"])[0][0:128]), "g2b": f(f(inp["rwkv_g2"])[0][128:160]),
        "PRM": f(PRM), "gn_g": f(inp["rwkv_gn_g"]).reshape(1, 768), "gn_b": f(inp["rwkv_gn_b"]).reshape(1, 768),
        "RST": f(np.broadcast_to((c % 64 != 0)[None, :], (128, 512))),
        "SEL": f(np.stack([p < 64, p >= 64], axis=1)),
        "BONES": f((p[:, None] // 64) == (p[None, :] // 64)),
        "MK1": f(np.where(is_a, s_ < t_, s_ <= t_)),
        "ML": f(s_ > t_),
        "IDN": f(s_ == t_),
    }


def rwkv_prep(k, d, g, es_w):
    S = k.S
    W = {}
    W["F1"] = k.sb(es_w, "WF1", [128, 8, WF_COLS], BF16)
    W["F2"] = k.sb(es_w, "WF2", [128, 8, WF_COLS], BF16)
    W["V1"] = k.sb(es_w, "WV1", [128, 8, 768], BF16)
    W["V2"] = k.sb(es_w, "WV2", [128, 8, 768], BF16)
    small = [("QM", "WQM", D, 256), ("w2a2", "w2a2", 128, 768), ("g2a", "g2a", 128, 768), ("g2b", "g2b", 32, 768),
             ("bones", "BONES", 128, 128), ("sel", "SEL", 128, 2)]
    for nm, src_, R_, C_ in small:
        W[nm] = k.sb(es_w, nm + "_b", [128, (R_ + 127) // 128, C_], BF16)
    with contextlib.ExitStack() as es:
        stg = Stage(k, es, n=2, cols=WF_COLS)
        for (src, mus, C, t1, t2) in ((d["WF"], d["MUF"], WF_COLS, W["F1"], W["F2"]), (d["WV"], d["MUV"], 768, W["V1"], W["V2"])):
            MU = bcast_load(k, es, "MU%d" % C, mus, C)
            OM, OMb = k.sb(es, "OM%d" % C, [128, C], F32)
            k.ts("dve", OM[:], MU[0][:], -1.0, 1.0, ALU.mult, ALU.add, [MU[1]], [OMb])
            for c in range(8):
                st, stb = stg.nxt()
                k.dma(st[:, 0:C], src[c * 128:(c + 1) * 128, :], stb, writes=[stb])
                k.tt("dve", t1[0][:, c, :], st[:, 0:C], OM[:], ALU.mult, [stb, OMb], [t1[1]], partial=True)
                k.tt("pool", t2[0][:, c, :], st[:, 0:C], MU[0][:], ALU.mult, [stb, MU[1]], [t2[1]], partial=True)
        for nm, src_, R_, C_ in small:
            load_w(k, es_w, stg, nm, d[src_], R_, C_, dst=W[nm])
        S.flush()
    return W


def phaseD(k, d, g):
    S = k.S
    with contextlib.ExitStack() as es:
        W = rwkv_prep(k, d, g, es)
        F1, F1b = W["F1"]
        F2, F2b = W["F2"]
        V1, V1b = W["V1"]
        V2, V2b = W["V2"]
        prm, prmb = k.sb(es, "prm", [128, 6, 6], F32)
        k.dma(prm[:, :, 0:5], d["PRM"], prmb, writes=[prmb])
        k.ts("dve", prm[:, :, 5:6], prm[:, :, 3:4], -1.0, 1.0, ALU.mult, ALU.add, [prmb], [prmb])
        rst, rstb = k.sb(es, "rst", [128, 512], F32)
        k.dma(rst[:], d["RST"], rstb, writes=[rstb])
        xThs = [k.sb(es, "xTh%d" % i, [128, 8, 514], BF16) for i in range(2)]
        lr, lrb = k.sb(es, "lr", [128, 512], BF16)
        SG, SGb = k.sb(es, "SG", [128, 512], BF16)
        SG2, SG2b = k.sb(es, "SG2", [32, 512], BF16)
        vtok, vtokb = k.sb(es, "vtok", [128, 768], BF16)
        gtok, gtokb = k.sb(es, "gtok", [128, 768], F32)
        obs = [k.sb(es, "qmo%d" % i, [128, 512], BF16) for i in range(2)]
        fnames = ["r_s", "k_s", "lw", "a_s", "kk", "nrm", "kp", "bsc", "cw", "cwx", "dlt", "E1", "E2", "kW", "bW"]
        Fws = [{n: k.sb(es, "%s_%d" % (n, i), [128, 512], F32) for n in fnames} for i in range(2)]
        for Fw_ in Fws:
            Fw_["kkn"] = Fw_["kk"]
            Fw_["tmp"] = Fw_["kp"]
            Fw_["E3"] = Fw_["cwx"]
            Fw_["E4"] = Fw_["dlt"]
        kk2s = [k.sb(es, "kk2_%d" % i, [128, 512], BF16) for i in range(2)]
        rkps = [k.sb(es, "rkp_%d" % i, [128, 512], BF16) for i in range(2)]
        kbws = [k.sb(es, "kbw_%d" % i, [128, 2, 2, 64], BF16) for i in range(4)]
        ARts = [k.sb(es, "ARt%d" % i, [128, 8, 2, 64], BF16) for i in range(2)]
        KBts = [k.sb(es, "KBt%d" % i, [128, 8, 2, 64], BF16) for i in range(2)]
        wcs = [k.sb(es, "wc%d" % i, [128, 8], F32) for i in range(2)]
        bon, bonb = k.sb(es, "bon", [128, 4, 12], F32)
        XTv = d["XT"].rearrange("(c p) t -> p c t", p=128)
        KBWv = d["KBWd"].rearrange("h s c b n -> s h c b n")
        Vcv = d["Vcd"].rearrange("h s c n -> s h c n")
        BONv = d["BONd"].rearrange("(a p) h -> p a h", p=128)
        w2a2, w2a2b = W["w2a2"]
        g2a, g2ab = W["g2a"]
        g2b, g2bb = W["g2b"]
        bones, bonesb = W["bones"]
        sel, selb = W["sel"]
        identF = g["identF"]

        def proj(pt, ptb, rows, c0, c1, xTh, xThb):
            for c in range(8):
                k.mm(pt[0:rows, :], F1[:, c, c0:c1], xTh[:, c, 1:513], c == 0, False, [F1b, xThb], [ptb])
            for c in range(8):
                k.mm(pt[0:rows, :], F2[:, c, c0:c1], xTh[:, c, 0:512], False, c == 7, [F2b, xThb], [ptb])

        for s in range(T // 512):
            tk0 = s * 512
            xTh, xThb = xThs[s % 2]
            if s == 0:
                k.memset("pool", xTh[:, :, 0:1], 0.0, [xThb])
                k.dma(xTh[:, :, 1:513], XTv[:, :, 0:512], xThb, writes=[xThb], partial=True)
            if s + 1 < T // 512:
                nx, nxb = xThs[(s + 1) % 2]
                k.dma(nx[:, :, 0:513], XTv[:, :, tk0 + 511:tk0 + 1024], nxb, writes=[nxb])
            pt, ptb = k.nps()
            proj(pt, ptb, 128, 1536, 1664, xTh, xThb)
            k.act(lr[0:64, :], pt[0:64, :], AF.Tanh, [ptb], [lrb])
            k.cp("act", lr[64:128, :], pt[64:128, :], [ptb], [lrb], partial=True)
            pt, ptb = k.nps()
            proj(pt, ptb, 128, 1664, 1792, xTh, xThb)
            k.act(SG[:], pt[:], AF.Sigmoid, [ptb], [SGb])
            pt, ptb = k.nps()
            proj(pt, ptb, 32, 1792, 1824, xTh, xThb)
            k.act(SG2[0:32, :], pt[0:32, :], AF.Sigmoid, [ptb], [SG2b])
            def front_b():
                WQ_, WQb_ = W["QM"]
                for j in range(2):
                    pt, ptb = k.nps()
                    for c in range(8):
                        k.mm(pt[:], WQ_[:, c, j * 128:(j + 1) * 128], xTh[:, c, 1:513], c == 0, c == 7, [WQb_, xThb], [ptb])
                    o, ob = obs[j]
                    k.cp("act", o[:], pt[:], [ptb], [ob])
                    yield
                    k.dma(d["QM"][j * 128:(j + 1) * 128, tk0:tk0 + 512], o[:], ob, reads=[ob])
                    yield
                for t4 in range(4):
                    tk = slice(tk0 + t4 * 128, tk0 + (t4 + 1) * 128)
                    for (c0, c1) in ((0, 512), (512, 768)):
                        pt, ptb = k.nps()
                        n = c1 - c0
                        for c in range(8):
                            k.mm(pt[:, 0:n], xTh[:, c, 1 + t4 * 128:1 + (t4 + 1) * 128], V1[:, c, c0:c1], c == 0, False, [V1b, xThb], [ptb])
                        for c in range(8):
                            k.mm(pt[:, 0:n], xTh[:, c, t4 * 128:(t4 + 1) * 128], V2[:, c, c0:c1], False, c == 7, [V2b, xThb], [ptb])
                        k.cp("act", vtok[:, c0:c1], pt[:, 0:n], [ptb], [vtokb], partial=(c0 > 0))
                        yield
                    k.dma(d["Vtd"][tk, :], vtok[:], vtokb, reads=[vtokb])
                    yield
                    for half in range(2):
                        cg = (s * 4 + t4) * 2 + half
                        k.dma(Vcv[:, :, cg, :], vtok[half * 64:(half + 1) * 64, :].rearrange("p (h n) -> p h n", h=12), vtokb, reads=[vtokb])
                        yield
                    for (c0, c1) in ((0, 512), (512, 768)):
                        pt, ptb = k.nps()
                        n = c1 - c0
                        k.mm(pt[:, 0:n], SG[:, t4 * 128:(t4 + 1) * 128], g2a[:, 0, c0:c1], True, False, [SGb, g2ab], [ptb])
                        k.mm(pt[:, 0:n], SG2[0:32, t4 * 128:(t4 + 1) * 128], g2b[0:32, 0, c0:c1], False, True, [SG2b, g2bb], [ptb])
                        k.cp("dve", gtok[:, c0:c1], pt[:, 0:n], [ptb], [gtokb], partial=(c0 > 0))
                        yield
                    k.dma(d["Gtd"][tk, :], gtok[:], gtokb, reads=[gtokb])
                    yield
            pbon_i, (pbon, pbonb) = k.reserve()
            def pair(p):
                ARt, ARtb = ARts[p % 2]
                KBt, KBtb = KBts[p % 2]
                wc, wcb = wcs[p % 2]
                Fw = Fws[p % 2]
                kk2, kk2b = kk2s[p % 2]
                rkp, rkpb = rkps[p % 2]
                f = lambda n, Fw=Fw: Fw[n][0]
                fb = lambda n, Fw=Fw: Fw[n][1]
                pcol = lambda i: prm[:, p, i:i + 1]
                psr, psrb = k.nps()
                proj(psr, psrb, 128, p * 128, (p + 1) * 128, xTh, xThb)
                yield
                k.cp("act", f("r_s")[:], psr[:], [psrb], [fb("r_s")])
                yield
                psk, pskb = k.nps()
                proj(psk, pskb, 128, 768 + p * 128, 768 + (p + 1) * 128, xTh, xThb)
                yield
                k.cp("dve", f("k_s")[:], psk[:], [pskb], [fb("k_s")])
                yield
                k.act(kk2[:], psk[:], AF.Square, [pskb, prmb], [kk2b], scale=pcol(2))
                yield
                psw, pswb = k.nps()
                k.mm(psw[:], w2a2[0:64, 0, p * 128:(p + 1) * 128], lr[0:64, :], True, True, [w2a2b, lrb], [pswb])
                yield
                k.act(f("lw")[:], psw[:], AF.Sigmoid, [pswb, prmb], [fb("lw")], bias=pcol(0))
                yield
                k.ts("dve", f("lw")[:], f("lw")[:], -float(np.exp(-0.5)), None, ALU.mult, None, [fb("lw")], [fb("lw")])
                yield
                psa, psab = k.nps()
                k.mm(psa[:], w2a2[64:128, 0, p * 128:(p + 1) * 128], lr[64:128, :], True, True, [w2a2b, lrb], [psab])
                yield
                k.act(f("a_s")[:], psa[:], AF.Sigmoid, [psab, prmb], [fb("a_s")], bias=pcol(1))
                yield
                k.ts("dve", f("kk")[:], f("k_s")[:], pcol(2), None, ALU.mult, None, [fb("k_s"), prmb], [fb("kk")])
                yield
                pss, pssb = k.nps()
                k.mm(pss[:], bones[:, 0, :], kk2[:], True, True, [bonesb, kk2b], [pssb])
                yield
                k.ts("dve", f("nrm")[:], pss[:], 1e-24, None, ALU.add, None, [pssb], [fb("nrm")])
                yield
                k.act(f("nrm")[:], f("nrm")[:], AF.Ln, [fb("nrm")], [fb("nrm")])
                yield
                k.act(f("nrm")[:], f("nrm")[:], AF.Exp, [fb("nrm")], [fb("nrm")], scale=-0.5)
                yield
                k.tt("pool", f("kkn")[:], f("kk")[:], f("nrm")[:], ALU.mult, [fb("kk"), fb("nrm")], [fb("kkn")])
                yield
                k.ts("dve", f("tmp")[:], f("a_s")[:], pcol(3), pcol(5), ALU.mult, ALU.add, [fb("a_s"), prmb], [fb("tmp")])
                yield
                k.tt("pool", f("kp")[:], f("k_s")[:], f("tmp")[:], ALU.mult, [fb("k_s"), fb("tmp")], [fb("kp")])
                yield
                k.tt("pool", f("bsc")[:], f("kkn")[:], f("a_s")[:], ALU.mult, [fb("kkn"), fb("a_s")], [fb("bsc")])
                yield
                k.S.op("dve", lambda e, Fw=Fw: e.tensor_tensor_scan(out=Fw["cw"][0][:], data0=rst[:], data1=Fw["lw"][0][:], initial=0.0,
                                                                     op0=ALU.mult, op1=ALU.add), [rstb, fb("lw")], [fb("cw")])
                yield
                k.tt("pool", f("cwx")[:], f("cw")[:], f("lw")[:], ALU.subtract, [fb("cw"), fb("lw")], [fb("cwx")])
                yield
                cw3 = f("cw")[:].rearrange("p (c t) -> p c t", t=64)
                k.tt("dve", f("dlt")[:].rearrange("p (c t) -> p c t", t=64), cw3[:, :, 63:64].to_broadcast([128, 8, 64]), cw3,
                     ALU.subtract, [fb("cw")], [fb("dlt")])
                yield
                k.act(f("E1")[:], f("cw")[:], AF.Exp, [fb("cw")], [fb("E1")])
                yield
                k.act(f("E2")[:], f("cw")[:], AF.Exp, [fb("cw")], [fb("E2")], scale=-1.0)
                yield
                k.act(f("E3")[:], f("cwx")[:], AF.Exp, [fb("cwx")], [fb("E3")])
                yield
                k.act(f("E4")[:], f("dlt")[:], AF.Exp, [fb("dlt")], [fb("E4")])
                yield
                k.act(wc[:].rearrange("p (c o) -> p c o", o=1), cw3[:, :, 63:64], AF.Exp, [fb("cw")], [wcb])
                yield
                v4 = lambda t_, i: t_[:, :, i, :]
                k.tt("pool", v4(ARt, 0), f("r_s")[:].rearrange("p (c t) -> p c t", t=64), f("E1")[:].rearrange("p (c t) -> p c t", t=64),
                     ALU.mult, [fb("r_s"), fb("E1")], [ARtb])
                yield
                k.stt(v4(ARt, 1), f("kkn")[:].rearrange("p (c t) -> p c t", t=64), -1.0, f("E3")[:].rearrange("p (c t) -> p c t", t=64),
                      ALU.mult, ALU.mult, [fb("kkn"), fb("E3")], [ARtb], partial=True)
                yield
                r3 = lambda t_, lo: t_[lo:lo + 64, :].rearrange("p (c t) -> p c t", t=64)
                for lo in (0, 64):
                    ki, bi = (0, 1) if lo == 0 else (1, 0)
                    k.tt("pool", KBt[lo:lo + 64, :, ki, :], r3(f("kp"), lo), r3(f("E2"), lo), ALU.mult, [fb("kp"), fb("E2")], [KBtb],
                         partial=(lo == 64))
                    yield
                    k.tt("dve", KBt[lo:lo + 64, :, bi, :], r3(f("bsc"), lo), r3(f("E2"), lo), ALU.mult, [fb("bsc"), fb("E2")], [KBtb],
                         partial=True)
                    yield
                k.tt("pool", f("kW")[:], f("kp")[:], f("E4")[:], ALU.mult, [fb("kp"), fb("E4")], [fb("kW")])
                yield
                k.tt("dve", f("bW")[:], f("bsc")[:], f("E4")[:], ALU.mult, [fb("bsc"), fb("E4")], [fb("bW")])
                yield
                k.stt(rkp[:], f("r_s")[:], pcol(4), f("kp")[:], ALU.mult, ALU.mult, [fb("r_s"), fb("kp"), prmb], [rkpb])
                yield
                k.dma(d["ARd"][p][:, s * 8:(s + 1) * 8, :], ARt[:].rearrange("p c a t -> p c (a t)"), ARtb, reads=[ARtb])
                yield
                k.dma(d["KBd"][p][:, s * 8:(s + 1) * 8, :], KBt[:].rearrange("p c a t -> p c (a t)"), KBtb, reads=[KBtb])
                yield
                k.dma(d["WCd"][p][:, s * 8:(s + 1) * 8], wc[:], wcb, reads=[wcb])
                yield
                for t4 in range(4):
                    k.mm(pbon[:, t4 * 12 + 2 * p:t4 * 12 + 2 * p + 2], rkp[:, t4 * 128:(t4 + 1) * 128], sel[:, 0, :], True, True,
                         [rkpb, selb], [pbonb])
                    yield
                    kbw, kbwb = kbws[(p % 2) * 2 + t4 % 2]
                    pT, pTb = k.nps()
                    k.tp(pT[:, 0:128], f("kW")[:, t4 * 128:(t4 + 1) * 128], identF[0][:], [fb("kW"), identF[1]], [pTb], first=True)
                    yield
                    k.tp(pT[:, 128:256], f("bW")[:, t4 * 128:(t4 + 1) * 128], identF[0][:], [fb("bW"), identF[1]], [pTb])
                    yield
                    k.cp(k.alt(), kbw[:].rearrange("p h b n -> p b h n"), pT[:, 0:256].rearrange("p (b h n) -> p b h n", b=2, h=2),
                         [pTb], [kbwb])
                    yield
                    for half in range(2):
                        cg = (s * 4 + t4) * 2 + half
                        k.dma(KBWv[:, 2 * p:2 * p + 2, cg, :, :], kbw[half * 64:(half + 1) * 64, :, :, :], kbwb, reads=[kbwb])
                        yield
            interleave([front_b(), interleave_gen((pair(p) for p in range(6)), 2)], 2)
            k.cp("act", bon[:].rearrange("p a h -> p (a h)"), pbon[:, 0:48], [pbonb], [bonb])
            k.release(pbon_i)
            k.dma(BONv[:, s * 4:(s + 1) * 4, :], bon[:], bonb, reads=[bonb])
    S.flush()


def phaseE(k, d, g):
    S = k.S
    with contextlib.ExitStack() as es:
        stg = Stage(k, es, n=2, cols=512)
        MK1, MK1b = load_w(k, es, stg, "MK1b", d["MK1"], 128, 512)
        ML, MLb = load_w(k, es, stg, "MLb", d["ML"], 128, 512)
        IDN, IDNb = load_w(k, es, stg, "IDNb", d["IDN"], 128, 512)
        STs = [k.sb(es, "ST%d" % i, [128, 3, 64], F32) for i in range(2)]
        SBs = [k.sb(es, "SB%d" % i, [128, 3, 64], BF16) for i in range(2)]
        for i in range(2):
            k.memset("pool", STs[i][0][:], 0.0, [STs[i][1]])
            k.memset("pool", SBs[i][0][:], 0.0, [SBs[i][1]])
        sets = []
        for i in range(2):
            sets.append({
                "AR": [k.sb(es, "AR%d_%d" % (i, p), [128, 8, 128], BF16) for p in range(6)],
                "KB": [k.sb(es, "KB%d_%d" % (i, p), [128, 8, 128], BF16) for p in range(6)],
                "KW": [k.sb(es, "KW%d_%d" % (i, h), [128, 8, 64], BF16) for h in range(12)],
                "VU": [k.sb(es, "VU%d_%d" % (i, gI), [128, 3, 8, 2, 64], BF16) for gI in range(2)],
                "WC": k.sb(es, "WC%d" % i, [128, 6, 8], F32),
            })
        M1s = [[k.sb(es, "M1_%d_%d" % (i, h), [128, 8, 128], BF16) for h in range(12)] for i in range(2)]
        PFs = [[k.sb(es, "PF_%d_%d" % (i, p), [128, 8, 128], BF16) for p in range(6)] for i in range(2)]
        XD = [k.sb(es, "XD%d" % i, [128, 8, 128], BF16) for i in range(3)]
        XTD = [k.sb(es, "XTD%d" % i, [128, 8, 128], BF16) for i in range(3)]
        PD = [k.sb(es, "PD%d" % i, [128, 8, 128], BF16) for i in range(3)]
        Zs = [k.sb(es, "Zs%d" % i, [128, 192], BF16) for i in range(2)]
        Yb = [k.sb(es, "Yb%d" % i, [64, 768], F32) for i in range(2)]
        cnt = {"x": 0}
        KBWv = d["KBWd"]

        def loads(b):
            st = sets[b % 2]
            cs = slice(b * 8, (b + 1) * 8)
            for p in range(6):
                k.dma(st["AR"][p][0][:], d["ARd"][p][:, cs, :], st["AR"][p][1], writes=[st["AR"][p][1]])
                k.dma(st["KB"][p][0][:], d["KBd"][p][:, cs, :], st["KB"][p][1], writes=[st["KB"][p][1]])
            k.dma(st["WC"][0][:], d["WCd"].rearrange("p q c -> q p c")[:, :, cs], st["WC"][1], writes=[st["WC"][1]])
            for h in range(12):
                e = h % 2
                kp_, bp_ = e * 64, 64 - e * 64
                kw, kwb = st["KW"][h]
                k.dma(kw[kp_:kp_ + 64, :, :], KBWv[h][:, cs, 0, :], kwb, writes=[kwb])
                k.dma(kw[bp_:bp_ + 64, :, :], KBWv[h][:, cs, 1, :], kwb, writes=[kwb], partial=True)
                vu, vub = st["VU"][h // 6]
                k.dma(vu[kp_:kp_ + 64, (h % 6) // 2, :, e, :], d["Vcd"][h][:, cs, :], vub, writes=[vub], partial=True)

        def setup(b):
            for p in range(6):
                yield from setup_pair(b, p)

        def setup_pair(b, p):
            st = sets[b % 2]
            AR, ARb = st["AR"][p]
            KB, KBb = st["KB"][p]
            for e in range(2):
                h, hp = 2 * p + e, e * 64
                m1, m1b = M1s[b % 2][h]
                for half in range(2):
                    ps, psb = k.nps()
                    for j4 in range(4):
                        j = half * 4 + j4
                        k.mm(ps[:, j4 * 128:(j4 + 1) * 128], KB[hp:hp + 64, j, :], AR[hp:hp + 64, j, :], True, True, [KBb, ARb], [psb])
                    k.tt("dve", m1[:, half * 4:(half + 1) * 4, :].rearrange("p c n -> p (c n)"), ps[:], MK1[:, 0, :], ALU.mult,
                         [psb, MK1b], [m1b], partial=(half == 1))
                    yield
            ci = cnt["x"]
            cnt["x"] += 1
            X, Xb = XD[ci % 3]
            XT, XTb = XTD[ci % 3]
            P, Pb = PD[ci % 3]
            k.memset("pool", X[:], 0.0, [Xb])
            k.memset("pool", XT[:], 0.0, [XTb])
            k.memset("pool", P[:], 0.0, [Pb])
            yield
            v3 = lambda a: a.rearrange("p (c n) -> p c n", n=64)
            for e in range(2):
                h, hp, bp_ = 2 * p + e, e * 64, 64 - e * 64
                m1, m1b = M1s[b % 2][h]
                bc = slice(64, 128) if e == 0 else slice(0, 64)
                ps, psb = k.nps()
                for j in range(8):
                    k.mm(ps[bp_:bp_ + 64, j * 64:(j + 1) * 64], AR[hp:hp + 64, j, 64:128], KB[hp:hp + 64, j, bc], True, True, [ARb, KBb], [psb])
                k.tt("dve", X[bp_:bp_ + 64, :, bp_:bp_ + 64], v3(ps[bp_:bp_ + 64, :]), v3(ML[bp_:bp_ + 64, 0, :]), ALU.mult,
                     [psb, MLb], [Xb], partial=True)
                k.cp("pool", XT[bp_:bp_ + 64, :, bp_:bp_ + 64], m1[bp_:bp_ + 64, :, 64:128], [m1b], [XTb], partial=True)
                k.tt("pool", P[bp_:bp_ + 64, :, bp_:bp_ + 64], m1[bp_:bp_ + 64, :, 64:128], v3(IDN[bp_:bp_ + 64, 0, :]), ALU.add,
                     [m1b, IDNb], [Pb], partial=True)
                yield
            fl = lambda T_, half: T_[:, half * 4:(half + 1) * 4, :].rearrange("p c n -> p (c n)")
            for lvl in range(1, 6):
                ci = cnt["x"]
                cnt["x"] += 1
                Xn, Xnb = XD[ci % 3]
                for half in range(2):
                    ps, psb = k.nps()
                    for j4 in range(4):
                        j = half * 4 + j4
                        k.mm(ps[:, j4 * 128:(j4 + 1) * 128], XT[:, j, :], X[:, j, :], True, True, [XTb, Xb], [psb])
                    k.cp("act", fl(Xn, half), ps[:], [psb], [Xnb], partial=(half == 1))
                    yield
                if lvl < 5:
                    XTn, XTnb = XTD[ci % 3]
                    for half in range(2):
                        ps, psb = k.nps()
                        for j4 in range(4):
                            j = half * 4 + j4
                            k.mm(ps[:, j4 * 128:(j4 + 1) * 128], X[:, j, :], XT[:, j, :], True, True, [XTb, Xb], [psb])
                        k.cp("act" if half == 0 else "dve", fl(XTn, half), ps[:], [psb], [XTnb], partial=(half == 1))
                        yield
                Pn, Pnb = PD[ci % 3] if lvl < 5 else PFs[b % 2][p]
                for half in range(2):
                    ps, psb = k.nps()
                    for j4 in range(4):
                        j = half * 4 + j4
                        k.mm(ps[:, j4 * 128:(j4 + 1) * 128], Xn[:, j, :], P[:, j, :], True, True, [Xnb, Pb], [psb])
                    k.tt("dve", fl(Pn, half), ps[:], fl(P, half), ALU.add, [psb, Pb], [Pnb], partial=(half == 1))
                    yield
                X, Xb = Xn, Xnb
                if lvl < 5:
                    XT, XTb = XTn, XTnb
                P, Pb = Pn, Pnb

        def pull(gen, n):
            if gen is None:
                return
            for _ in range(n):
                try:
                    next(gen)
                except StopIteration:
                    return

        def sequential(b, gen):
            st = sets[b % 2]
            M1, PF = M1s[b % 2], PFs[b % 2]
            WC, WCb = st["WC"]
            for j in range(8):
                cg = b * 8 + j
                for gI in range(2):
                    vu, vub = st["VU"][gI]
                    SB, SBb = SBs[gI]
                    ps, psb = k.nps()
                    for m in range(6):
                        h = gI * 6 + m
                        p, e, pl = h // 2, h % 2, m // 2
                        hp, kp_, bp_ = e * 64, e * 64, 64 - e * 64
                        AR, ARb = st["AR"][p]
                        m1, m1b = M1[h]
                        k.mm(ps[bp_:bp_ + 64, pl * 64:(pl + 1) * 64], m1[kp_:kp_ + 64, j, 64:128], vu[kp_:kp_ + 64, pl, j, e, :],
                             True, False, [m1b, vub], [psb])
                        k.mm(ps[bp_:bp_ + 64, pl * 64:(pl + 1) * 64], AR[hp:hp + 64, j, 64:128], SB[hp:hp + 64, pl, :],
                             False, True, [ARb, SBb], [psb])
                    zs, zsb = Zs[gI]
                    k.cp("act", zs[:], ps[:, 0:192], [psb], [zsb])
                pull(gen, 10)
                for gI in range(2):
                    vu, vub = st["VU"][gI]
                    zs, zsb = Zs[gI]
                    ps, psb = k.nps()
                    for m in range(6):
                        h = gI * 6 + m
                        e, pl = h % 2, m // 2
                        bp_ = 64 - e * 64
                        pf, pfb = PF[h // 2]
                        k.mm(ps[bp_:bp_ + 64, pl * 64:(pl + 1) * 64], pf[bp_:bp_ + 64, j, bp_:bp_ + 64], zs[bp_:bp_ + 64, pl * 64:(pl + 1) * 64],
                             True, True, [pfb, zsb], [psb])
                    for e in range(2):
                        bp_ = 64 - e * 64
                        k.cp("act", vu[bp_:bp_ + 64, :, j, e, :], ps[bp_:bp_ + 64, 0:192].rearrange("p (a n) -> p a n", n=64), [psb], [vub],
                             partial=True)
                pull(gen, 10)
                for gI in range(2):
                    vu, vub = st["VU"][gI]
                    SB, SBb = SBs[gI]
                    ST, STb_ = STs[gI]
                    pss, pssb = k.nps()
                    for m in range(6):
                        h = gI * 6 + m
                        p, e, pl = h // 2, h % 2, m // 2
                        hp = e * 64
                        kw, kwb = st["KW"][h]
                        k.mm(pss[hp:hp + 64, pl * 64:(pl + 1) * 64], kw[:, j, :], vu[:, pl, j, e, :], True, True, [kwb, vub], [pssb])
                    psy, psyb = k.nps()
                    for m in range(6):
                        h = gI * 6 + m
                        p, e, pl = h // 2, h % 2, m // 2
                        hp = e * 64
                        AR, ARb = st["AR"][p]
                        m1, m1b = M1[h]
                        k.mm(psy[0:64, m * 64:(m + 1) * 64], m1[:, j, 0:64], vu[:, pl, j, e, :], True, False, [m1b, vub], [psyb])
                        k.mm(psy[0:64, m * 64:(m + 1) * 64], AR[hp:hp + 64, j, 0:64], SB[hp:hp + 64, pl, :], False, True, [ARb, SBb], [psyb])
                    for pl in range(3):
                        p = gI * 3 + pl
                        k.stt(ST[:, pl, :], ST[:, pl, :], WC[:, p, j:j + 1], pss[:, pl * 64:(pl + 1) * 64], ALU.mult, ALU.add,
                              [STb_, WCb, pssb], [STb_])
                    k.cp("pool", SB[:], ST[:], [STb_], [SBb])
                    yb, ybb = Yb[j % 2]
                    k.cp("act", yb[0:64, gI * 384:(gI + 1) * 384], psy[0:64, 0:384], [psyb], [ybb], partial=(gI == 1))
                yb, ybb = Yb[j % 2]
                k.dma(d["Ysd"][cg * 64:(cg + 1) * 64, :], yb[0:64, :], ybb, reads=[ybb])
                pull(gen, 10)

        nb = NCH // 8
        loads(0)
        g0 = setup(0)
        pull(g0, 10 ** 6)
        for b in range(nb):
            gen = None
            if b + 1 < nb:
                loads(b + 1)
                gen = setup(b + 1)
            sequential(b, gen)
            pull(gen, 10 ** 6)
    S.flush()


def phaseF(k, d, g):
    S = k.S
    identF = g["identF"]
    NB = 3
    with contextlib.ExitStack() as es:
        GG = bcast_load(k, es, "gnG", d["gn_g"], 768)
        GB = bcast_load(k, es, "gnB", d["gn_b"], 768)
        NL = NB + 1
        ys = [k.sb(es, "y%d" % i, [128, 768], F32) for i in range(NL)]
        vs = [k.sb(es, "v%d" % i, [128, 768], BF16) for i in range(NL)]
        gs = [k.sb(es, "g%d" % i, [128, 768], F32) for i in range(NL)]
        bs = [k.sb(es, "b%d" % i, [128, 12], F32) for i in range(NL)]
        sqs = [k.sb(es, "sq%d" % i, [128, 768], F32) for i in range(NB)]
        t2s = [k.sb(es, "t2%d" % i, [128, 768], F32) for i in range(NB)]
        sts = [k.sb(es, "st%d" % i, [128, 4, 12], F32) for i in range(NB)]
        oTs = [k.sb(es, "oT%d" % i, [128, 6, 128], BF16) for i in range(NB)]
        OTv = d["OT"].rearrange("(c p) t -> p c t", p=128)
        v3 = lambda a: a.rearrange("p (h n) -> p h n", n=64)
        bc = lambda a: a.rearrange("p (h o) -> p h o", o=1).to_broadcast([128, 12, 64])

        def floads(t):
            tk = slice(t * 128, (t + 1) * 128)
            k.dma(ys[t % NL][0][:], d["Ysd"][tk, :], ys[t % NL][1], writes=[ys[t % NL][1]])
            k.dma(vs[t % NL][0][:], d["Vtd"][tk, :], vs[t % NL][1], writes=[vs[t % NL][1]])
            k.dma(gs[t % NL][0][:], d["Gtd"][tk, :], gs[t % NL][1], writes=[gs[t % NL][1]])
            k.dma(bs[t % NL][0][:], d["BONd"][tk, :], bs[t % NL][1], writes=[bs[t % NL][1]])

        def tile(t):
            tk = slice(t * 128, (t + 1) * 128)
            y, yb = ys[t % NL]
            v, vb = vs[t % NL]
            gt, gb = gs[t % NL]
            bn, bnb = bs[t % NL]
            sq, sqb = sqs[t % NB]
            t2, t2b = t2s[t % NB]
            stt_, sttb = sts[t % NB]
            oT, oTb = oTs[t % NB]
            if t + 1 < NT:
                floads(t + 1)
            yield
            k.S.op("dve", lambda e: e.tensor_reduce(out=stt_[:, 0, :], in_=v3(y[:]), axis=AX.X, op=ALU.add), [yb], [sttb])
            k.act(sq[:], y[:], AF.Square, [yb], [sqb])
            yield
            k.S.op("dve", lambda e: e.tensor_reduce(out=stt_[:, 1, :], in_=v3(sq[:]), axis=AX.X, op=ALU.add), [sqb], [sttb], partial=True)
            yield
            k.ts("dve", stt_[:, 0, :], stt_[:, 0, :], 1.0 / 64.0, None, ALU.mult, None, [sttb], [sttb])
            yield
            k.tt("dve", stt_[:, 2, :], stt_[:, 0, :], stt_[:, 0, :], ALU.mult, [sttb], [sttb])
            yield
            k.stt(stt_[:, 1, :], stt_[:, 1, :], 1.0 / 64.0, stt_[:, 2, :], ALU.mult, ALU.subtract, [sttb], [sttb])
            yield
            k.ts("dve", stt_[:, 1, :], stt_[:, 1, :], float(GN_EPS), None, ALU.add, None, [sttb], [sttb])
            yield
            k.act(stt_[:, 1, :], stt_[:, 1, :], AF.Sqrt, [sttb], [sttb])
            yield
            k.recip(stt_[:, 3, :], stt_[:, 1, :], [sttb], [sttb])
            k.tt("pool", v3(sq[:]), v3(v[:]), bc(bn[:]), ALU.mult, [vb, bnb], [sqb])
            yield
            k.stt(stt_[:, 2, :], stt_[:, 0, :], -1.0, stt_[:, 3, :], ALU.mult, ALU.mult, [sttb], [sttb])
            yield
            for h in range(12):
                k.act(t2[:, h * 64:(h + 1) * 64], y[:, h * 64:(h + 1) * 64], AF.Identity, [yb, sttb], [t2b],
                      bias=stt_[:, 2, h:h + 1], scale=stt_[:, 3, h:h + 1], partial=(h > 0))
            yield
            k.tt("dve", t2[:], t2[:], GG[0][:], ALU.mult, [t2b, GG[1]], [t2b])
            yield
            k.tt("pool", t2[:], t2[:], GB[0][:], ALU.add, [t2b, GB[1]], [t2b])
            yield
            k.tt("dve", t2[:], t2[:], sq[:], ALU.add, [t2b, sqb], [t2b])
            yield
            k.tt("pool", t2[:], t2[:], gt[:], ALU.mult, [t2b, gb], [t2b])
            yield
            for half in range(2):
                pt, ptb = k.nps()
                for j in range(3):
                    c = half * 3 + j
                    k.tp(pt[:, j * 128:(j + 1) * 128], t2[:, c * 128:(c + 1) * 128], identF[0][:], [t2b, identF[1]], [ptb], first=(j == 0))
                k.cp("act", oT[:, half * 3:half * 3 + 3, :], pt[:, 0:384].rearrange("p (c t) -> p c t", c=3), [ptb], [oTb],
                     partial=(half == 1))
                yield
            k.dma(OTv[:, 0:6, tk], oT[:], oTb, reads=[oTb])

        bufs = {
            "pt": [k.sb(es, "mpt%d" % i, [128, 512], BF16) for i in range(6)],
            "rec": [k.sb(es, "mrec%d" % i, [64, 512], F32) for i in range(2)],
            "osb": [k.sb(es, "mosb%d" % i, [64, 512], BF16) for i in range(2)],
        }
        qms = [k.sb(es, "mqm%d" % i, [128, T], BF16) for i in range(2)]
        mk, mkb = g["mem_kT"]
        mv, mvb = g["mem_va"]

        def memheads():
            st = {"s": 0, "p": 0, "o": 0, "r": 0, "bg": None}
            for pair in range(2):
                qm, qmb = qms[pair]
                k.dma(qm[:], d["QM"][pair * 128:(pair + 1) * 128, :], qmb, writes=[qmb])
            yield
            for pair in range(2):
                qm, qmb = qms[pair]
                for j2 in range(2):
                    j = pair * 2 + j2
                    b0 = j2 * 64
                    attn_head(k, g, bufs,
                              lambda kt, b0=b0, pair=pair: mk[b0:b0 + 64, pair, kt * 128:(kt + 1) * 128], mkb,
                              lambda c0, c1, b0=b0, qm=qm: qm[b0:b0 + 64, c0:c1], qmb,
                              lambda kt, j=j: mv[:, kt, j, :], mvb,
                              False, 64.0 ** -0.5, d["OT"][768 + j * 64:768 + (j + 1) * 64, :], st)
                    for _ in range(6):
                        yield

        floads(0)
        interleave([memheads(), interleave_gen((tile(t) for t in range(NT)), NB)], 2)
    S.flush()


def rwkv_layer(k, d, g, stage):
    phaseD(k, d, g)
    if stage == "pD":
        return
    phaseE(k, d, g)
    if stage == "pE":
        return
    phaseF(k, d, g)
    if stage == "pF":
        return
    phaseC(k, d, g, 1, d["X1"], d["out"], False)
```

```python
import contextlib
import numpy as np
import concourse.bass as bass
import concourse.mybir as mybir
from concourse.bass_utils import run_bass_kernel_spmd

F32 = mybir.dt.float32
BF16 = mybir.dt.bfloat16
I32 = mybir.dt.int32
AF = mybir.ActivationFunctionType
ALU = mybir.AluOpType
AX = mybir.AxisListType

T = 4096
D = 1024
NT = T // 128
ALPHA = 4.0 ** 0.25
LN_EPS = 1e-5
RMS_EPS = 1e-6
GN_EPS = 64e-5
COMPUTE = ("pe", "act", "dve", "pool")
SKIP = set()


class Buf:
    __slots__ = ("name", "writers", "readers", "dsem", "excl", "base")

    def __init__(self, name, excl=False):
        self.name = name
        self.excl = excl
        self.base = []
        self.writers = []
        self.readers = []
        self.dsem = None


class Op:
    __slots__ = ("eng", "fn", "deps", "ddeps", "signal", "val", "is_dma", "dtok", "epoch")

    def __init__(self, eng, fn, epoch):
        self.eng = eng
        self.fn = fn
        self.deps = []
        self.ddeps = {}
        self.signal = False
        self.val = None
        self.is_dma = False
        self.dtok = None
        self.epoch = epoch


class Sched:
    def __init__(self, nc, es, n_dma_sems=40):
        self.nc = nc
        self.sem = {e: es.enter_context(nc.semaphore("s_" + e)) for e in COMPUTE}
        self.count = {e: 0 for e in COMPUTE}
        self.dsems = [es.enter_context(nc.semaphore("d%d" % i)) for i in range(n_dma_sems)]
        self.dtotal = [0] * n_dma_sems
        self.dnext = 0
        self.epoch = 0
        self.ops = {e: [] for e in ("pe", "act", "dve", "pool", "sp")}
        self.nops = 0
        self.phase_log = []

    def _dep(self, op, tok):
        if isinstance(tok, Op):
            if tok.epoch != self.epoch:
                return
            if tok.eng == "pe" and op.eng == "pe" and not op.is_dma:
                return
            tok.signal = True
            op.deps.append(tok)
        else:
            s, _, ep = tok
            if ep != self.epoch:
                return
            op.ddeps[s] = self.dtotal[s]

    @staticmethod
    def _push(lst, tok):
        if lst:
            last = lst[-1]
            if isinstance(tok, Op) and isinstance(last, Op) and last.eng == tok.eng:
                lst[-1] = tok
                return
            if (not isinstance(tok, Op)) and (not isinstance(last, Op)) and last[0] == tok[0]:
                lst[-1] = tok
                return
        lst.append(tok)

    def _record(self, op, reads, writes, partial):
        for b in reads:
            for t in b.writers:
                self._dep(op, t)
            if b.excl:
                for t in b.readers:
                    self._dep(op, t)
        newgen = {}
        for b in writes:
            if b.readers or not partial:
                base = list(b.readers) + list(b.writers)
                newgen[id(b)] = base
                for t in base:
                    self._dep(op, t)
            else:
                for t in b.base:
                    self._dep(op, t)
        if op.is_dma:
            s_ = op.dtok
            self.dtotal[s_] += 16
            op.dtok = (s_, self.dtotal[s_], self.epoch)
        tok = op.dtok if op.is_dma else op
        for b in writes:
            if id(b) in newgen:
                b.base = newgen[id(b)]
                b.readers = []
                b.writers = [tok]
            else:
                self._push(b.writers, tok)
        for b in reads:
            if b not in writes:
                if b.excl:
                    b.readers = [tok]
                else:
                    self._push(b.readers, tok)
        self.ops[op.eng].append(op)
        self.nops += 1

    def op(self, eng, fn, reads=(), writes=(), partial=False):
        o = Op(eng, fn, self.epoch)
        self._record(o, list(reads), list(writes), partial)
        return o

    def dma(self, fn, sb, reads=(), writes=(), queue="sp", partial=False):
        if sb.dsem is None:
            sb.dsem = self.dnext % len(self.dsems)
            self.dnext += 1
        o = Op(queue, fn, self.epoch)
        o.is_dma = True
        o.dtok = sb.dsem
        self._record(o, list(reads), list(writes), partial)
        return o

    def _simulate(self, ops):
        cnt = dict(getattr(self, "_sim_cnt", {e: 0 for e in COMPUTE}))
        dval = list(getattr(self, "_sim_d", [0] * len(self.dsems)))
        pos = {e: 0 for e in ops}
        progress = True
        while progress:
            progress = False
            for e in ops:
                while pos[e] < len(ops[e]):
                    o = ops[e][pos[e]]
                    ok = all(cnt[dd.eng] >= dd.val for dd in o.deps) and all(dval[s_] >= v for s_, v in o.ddeps.items())
                    if not ok:
                        break
                    if o.is_dma:
                        dval[o.dtok[0]] += 16
                    elif o.signal:
                        cnt[o.eng] += 1
                        assert cnt[o.eng] == o.val, (cnt[o.eng], o.val)
                    pos[e] += 1
                    progress = True
        stuck = {e: (pos[e], len(ops[e])) for e in ops if pos[e] < len(ops[e])}
        if stuck:
            for e in stuck:
                o = ops[e][pos[e]]
                print("STUCK", e, pos[e], [(dd.eng, dd.val, cnt[dd.eng]) for dd in o.deps], [(s_, v, dval[s_]) for s_, v in o.ddeps.items()])
            raise RuntimeError("schedule deadlock: %s" % stuck)
        assert dval == self.dtotal, "dma totals mismatch"
        self._sim_cnt, self._sim_d = cnt, dval

    def flush(self):
        nc = self.nc
        for e in COMPUTE:
            for o in self.ops[e]:
                if o.is_dma:
                    continue
                if o.signal:
                    self.count[e] += 1
                    o.val = self.count[e]
        final_d = {i: v for i, v in enumerate(self.dtotal) if v > 0}
        ops, sem, dsems = self.ops, self.sem, self.dsems

        def emit(engname, eng, final=False):
            waited = {}
            for o in ops[engname]:
                need = {}
                for d in o.deps:
                    k = ("c", d.eng)
                    if need.get(k, 0) < d.val:
                        need[k] = d.val
                for s, v in o.ddeps.items():
                    k = ("d", s)
                    if need.get(k, 0) < v:
                        need[k] = v
                for k, v in need.items():
                    if waited.get(k, 0) >= v:
                        continue
                    waited[k] = v
                    eng.wait_ge(sem[k[1]] if k[0] == "c" else dsems[k[1]], v)
                ins = o.fn(eng)
                if o.is_dma:
                    ins.then_inc(dsems[o.dtok[0]], 16)
                elif o.signal:
                    ins.then_inc(sem[o.eng], 1)
            if final:
                for s, v in final_d.items():
                    if waited.get(("d", s), 0) < v:
                        eng.wait_ge(dsems[s], v)

        if getattr(self, "check", True):
            self._simulate(ops)
        with nc.Block() as block:
            if ops["pe"]:
                @block.tensor
                def _(e):
                    emit("pe", e)
            if ops["act"]:
                @block.scalar
                def _(e):
                    emit("act", e)
            if ops["dve"]:
                @block.vector
                def _(e):
                    emit("dve", e)
            if ops["pool"]:
                @block.gpsimd
                def _(e):
                    emit("pool", e)

            @block.sync
            def _(e):
                emit("sp", e, final=True)
        self.phase_log.append((dict(self.count), {e: len(ops[e]) for e in ops}))
        self.ops = {e: [] for e in ("pe", "act", "dve", "pool", "sp")}
        self.epoch += 1


class K:
    def __init__(self, nc, S, es):
        self.nc, self.S, self.es = nc, S, es
        self.ps = []
        for i in range(8):
            t = es.enter_context(nc.psum_tensor("ps%d" % i, [128, 512], F32))
            self.ps.append((t, Buf("ps%d" % i, excl=True)))
        self.psi = 0
        self.rr = 0
        self.reserved = set()

    def nps(self):
        while True:
            i = self.psi % 8
            self.psi += 1
            if i not in self.reserved:
                return self.ps[i]

    def reserve(self):
        while True:
            i = self.psi % 8
            self.psi += 1
            if i not in self.reserved:
                self.reserved.add(i)
                return i, self.ps[i]

    def release(self, i):
        self.reserved.discard(i)

    def sb(self, es, name, shape, dt):
        self.nsb = getattr(self, "nsb", 0) + 1
        nm = "sb%d_%s" % (self.nsb, name)
        return es.enter_context(self.nc.sbuf_tensor(nm, list(shape), dt)), Buf(nm)

    def mm(self, out, lhsT, rhs, start, stop, reads, writes):
        self.S.op("pe", lambda e: e.matmul(out, lhsT=lhsT, rhs=rhs, start=start, stop=stop),
                  reads, writes, partial=not start)

    def tp(self, out, in_, ident, reads, writes, first=False):
        self.S.op("pe", lambda e: e.transpose(out=out, in_=in_, identity=ident), reads, writes,
                  partial=not first)

    def act(self, out, in_, func, reads, writes, bias=None, scale=None, accum=None, partial=False):
        kw = {}
        if bias is not None:
            kw["bias"] = bias
        if scale is not None:
            kw["scale"] = scale
        if accum is not None:
            kw["accum_out"] = accum
        self.S.op("act", lambda e: e.activation(out=out, in_=in_, func=func, **kw), reads, writes,
                  partial=partial)

    def tt(self, eng, out, in0, in1, op, reads, writes, partial=False):
        self.S.op(eng, lambda e: e.tensor_tensor(out=out, in0=in0, in1=in1, op=op), reads, writes,
                  partial=partial)

    def ts(self, eng, out, in0, s1, s2, op0, op1, reads, writes, partial=False):
        if op1 is None:
            self.S.op(eng, lambda e: e.tensor_scalar(out=out, in0=in0, scalar1=s1, scalar2=None, op0=op0),
                      reads, writes, partial=partial)
        else:
            self.S.op(eng, lambda e: e.tensor_scalar(out=out, in0=in0, scalar1=s1, scalar2=s2, op0=op0, op1=op1),
                      reads, writes, partial=partial)

    def stt(self, out, in0, scalar, in1, op0, op1, reads, writes, partial=False):
        self.S.op("dve", lambda e: e.scalar_tensor_tensor(out=out, in0=in0, scalar=scalar, in1=in1, op0=op0, op1=op1),
                  reads, writes, partial=partial)

    def cp(self, eng, out, in_, reads, writes, partial=False):
        if eng == "act":
            self.S.op("act", lambda e: e.activation(out=out, in_=in_, func=AF.Copy), reads, writes, partial=partial)
        else:
            self.S.op(eng, lambda e: e.tensor_copy(out=out, in_=in_), reads, writes, partial=partial)

    def recip(self, out, in_, reads, writes, partial=False):
        self.S.op("dve", lambda e: e.reciprocal(out=out, in_=in_), reads, writes, partial=partial)

    def recip_fast(self, out, in_, reads, writes, partial=False):
        self.S.op("dve", lambda e: e.reciprocal_approx_fast(out=out, in_=in_), reads, writes, partial=partial)

    def memset(self, eng, ap, val, writes, partial=False):
        self.S.op(eng, lambda e: e.memset(ap, val), (), writes, partial=partial)

    def dma(self, out, in_, sb, reads=(), writes=(), partial=False):
        self.S.dma(lambda e: e.dma_start(out=out, in_=in_), sb, reads, writes, partial=partial)

    def alt(self):
        self.rr += 1
        return "act" if self.rr % 2 else "dve"


def interleave_gen(gens, width):
    gens = iter(gens)
    active = []
    for _ in range(width):
        g_ = next(gens, None)
        if g_ is not None:
            active.append(g_)
    while active:
        for g_ in list(active):
            try:
                next(g_)
            except StopIteration:
                i = active.index(g_)
                n_ = next(gens, None)
                if n_ is not None:
                    active[i] = n_
                else:
                    active.pop(i)
        yield


def interleave(gens, width):
    gens = iter(gens)
    active = []
    for _ in range(width):
        g_ = next(gens, None)
        if g_ is not None:
            active.append(g_)
    while active:
        for g_ in list(active):
            try:
                next(g_)
            except StopIteration:
                i = active.index(g_)
                n_ = next(gens, None)
                if n_ is not None:
                    active[i] = n_
                else:
                    active.pop(i)


SC = 2048


class Stage:
    def __init__(self, k, es, n=2, cols=SC):
        self.k = k
        self.cols = cols
        self.t = [k.sb(es, "stg%d" % i, [128, cols], F32) for i in range(n)]
        self.i = 0

    def nxt(self):
        r = self.t[self.i % len(self.t)]
        self.i += 1
        return r


def load_w(k, es, stg, name, src, R, C, scale=None, dst=None):
    nch = (R + 127) // 128
    w, wb = dst if dst is not None else k.sb(es, name, [128, nch, C], BF16)
    for c in range(nch):
        rows = min(128, R - c * 128)
        for j0 in range(0, C, stg.cols):
            cols = min(stg.cols, C - j0)
            st, stb = stg.nxt()
            k.dma(st[0:rows, 0:cols], src[c * 128:c * 128 + rows, j0:j0 + cols], stb, writes=[stb])
            if scale is not None:
                k.act(w[0:rows, c, j0:j0 + cols], st[0:rows, 0:cols], AF.Identity, [stb, scale[1]], [wb],
                      scale=scale[0][0:rows, c:c + 1], partial=True)
            else:
                eng = ("act", "dve", "pool")[k.rr % 3]
                k.rr += 1
                k.cp(eng, w[0:rows, c, j0:j0 + cols], st[0:rows, 0:cols], [stb], [wb], partial=True)
    return w, wb


def bcast_load(k, es, name, src_row, n):
    t, b = k.sb(es, name, [128, n], F32)
    k.dma(t[:], src_row[0, :].partition_broadcast(128), b, writes=[b])
    return t, b


def layernorm(k, x, xb, out, outb, G, B, tmp, eps=LN_EPS):
    st, stb = tmp["stats"]
    mv, mvb = tmp["mv"]
    for h in range(2):
        k.S.op("dve", lambda e, h=h: e.bn_stats(out=st[:, h, :], in_=x[:, h * 512:(h + 1) * 512]), [xb], [stb],
               partial=(h > 0))
    k.S.op("dve", lambda e: e.bn_aggr(out=mv[:, 0:2], in_=st[:].rearrange("p a b -> p (a b)")), [stb], [mvb])
    k.ts("dve", mv[:, 2:3], mv[:, 1:2], float(eps), None, ALU.add, None, [mvb], [mvb])
    k.act(mv[:, 2:3], mv[:, 2:3], AF.Sqrt, [mvb], [mvb])
    k.recip(mv[:, 3:4], mv[:, 2:3], [mvb], [mvb])
    k.ts("dve", out, x, mv[:, 0:1], mv[:, 3:4], ALU.subtract, ALU.mult, [xb, mvb], [outb])
    k.tt("pool", out, out, G[0][:], ALU.mult, [outb, G[1]], [outb])
    k.tt("pool", out, out, B[0][:], ALU.add, [outb, B[1]], [outb])


def transpose_tile(k, x, xb, identF, dst, dstb, col0, ncols=128):
    for half in range(2):
        pt, ptb = k.nps()
        for j in range(4):
            c = half * 4 + j
            k.tp(pt[:, j * 128:(j + 1) * 128], x[:, c * 128:(c + 1) * 128], identF[0][:], [xb, identF[1]], [ptb],
                 first=(j == 0))
        k.cp(k.alt(), dst[:, half * 4:half * 4 + 4, col0:col0 + 128],
             pt[:].rearrange("p (c t) -> p c t", c=4), [ptb], [dstb], partial=True)


def phase0(k, d, g):
    S = k.S
    with contextlib.ExitStack() as es:
        identF = g["identF"]
        k.dma(identF[0][:], d["identF"], identF[1], writes=[identF[1]])
        trif, trifb = k.sb(es, "trif", [128, 128], F32)
        k.dma(trif[:], d["tri"], trifb, writes=[trifb])
        k.cp("dve", g["tri"][0][:], trif[:], [trifb], [g["tri"][1]])
        k.memset("pool", g["ones"][0][:], 1.0, [g["ones"][1]])
        k.memset("pool", g["mem_va"][0][:], 1.0, [g["mem_va"][1]])
        stg = Stage(k, es)
        tmp = {"stats": k.sb(es, "lnst", [128, 2, 6], F32), "mv": k.sb(es, "lnmv", [128, 4], F32)}
        G = bcast_load(k, es, "memG", d["mem_ln_g"], D)
        B = bcast_load(k, es, "memB", d["mem_ln_b"], D)
        wm, wmb = load_w(k, es, stg, "wm", d["w_mem_kv"], D, 512)
        mT, mTb = k.sb(es, "memT", [128, 8, 256], BF16)
        for mt in range(2):
            xm, xmb = k.sb(es, "xm%d" % mt, [128, D], F32)
            k.dma(xm[:], d["mem"][mt * 128:(mt + 1) * 128, :], xmb, writes=[xmb])
            layernorm(k, xm[:], xmb, xm[:], xmb, G, B, tmp)
            transpose_tile(k, xm, xmb, identF, mT, mTb, mt * 128)
        for p in range(2):
            pt, ptb = k.nps()
            for c in range(8):
                k.mm(pt[:, 0:256], wm[:, c, p * 128:(p + 1) * 128], mT[:, c, :], c == 0, c == 7, [wmb, mTb], [ptb])
            k.cp("act", g["mem_kT"][0][:, p, :], pt[:, 0:256], [ptb], [g["mem_kT"][1]], partial=True)
        for mt in range(2):
            pt, ptb = k.nps()
            for c in range(8):
                k.mm(pt[:, 0:256], mT[:, c, mt * 128:(mt + 1) * 128], wm[:, c, 256:512], c == 0, c == 7, [wmb, mTb], [ptb])
            k.cp("dve", g["mem_va"][0][:, mt, :, 0:64], pt[:, 0:256].rearrange("p (h v) -> p h v", h=4),
                 [ptb], [g["mem_va"][1]], partial=True)
        invf, invfb = k.sb(es, "invf", [128, 1], F32)
        sgn, sgnb = k.sb(es, "sgn", [128, 1], F32)
        k.dma(invf[:], d["invf"], invfb, writes=[invfb])
        k.dma(sgn[:], d["sgn"], sgnb, writes=[sgnb])
        W = 1024
        pi, pib = k.sb(es, "pos_i", [128, W], I32)
        r_, rb = k.sb(es, "rr", [128, W], F32)
        ki, kib = k.sb(es, "ki", [128, W], I32)
        kf, kfb = k.sb(es, "kf", [128, W], F32)
        f_, fb = k.sb(es, "ff", [128, W], F32)
        o_, ob = k.sb(es, "oo", [128, W], F32)
        def rot():
            for blk in range(T // W):
                k.dma(pi[:], d["pos"][0, blk * W:(blk + 1) * W].partition_broadcast(128), pib, writes=[pib])
                yield
                k.cp("dve", r_[:], pi[:], [pib], [rb])
                yield
                k.ts("dve", r_[:], r_[:], invf[:, 0:1], None, ALU.mult, None, [rb, invfb], [rb])
                yield
                for which in range(2):
                    if which == 1:
                        k.ts("dve", r_[:], r_[:], 0.25, None, ALU.add, None, [rb], [rb])
                        yield
                    k.cp("dve", ki[:], r_[:], [rb], [kib])
                    yield
                    k.cp("dve", kf[:], ki[:], [kib], [kfb])
                    yield
                    k.tt("dve", f_[:], r_[:], kf[:], ALU.subtract, [rb, kfb], [fb])
                    yield
                    k.ts("dve", kf[:], f_[:], 0.5, None, ALU.is_gt, None, [fb], [kfb])
                    yield
                    k.tt("dve", f_[:], f_[:], kf[:], ALU.subtract, [fb, kfb], [fb])
                    yield
                    k.ts("dve", kf[:], f_[:], -0.5, None, ALU.is_lt, None, [fb], [kfb])
                    yield
                    k.tt("dve", f_[:], f_[:], kf[:], ALU.add, [fb, kfb], [fb])
                    yield
                    k.act(o_[:], f_[:], AF.Sin, [fb], [ob], scale=float(2.0 * np.pi))
                    yield
                    if which == 0:
                        k.ts("dve", o_[:], o_[:], sgn[:, 0:1], None, ALU.mult, None, [ob, sgnb], [ob])
                        yield
                        k.dma(d["SSd"][:, blk * W:(blk + 1) * W], o_[:], ob, reads=[ob])
                        yield
                    else:
                        k.dma(d["CCd"][:, blk * W:(blk + 1) * W], o_[:], ob, reads=[ob])
                        yield
        NBX = 4
        xin = [k.sb(es, "xin%d" % i, [128, D], F32) for i in range(NBX)]
        xt = [k.sb(es, "xt%d" % i, [128, 8, 128], BF16) for i in range(NBX)]
        XTv = d["XT"].rearrange("(c p) t -> p c t", p=128)

        def xtile(t):
            xi, xib = xin[t % NBX]
            xo, xob = xt[t % NBX]
            k.dma(xi[:], d["x"][t * 128:(t + 1) * 128, :], xib, writes=[xib])
            yield
            for half in range(2):
                pt, ptb = k.nps()
                for j in range(4):
                    c = half * 4 + j
                    k.tp(pt[:, j * 128:(j + 1) * 128], xi[:, c * 128:(c + 1) * 128], identF[0][:], [xib, identF[1]], [ptb],
                         first=(j == 0))
                k.cp(k.alt(), xo[:, half * 4:half * 4 + 4, :], pt[:].rearrange("p (c t) -> p c t", c=4), [ptb], [xob],
                     partial=(half == 1))
                yield
            k.dma(XTv[:, :, t * 128:(t + 1) * 128], xo[:], xob, reads=[xob])
            yield

        interleave([rot(), interleave_gen((xtile(t) for t in range(NT)), NBX)], 2)
    S.flush()


def phaseA(k, d, g):
    S = k.S
    ones = g["ones"]
    with contextlib.ExitStack() as es:
        stg = Stage(k, es)
        qn, qnb = k.sb(es, "qn", [128, 3], F32)
        kvn, kvnb = k.sb(es, "kvn", [128, 2], F32)
        k.dma(qn[:], d["qn"], qnb, writes=[qnb])
        k.dma(kvn[:], d["kvn"], kvnb, writes=[kvnb])
        WA, WAb = load_w(k, es, stg, "WA", d["WA"], D, 960)
        WQ, WQb = load_w(k, es, stg, "WQ", d["WQ"], 384, 1536, scale=(None if 'scale' in SKIP else (qn, qnb)))
        WKV, WKVb = load_w(k, es, stg, "WKV", d["WKV"], 256, 1536, scale=(None if 'scale' in SKIP else (kvn, kvnb)))
        xTs = [k.sb(es, "xT%d" % i, [128, 8, 512], BF16) for i in range(2)]
        CCs = [k.sb(es, "CC%d" % i, [128, 512], F32) for i in range(2)]
        SSs = [k.sb(es, "SS%d" % i, [128, 512], F32) for i in range(2)]
        cTs = [k.sb(es, "cT%d" % i, [128, 5, 512], BF16) for i in range(2)]
        sqs_ = [k.sb(es, "sq%d" % i, [128, 5, 512], BF16) for i in range(2)]
        cns = [k.sb(es, "cn%d" % i, [128, 5, 512], BF16) for i in range(2)]
        rqs = [k.sb(es, "rq%d" % i, [128, 512], F32) for i in range(2)]
        rkvs = [k.sb(es, "rkv%d" % i, [128, 512], F32) for i in range(2)]
        t1s = [k.sb(es, "t1_%d" % i, [128, 512], F32) for i in range(2)]
        t2s = [k.sb(es, "t2_%d" % i, [128, 512], F32) for i in range(2)]
        obs = [k.sb(es, "ob%d" % i, [128, 512], BF16) for i in range(6)]
        vts = [k.sb(es, "vt%d" % i, [128, 12, 128], BF16) for i in range(2)]
        for vt, vtb in vts:
            k.memset("pool", vt[:], 1.0, [vtb])
        XTv = d["XT"].rearrange("(c p) t -> p c t", p=128)
        VAv = d["VA"].rearrange("h p kt c -> p h kt c")
        oi = [0]

        def nob():
            r = obs[oi[0] % len(obs)]
            oi[0] += 1
            return r

        def rope_out(psA, psAb, psBt, psBb, rows, CC, CCb, SS, SSb, idx):
            t1, t1b = t1s[idx % 2]
            t2, t2b = t2s[idx % 2]
            o, ob = nob()
            k.tt("dve", t1[0:rows, :], psA[0:rows, :], CC[0:rows, :], ALU.mult, [psAb, CCb], [t1b])
            k.tt("dve", t2[0:rows, :], psBt[0:rows, :], SS[0:rows, :], ALU.mult, [psBb, SSb], [t2b])
            k.tt("pool", o[0:rows, :], t1[0:rows, :], t2[0:rows, :], ALU.add, [t1b, t2b], [ob])
            return o, ob

        def part1(s):
                tk = slice(s * 512, (s + 1) * 512)
                xT, xTb = xTs[s % 2]
                CC, CCb = CCs[s % 2]
                SS, SSb = SSs[s % 2]
                cT, cTb = cTs[s % 2]
                sq, sqb = sqs_[s % 2]
                cn, cnb = cns[s % 2]
                rq, rqb = rqs[s % 2]
                rkv, rkvb = rkvs[s % 2]
                k.dma(xT[:], XTv[:, :, tk], xTb, writes=[xTb])
                yield
                k.dma(CC[:], d["CCd"][:, tk], CCb, writes=[CCb])
                yield
                k.dma(SS[:], d["SSd"][:, tk], SSb, writes=[SSb])
                yield
                for ci in range(5):
                    pt, ptb = k.nps()
                    for c in range(8):
                        k.mm(pt[:], WA[:, c, ci * 128:(ci + 1) * 128], xT[:, c, :], c == 0, c == 7, [WAb, xTb], [ptb])
                    k.cp("dve", cT[:, ci, :], pt[:], [ptb], [cTb], partial=True)
                    yield
                    if 'sq' not in SKIP:
                        k.act(sq[:, ci, :], pt[:], AF.Square, [ptb], [sqb], partial=True)
                        yield
                for (lo, hi, n, r_, rb_) in (() if 'rms' in SKIP else ((0, 3, 384.0, rq, rqb), (3, 5, 256.0, rkv, rkvb))):
                    pt, ptb = k.nps()
                    for ci in range(lo, hi):
                        k.mm(pt[:], ones[0][:], sq[:, ci, :], ci == lo, ci == hi - 1, [ones[1], sqb], [ptb])
                    k.ts("dve", r_[:], pt[:], 1.0 / n, RMS_EPS, ALU.mult, ALU.add, [ptb], [rb_])
                    yield
                    k.act(r_[:], r_[:], AF.Ln, [rb_], [rb_])
                    yield
                    k.act(r_[:], r_[:], AF.Exp, [rb_], [rb_], scale=-0.5)
                    yield
                    for ci in range(lo, hi):
                        k.tt("dve", cn[:, ci, :], cT[:, ci, :], r_[:], ALU.mult, [cTb, rb_], [cnb], partial=True)
                        yield

        def part2(s):
                tk = slice(s * 512, (s + 1) * 512)
                xT, xTb = xTs[s % 2]
                CC, CCb = CCs[s % 2]
                SS, SSb = SSs[s % 2]
                cT, cTb = cTs[s % 2]
                sq, sqb = sqs_[s % 2]
                cn, cnb = cns[s % 2]
                rq, rqb = rqs[s % 2]
                rkv, rkvb = rkvs[s % 2]
                for j in range(0 if 'qm' in SKIP else 2):
                    pt, ptb = k.nps()
                    for c in range(8):
                        k.mm(pt[:], WA[:, c, 640 + j * 128:640 + (j + 1) * 128], xT[:, c, :], c == 0, c == 7, [WAb, xTb], [ptb])
                    o, ob = nob()
                    k.cp("act", o[:], pt[:], [ptb], [ob])
                    yield
                    k.dma(d["QM"][j * 128:(j + 1) * 128, tk], o[:], ob, reads=[ob])
                    yield
                if 'kr' not in SKIP:
                  pA, pAb = k.nps()
                  pB, pBb = k.nps()
                  for c in range(8):
                    k.mm(pA[0:32, :], WA[:, c, 896:928], xT[:, c, :], c == 0, c == 7, [WAb, xTb], [pAb])
                  for c in range(8):
                    k.mm(pB[0:32, :], WA[:, c, 928:960], xT[:, c, :], c == 0, c == 7, [WAb, xTb], [pBb])
                  o, ob = rope_out(pA, pAb, pB, pBb, 32, CC, CCb, SS, SSb, 0)
                  k.dma(d["KR"][0:32, tk], o[0:32, :], ob, reads=[ob])
                  yield
                for p in range(0 if 'qn' in SKIP else 6):
                    pt, ptb = k.nps()
                    for r in range(3):
                        k.mm(pt[:], WQ[:, r, p * 128:(p + 1) * 128], cn[:, r, :], r == 0, r == 2, [WQb, cnb], [ptb])
                    o, ob = nob()
                    k.cp(k.alt(), o[:], pt[:], [ptb], [ob])
                    yield
                    for j in range(2):
                        k.dma(d["QT"][2 * p + j, 0:64, tk], o[j * 64:(j + 1) * 64, :], ob, reads=[ob])
                        yield
                for c4 in range(0 if 'qr' in SKIP else 3):
                    pA, pAb = k.nps()
                    pB, pBb = k.nps()
                    for r in range(3):
                        k.mm(pA[:], WQ[:, r, 768 + c4 * 128:768 + (c4 + 1) * 128], cn[:, r, :], r == 0, r == 2, [WQb, cnb], [pAb])
                    for r in range(3):
                        k.mm(pB[:], WQ[:, r, 1152 + c4 * 128:1152 + (c4 + 1) * 128], cn[:, r, :], r == 0, r == 2, [WQb, cnb], [pBb])
                    o, ob = rope_out(pA, pAb, pB, pBb, 128, CC, CCb, SS, SSb, c4 + 1)
                    for j in range(4):
                        k.dma(d["QT"][4 * c4 + j, 64:96, tk], o[j * 32:(j + 1) * 32, :], ob, reads=[ob])
                        yield
                for p in range(0 if 'kn' in SKIP else 6):
                    pt, ptb = k.nps()
                    for r in range(2):
                        k.mm(pt[:], WKV[:, r, p * 128:(p + 1) * 128], cn[:, 3 + r, :], r == 0, r == 1, [WKVb, cnb], [ptb])
                    o, ob = nob()
                    k.cp(k.alt(), o[:], pt[:], [ptb], [ob])
                    yield
                    for j in range(2):
                        k.dma(d["KT"][2 * p + j, 0:64, tk], o[j * 64:(j + 1) * 64, :], ob, reads=[ob])
                        yield
                for t4 in range(0 if 'v' in SKIP else 4):
                    p1, p1b = k.nps()
                    p2, p2b = k.nps()
                    for r in range(2):
                        k.mm(p1[:], cn[:, 3 + r, t4 * 128:(t4 + 1) * 128], WKV[:, r, 768:1280], r == 0, r == 1, [WKVb, cnb], [p1b])
                    for r in range(2):
                        k.mm(p2[:, 0:256], cn[:, 3 + r, t4 * 128:(t4 + 1) * 128], WKV[:, r, 1280:1536], r == 0, r == 1, [WKVb, cnb], [p2b])
                    vt, vtb = vts[t4 % 2]
                    k.cp("act", vt[:, 0:8, 0:64], p1[:].rearrange("p (h v) -> p h v", h=8), [p1b], [vtb])
                    yield
                    k.cp("dve", vt[:, 8:12, 0:64], p2[:, 0:256].rearrange("p (h v) -> p h v", h=4), [p2b], [vtb], partial=True)
                    yield
                    k.dma(VAv[:, :, s * 4 + t4, :], vt[:], vtb, reads=[vtb])
                    yield

        for _ in part1(0):
            pass
        for s in range(T // 512):
            interleave([part2(s)] + ([part1(s + 1)] if s + 1 < T // 512 else []), 2)
    S.flush()


def attn_head(k, g, bufs, Kap, Kb, Qap, Qb, Vap, Vb, causal, scale, out_dram, st):
    tri = g["tri"]
    tiles = []
    for qb in range(8):
        nkt = 4 * qb + 4 if causal else 2
        for kt in range(nkt):
            n0 = (kt - 4 * qb) * 128 if (causal and kt >= 4 * qb) else 0
            tiles.append((qb, kt, nkt, n0))
    pend = []

    def qk(i):
        qb, kt, nkt, n0 = tiles[i]
        cols = 512 - n0
        ps, psb = k.ps[st["s"] % 4]
        st["s"] += 1
        pt, ptb = bufs["pt"][st["p"] % len(bufs["pt"])]
        st["p"] += 1
        k.mm(ps[:, 0:cols], Kap(kt), Qap(qb * 512 + n0, (qb + 1) * 512), True, True, [Kb, Qb], [psb])
        if "sc" in bufs and i % 10 in (2, 5, 8):
            sc, scb = bufs["sc"][st["c"] % len(bufs["sc"])]
            st["c"] += 1
            k.cp("dve", sc[:, 0:cols], ps[:, 0:cols], [psb], [scb])
            k.act(pt[:, 0:cols], sc[:, 0:cols], AF.Exp, [scb], [ptb], scale=scale)
        else:
            k.act(pt[:, 0:cols], ps[:, 0:cols], AF.Exp, [psb], [ptb], scale=scale)
        if causal and kt >= 4 * qb:
            k.tt("pool", pt[:, 0:128], pt[:, 0:128], tri[0][:], ALU.mult, [ptb, tri[1]], [ptb])
        return pt, ptb

    def pv(i, pt, ptb):
        qb, kt, nkt, n0 = tiles[i]
        cols = 512 - n0
        if kt == 0:
            st["po"] = k.ps[4 + st["o"] % 2]
            st["o"] += 1
        po, pob = st["po"]
        k.mm(po[:, n0:512], Vap(kt), pt[:, 0:cols], kt == 0, kt == nkt - 1, [Vb, ptb], [pob])
        if kt == nkt - 1:
            rec, recb = bufs["rec"][st["r"] % 2]
            osb, osbb = bufs["osb"][st["r"] % 2]
            st["r"] += 1
            k.recip(rec[0:64, :], po[64:128, :], [pob], [recb])
            k.tt("dve", osb[0:64, :], po[0:64, :], rec[0:64, :], ALU.mult, [pob, recb], [osbb])
            k.dma(out_dram[:, qb * 512:(qb + 1) * 512], osb[0:64, :], osbb, reads=[osbb])

    LAG = 3
    q = []
    for i in range(len(tiles)):
        q.append((i,) + qk(i))
        if len(q) > LAG:
            pv(*q.pop(0))
        if i == 8 and st.get("prefetch") is not None:
            st["prefetch"]()
            st["prefetch"] = None
        if st.get("bg") is not None and i % 5 == 4:
            try:
                next(st["bg"])
            except StopIteration:
                st["bg"] = None
    while q:
        pv(*q.pop(0))


def precast_gen(k, es, d):
    stg = [k.sb(es, "pcs%d" % i, [128, 2048], F32) for i in range(2)]
    outb = [k.sb(es, "pcb%d" % i, [128, 2048], BF16) for i in range(2)]
    i = 0
    for l in range(2):
        for (src, dst, R, C) in ((d["w_out"][l], d["WOb"][l], D, D), (d["w_ff1"][l], d["W1b"][l], D, 4096),
                                 (d["w_ff2"][l], d["W2b"][l], 4096, D)):
            for c in range(R // 128):
                for j0 in range(0, C, 2048):
                    cols = min(2048, C - j0)
                    st, stb = stg[i % 2]
                    ob, obb = outb[i % 2]
                    i += 1
                    k.dma(st[:, 0:cols], src[c * 128:(c + 1) * 128, j0:j0 + cols], stb, writes=[stb])
                    yield
                    k.cp("dve", ob[:, 0:cols], st[:, 0:cols], [stb], [obb])
                    yield
                    k.dma(dst[c * 128:(c + 1) * 128, j0:j0 + cols], ob[:, 0:cols], obb, reads=[obb])
                    yield


def phaseB(k, d, g, with_mla):
    S = k.S
    with contextlib.ExitStack() as es:
        bufs = {
            "pt": [k.sb(es, "pt%d" % i, [128, 512], BF16) for i in range(6)],
            "rec": [k.sb(es, "rec%d" % i, [64, 512], F32) for i in range(2)],
            "osb": [k.sb(es, "osb%d" % i, [64, 512], BF16) for i in range(2)],
        }
        st = {"s": 0, "p": 0, "o": 0, "r": 0, "c": 0, "bg": (precast_gen(k, es, d) if with_mla else None)}
        if with_mla:
            KTs = [k.sb(es, "KTh%d" % i, [96, T], BF16) for i in range(2)]
            QTs = [k.sb(es, "QTh%d" % i, [96, T], BF16) for i in range(2)]
            VAs = [k.sb(es, "VAh%d" % i, [128, NT, 128], BF16) for i in range(2)]
        if with_mla:
            k.dma(KTs[0][0][0:64, :], d["KT"][0], KTs[0][1], writes=[KTs[0][1]])
            k.dma(KTs[0][0][64:96, :], d["KR"], KTs[0][1], writes=[KTs[0][1]], partial=True)
            k.dma(QTs[0][0][:], d["QT"][0], QTs[0][1], writes=[QTs[0][1]])
            k.dma(VAs[0][0][:], d["VA"][0], VAs[0][1], writes=[VAs[0][1]])
        mk, mkb = g["mem_kT"]
        mv, mvb = g["mem_va"]
        for pair in range(2):
            qm, qmb = k.sb(es, "qm%d" % pair, [128, T], BF16)
            k.dma(qm[:], d["QM"][pair * 128:(pair + 1) * 128, :], qmb, writes=[qmb])
            for j2 in range(2):
                j = pair * 2 + j2
                b0 = j2 * 64
                attn_head(k, g, bufs,
                          lambda kt, b0=b0, pair=pair: mk[b0:b0 + 64, pair, kt * 128:(kt + 1) * 128], mkb,
                          lambda c0, c1, b0=b0, qm=qm: qm[b0:b0 + 64, c0:c1], qmb,
                          lambda kt, j=j: mv[:, kt, j, :], mvb,
                          False, 64.0 ** -0.5, d["OT"][768 + j * 64:768 + (j + 1) * 64, :], st)
        if with_mla:
            def head_loads(h):
                kt_, ktb = KTs[h % 2]
                qt_, qtb = QTs[h % 2]
                va_, vab = VAs[h % 2]
                k.dma(kt_[0:64, :], d["KT"][h], ktb, writes=[ktb])
                k.dma(kt_[64:96, :], d["KR"], ktb, writes=[ktb], partial=True)
                k.dma(qt_[:], d["QT"][h], qtb, writes=[qtb])
                k.dma(va_[:], d["VA"][h], vab, writes=[vab])

            for h in range(12):
                kt_, ktb = KTs[h % 2]
                qt_, qtb = QTs[h % 2]
                va_, vab = VAs[h % 2]
                st["prefetch"] = (lambda h=h: head_loads(h + 1)) if h + 1 < 12 else None
                attn_head(k, g, bufs,
                          lambda kt, kt_=kt_: kt_[:, kt * 128:(kt + 1) * 128], ktb,
                          lambda c0, c1, qt_=qt_: qt_[:, c0:c1], qtb,
                          lambda kt, va_=va_: va_[:, kt, :], vab,
                          True, 96.0 ** -0.5, d["OT"][h * 64:(h + 1) * 64, :], st)
        if st.get("bg") is not None:
            for _ in st["bg"]:
                pass
    S.flush()


def phaseC(k, d, g, l, Xin, Xout, write_xt):
    S = k.S
    identF = g["identF"]
    with contextlib.ExitStack() as es:
        wo, wob = k.sb(es, "wo", [128, 8, D], BF16)
        w1, w1b = k.sb(es, "w1", [128, 8, 4096], BF16)
        w2, w2b = k.sb(es, "w2", [128, 32, D], BF16)
        G1 = bcast_load(k, es, "G1", d["ln1_g"][l:l + 1, :], D)
        B1 = bcast_load(k, es, "B1", d["ln1_b"][l:l + 1, :], D)
        G2 = bcast_load(k, es, "G2", d["ln2_g"][l:l + 1, :], D)
        B2 = bcast_load(k, es, "B2", d["ln2_b"][l:l + 1, :], D)
        k.dma(wo[:], d["WOb"][l].rearrange("(c p) n -> p c n", p=128), wob, writes=[wob])
        w1bs = [Buf("w1blk%d" % i) for i in range(4)]
        for c4 in range(4):
            k.dma(w1[:, :, 1024 * c4:1024 * (c4 + 1)], d["W1b"][l].rearrange("(c p) n -> p c n", p=128)[:, :, 1024 * c4:1024 * (c4 + 1)],
                  w1bs[c4], writes=[w1bs[c4]])
        for c4 in range(4):
            k.dma(w2[:, 8 * c4:8 * c4 + 8, :], d["W2b"][l].rearrange("(c p) n -> p c n", p=128)[:, 8 * c4:8 * c4 + 8, :], w2b,
                  writes=[w2b], partial=(c4 > 0))
        tmp1 = {"stats": k.sb(es, "lnst1", [128, 2, 6], F32), "mv": k.sb(es, "lnmv1", [128, 4], F32)}
        tmp2 = {"stats": k.sb(es, "lnst2", [128, 2, 6], F32), "mv": k.sb(es, "lnmv2", [128, 4], F32)}
        ots = [k.sb(es, "ot%d" % i, [128, 8, 128], BF16) for i in range(2)]
        xins = [k.sb(es, "xi%d" % i, [128, D], F32) for i in range(2)]
        r1, r1b = k.sb(es, "r1", [128, D], F32)
        r2, r2b = r1, r1b
        x1s = [k.sb(es, "x1_%d" % i, [128, D], F32) for i in range(2)]
        x1T, x1Tb = k.sb(es, "x1T", [128, 8, 128], BF16)
        hT, hTb = k.sb(es, "hT", [128, 32, 128], BF16)
        rl = [k.sb(es, "rl%d" % i, [128, 512], F32) for i in range(2)]
        x2T, x2Tb = k.sb(es, "x2T", [128, 8, 128], BF16)
        OTv = d["OT"].rearrange("(c p) t -> p c t", p=128)
        XTv = d["XT"].rearrange("(c p) t -> p c t", p=128)

        def loads(t):
            tk = slice(t * 128, (t + 1) * 128)
            ot, otb = ots[t % 2]
            xi, xib = xins[t % 2]
            k.dma(ot[:], OTv[:, :, tk], otb, writes=[otb])
            k.dma(xi[:], Xin[tk, :], xib, writes=[xib])

        def s1a(t):
            ot, otb = ots[t % 2]
            xi, xib = xins[t % 2]
            x1, x1b = x1s[t % 2]
            for half in range(2):
                pt, ptb = k.nps()
                for c in range(8):
                    k.mm(pt[:], ot[:, c, :], wo[:, c, half * 512:(half + 1) * 512], c == 0, c == 7, [otb, wob], [ptb])
                k.stt(r1[:, half * 512:(half + 1) * 512], xi[:, half * 512:(half + 1) * 512], float(ALPHA), pt[:],
                      ALU.mult, ALU.add, [xib, ptb], [r1b], partial=(half == 1))
            layernorm(k, r1[:], r1b, x1[:], x1b, G1, B1, tmp1)

        def s1bT(t):
            x1, x1b = x1s[t % 2]
            transpose_tile(k, x1, x1b, identF, x1T, x1Tb, 0)

        def s1bF(t):
            for f4 in range(8):
                pt, ptb = k.nps()
                for j in range(4):
                    f = f4 * 4 + j
                    for c in range(8):
                        k.mm(pt[:, j * 128:(j + 1) * 128], w1[:, c, f * 128:(f + 1) * 128], x1T[:, c, :], c == 0, c == 7,
                             [w1bs[f // 8], x1Tb], [ptb])
                rt, rtb = rl[f4 % 2]
                k.act(rt[:], pt[:], AF.Relu, [ptb], [rtb])
                k.tt("pool", hT[:, f4 * 4:(f4 + 1) * 4, :], rt[:].rearrange("p (j t) -> p j t", j=4),
                     rt[:].rearrange("p (j t) -> p j t", j=4), ALU.mult, [rtb], [hTb], partial=(f4 > 0))

        s2st = {}

        def s2(t, part):
            tk = slice(t * 128, (t + 1) * 128)
            x1, x1b = x1s[t % 2]
            if part == 0:
                pt, ptb = k.nps()
                for f in range(32):
                    k.mm(pt[:], hT[:, f, :], w2[:, f, 0:512], f == 0, f == 31, [hTb, w2b], [ptb])
                k.stt(r2[:, 0:512], x1[:, 0:512], float(ALPHA), pt[:], ALU.mult, ALU.add, [x1b, ptb], [r2b])
                return
            if part == 1:
                s2st["i"], s2st["pt"] = k.reserve()
                pt, ptb = s2st["pt"]
                for f in range(16):
                    k.mm(pt[:], hT[:, f, :], w2[:, f, 512:1024], f == 0, False, [hTb, w2b], [ptb])
                return
            pt, ptb = s2st["pt"]
            for f in range(16, 32):
                k.mm(pt[:], hT[:, f, :], w2[:, f, 512:1024], False, f == 31, [hTb, w2b], [ptb])
            k.release(s2st["i"])
            k.stt(r2[:, 512:1024], x1[:, 512:1024], float(ALPHA), pt[:], ALU.mult, ALU.add, [x1b, ptb], [r2b], partial=True)
            layernorm(k, r2[:], r2b, r2[:], r2b, G2, B2, tmp2)
            k.dma(Xout[tk, :], r2[:], r2b, reads=[r2b])

        def s3(t):
            tk = slice(t * 128, (t + 1) * 128)
            if write_xt:
                transpose_tile(k, r2, r2b, identF, x2T, x2Tb, 0)
                k.dma(XTv[:, :, tk], x2T[:], x2Tb, reads=[x2Tb])

        loads(0)
        for t in range(NT + 1):
            if t + 1 < NT:
                loads(t + 1)
            if t < NT:
                s1a(t)
            if t > 0:
                s2(t - 1, 0)
                s2(t - 1, 1)
            if t < NT:
                s1bT(t)
            if t > 0:
                s2(t - 1, 2)
            if t < NT:
                s1bF(t)
            if t > 0:
                s3(t - 1)
    S.flush()


_CACHE = {}
IN_SPECS = [
    ("x", [T, D], F32), ("mem", [256, D], F32), ("pos", [1, T], I32),
    ("mem_ln_g", [1, D], F32), ("mem_ln_b", [1, D], F32), ("w_mem_kv", [D, 512], F32),
    ("WA", [D, 960], F32), ("qn", [128, 3], F32), ("WQ", [384, 1536], F32), ("kvn", [128, 2], F32),
    ("WKV", [256, 1536], F32),
    ("w_out", [2, D, D], F32), ("ln1_g", [2, D], F32), ("ln1_b", [2, D], F32),
    ("w_ff1", [2, D, 4096], F32), ("w_ff2", [2, 4096, D], F32), ("ln2_g", [2, D], F32), ("ln2_b", [2, D], F32),
    ("identF", [128, 128], F32), ("tri", [128, 128], F32), ("invf", [128, 1], F32), ("sgn", [128, 1], F32),
]
SCRATCH = [
    ("XT", [D, T], BF16), ("CCd", [128, T], F32), ("SSd", [128, T], F32),
    ("QT", [12, 96, T], BF16), ("KT", [12, 64, T], BF16), ("KR", [32, T], BF16),
    ("VA", [12, 128, NT, 128], BF16), ("QM", [256, T], BF16), ("OT", [D, T], BF16),
    ("X1", [T, D], F32),
    ("WOb", [2, D, D], BF16), ("W1b", [2, D, 4096], BF16), ("W2b", [2, 4096, D], BF16),
]


def build(stage="full", dbg=()):
    nc = bass.Bass("TRN2", target_bir_lowering=False)
    d = {}
    for name, shape, dt in IN_SPECS + RW_IN_SPECS:
        d[name] = nc.dram_tensor(name, list(shape), dt, kind="ExternalInput").ap()
    for name, shape, dt in SCRATCH + RW_SCRATCH:
        d[name] = nc.dram_tensor(name, list(shape), dt, kind=("ExternalOutput" if name in dbg else "Internal")).ap()
    d["out"] = nc.dram_tensor("out", [T, D], F32, kind="ExternalOutput").ap()
    with contextlib.ExitStack() as es:
        S = Sched(nc, es)
        _CACHE["S"] = S
        k = K(nc, S, es)
        g = {
            "identF": k.sb(es, "identF", [128, 128], F32),
            "tri": k.sb(es, "tri", [128, 128], BF16),
            "ones": k.sb(es, "ones", [128, 128], BF16),
            "mem_kT": k.sb(es, "mem_kT", [128, 2, 256], BF16),
            "mem_va": k.sb(es, "mem_va", [128, 2, 4, 128], BF16),
        }
        phase0(k, d, g)
        if stage == "p0":
            return nc
        phaseA(k, d, g)
        if stage == "pA":
            return nc
        phaseB(k, d, g, True)
        if stage == "pB":
            return nc
        phaseC(k, d, g, 0, d["x"], d["out"] if stage == "L0" else d["X1"], stage != "L0")
        if stage == "L0":
            return nc
        rwkv_layer(k, d, g, stage)
    return nc


def host_inputs(inp):
    f = lambda a: np.ascontiguousarray(np.asarray(a), dtype=np.float32)
    w_in = f(inp["mla_w_in"])[0]
    sw = (np.arange(32) + 16) % 32
    kr_cols = 640 + np.arange(32)
    WA = np.concatenate([w_in[:, 0:384], w_in[:, 384:640], w_in[:, 672:928], w_in[:, kr_cols], w_in[:, kr_cols[sw]]], axis=1)
    wq = f(inp["mla_w_q_up"])[0].reshape(384, 12, 96)
    WQ = np.concatenate([wq[:, :, 0:64].reshape(384, 768), wq[:, :, 64:96].reshape(384, 384),
                         wq[:, :, 64:96][:, :, sw].reshape(384, 384)], axis=1)
    wkv = f(inp["mla_w_kv_up"])[0].reshape(256, 12, 128)
    WKV = np.concatenate([wkv[:, :, 0:64].reshape(256, 768), wkv[:, :, 64:128].reshape(256, 768)], axis=1)
    p = np.arange(128)
    invf = (10000.0 ** (-(np.arange(16, dtype=np.float32)) * 2.0 / 32.0)).astype(np.float32)
    common = {
        "mem_ln_g": f(inp["mem_ln_g"]).reshape(1, D), "mem_ln_b": f(inp["mem_ln_b"]).reshape(1, D),
        "w_mem_kv": f(inp["w_mem_kv"]),
        "WA": f(WA), "qn": f(f(inp["mla_q_norm"])[0].reshape(3, 128).T), "WQ": f(WQ),
        "kvn": f(f(inp["mla_kv_norm"])[0].reshape(2, 128).T), "WKV": f(WKV),
        "w_out": f(inp["w_out"]), "ln1_g": f(inp["ln1_g"]), "ln1_b": f(inp["ln1_b"]),
        "w_ff1": f(inp["w_ff1"]), "w_ff2": f(inp["w_ff2"]), "ln2_g": f(inp["ln2_g"]), "ln2_b": f(inp["ln2_b"]),
        "identF": np.eye(128, dtype=np.float32),
        "tri": f(p[:, None] <= p[None, :]),
        "invf": f((invf[p % 16] / np.float32(2.0 * np.pi)).reshape(128, 1)),
        "sgn": f(np.where((p % 32) < 16, -1.0, 1.0).reshape(128, 1)),
    }
    common.update(rwkv_host_inputs(inp))
    maps = []
    x = np.asarray(inp["x"])
    mem = np.asarray(inp["mem"])
    pos = np.asarray(inp["positions"])
    for b in range(8):
        m = dict(common)
        m["x"] = f(x[b])
        m["mem"] = f(mem[b])
        m["pos"] = np.ascontiguousarray(pos[b].reshape(1, T).astype(np.int32))
        maps.append(m)
    return maps


def kernel(**inputs):
    if "nc" not in _CACHE:
        _CACHE["nc"] = build("full")
    maps = host_inputs(inputs)
    res = run_bass_kernel_spmd(_CACHE["nc"], maps, core_ids=list(range(8)))
    return np.stack([np.asarray(r["out"], dtype=np.float32) for r in res.results], axis=0)


NCH = T // 64
WF_COLS = 1824
RW_IN_SPECS = [
    ("WF", [D, WF_COLS], F32), ("MUF", [1, WF_COLS], F32), ("WV", [D, 768], F32), ("MUV", [1, 768], F32),
    ("WQM", [D, 256], F32), ("w2a2", [128, 768], F32), ("g2a", [128, 768], F32), ("g2b", [32, 768], F32),
    ("PRM", [128, 6, 5], F32), ("gn_g", [1, 768], F32), ("gn_b", [1, 768], F32),
    ("RST", [128, 512], F32), ("SEL", [128, 2], F32), ("BONES", [128, 128], F32),
    ("MK1", [128, 512], F32), ("ML", [128, 512], F32), ("IDN", [128, 512], F32),
]
RW_SCRATCH = [
    ("ARd", [6, 128, NCH, 128], BF16), ("KBd", [6, 128, NCH, 128], BF16),
    ("KBWd", [12, 64, NCH, 2, 64], BF16), ("Vcd", [12, 64, NCH, 64], BF16),
    ("Vtd", [T, 768], BF16), ("Gtd", [T, 768], F32), ("BONd", [T, 12], F32),
    ("WCd", [6, 128, NCH], F32), ("Ysd", [T, 768], F32),
]


def rwkv_host_inputs(inp):
    f = lambda a: np.ascontiguousarray(np.asarray(a), dtype=np.float32)
    w = f(inp["rwkv_w_in"])[0]
    mu = f(inp["rwkv_mu"])[0]
    fc = np.concatenate([np.arange(0, 1536), np.arange(2304, 2592)])
    col = lambda v: f(f(v).reshape(6, 128).T)
    PRM = np.stack([col(inp["rwkv_w0"][0]), col(inp["rwkv_a0"][0]), col(inp["rwkv_k_k"][0]),
                    col(inp["rwkv_k_a"][0]), col(np.asarray(inp["rwkv_r_k"])[0].reshape(768))], axis=-1)
    p = np.arange(128)
    c = np.arange(512)
    s_ = (p % 64)[:, None]
    t_ = (c % 64)[None, :]
    is_a = ((c // 64) % 2 == 1)[None, :]
    return {
        "WF": f(w[:, fc]), "MUF": f(mu[fc].reshape(1, -1)), "WV": f(w[:, 1536:2304]), "MUV": f(mu[1536:2304].reshape(1, -1)),
        "WQM": f(w[:, 2592:2848]),
        "w2a2": f(np.concatenate([f(inp["rwkv_w2"])[0], f(inp["rwkv_a2"])[0]], axis=0)),
        "g2a": f(f(inp["rwkv_g2"])[0][0:128]), "g2b": f(f(inp["rwkv_g2"])[0][128:160]),
        "PRM": f(PRM), "gn_g": f(inp["rwkv_gn_g"]).reshape(1, 768), "gn_b": f(inp["rwkv_gn_b"]).reshape(1, 768),
        "RST": f(np.broadcast_to((c % 64 != 0)[None, :], (128, 512))),
        "SEL": f(np.stack([p < 64, p >= 64], axis=1)),
        "BONES": f((p[:, None] // 64) == (p[None, :] // 64)),
        "MK1": f(np.where(is_a, s_ < t_, s_ <= t_)),
        "ML": f(s_ > t_),
        "IDN": f(s_ == t_),
    }


def rwkv_prep(k, d, g, es_w):
    S = k.S
    W = {}
    W["F1"] = k.sb(es_w, "WF1", [128, 8, WF_COLS], BF16)
    W["F2"] = k.sb(es_w, "WF2", [128, 8, WF_COLS], BF16)
    W["V1"] = k.sb(es_w, "WV1", [128, 8, 768], BF16)
    W["V2"] = k.sb(es_w, "WV2", [128, 8, 768], BF16)
    small = [("QM", "WQM", D, 256), ("w2a2", "w2a2", 128, 768), ("g2a", "g2a", 128, 768), ("g2b", "g2b", 32, 768),
             ("bones", "BONES", 128, 128), ("sel", "SEL", 128, 2)]
    for nm, src_, R_, C_ in small:
        W[nm] = k.sb(es_w, nm + "_b", [128, (R_ + 127) // 128, C_], BF16)
    with contextlib.ExitStack() as es:
        stg = Stage(k, es, n=2, cols=WF_COLS)
        for (src, mus, C, t1, t2) in ((d["WF"], d["MUF"], WF_COLS, W["F1"], W["F2"]), (d["WV"], d["MUV"], 768, W["V1"], W["V2"])):
            MU = bcast_load(k, es, "MU%d" % C, mus, C)
            OM, OMb = k.sb(es, "OM%d" % C, [128, C], F32)
            k.ts("dve", OM[:], MU[0][:], -1.0, 1.0, ALU.mult, ALU.add, [MU[1]], [OMb])
            for c in range(8):
                st, stb = stg.nxt()
                k.dma(st[:, 0:C], src[c * 128:(c + 1) * 128, :], stb, writes=[stb])
                k.tt("dve", t1[0][:, c, :], st[:, 0:C], OM[:], ALU.mult, [stb, OMb], [t1[1]], partial=True)
                k.tt("pool", t2[0][:, c, :], st[:, 0:C], MU[0][:], ALU.mult, [stb, MU[1]], [t2[1]], partial=True)
        for nm, src_, R_, C_ in small:
            load_w(k, es_w, stg, nm, d[src_], R_, C_, dst=W[nm])
        S.flush()
    return W


def phaseD(k, d, g):
    S = k.S
    with contextlib.ExitStack() as es:
        W = rwkv_prep(k, d, g, es)
        F1, F1b = W["F1"]
        F2, F2b = W["F2"]
        V1, V1b = W["V1"]
        V2, V2b = W["V2"]
        prm, prmb = k.sb(es, "prm", [128, 6, 6], F32)
        k.dma(prm[:, :, 0:5], d["PRM"], prmb, writes=[prmb])
        k.ts("dve", prm[:, :, 5:6], prm[:, :, 3:4], -1.0, 1.0, ALU.mult, ALU.add, [prmb], [prmb])
        rst, rstb = k.sb(es, "rst", [128, 512], F32)
        k.dma(rst[:], d["RST"], rstb, writes=[rstb])
        xThs = [k.sb(es, "xTh%d" % i, [128, 8, 514], BF16) for i in range(2)]
        lr, lrb = k.sb(es, "lr", [128, 512], BF16)
        SG, SGb = k.sb(es, "SG", [128, 512], BF16)
        SG2, SG2b = k.sb(es, "SG2", [32, 512], BF16)
        vtok, vtokb = k.sb(es, "vtok", [128, 768], BF16)
        gtok, gtokb = k.sb(es, "gtok", [128, 768], F32)
        obs = [k.sb(es, "qmo%d" % i, [128, 512], BF16) for i in range(2)]
        fnames = ["r_s", "k_s", "lw", "a_s", "kk", "nrm", "kp", "bsc", "cw", "cwx", "dlt", "E1", "E2", "kW", "bW"]
        Fws = [{n: k.sb(es, "%s_%d" % (n, i), [128, 512], F32) for n in fnames} for i in range(2)]
        for Fw_ in Fws:
            Fw_["kkn"] = Fw_["kk"]
            Fw_["tmp"] = Fw_["kp"]
            Fw_["E3"] = Fw_["cwx"]
            Fw_["E4"] = Fw_["dlt"]
        kk2s = [k.sb(es, "kk2_%d" % i, [128, 512], BF16) for i in range(2)]
        rkps = [k.sb(es, "rkp_%d" % i, [128, 512], BF16) for i in range(2)]
        kbws = [k.sb(es, "kbw_%d" % i, [128, 2, 2, 64], BF16) for i in range(4)]
        ARts = [k.sb(es, "ARt%d" % i, [128, 8, 2, 64], BF16) for i in range(2)]
        KBts = [k.sb(es, "KBt%d" % i, [128, 8, 2, 64], BF16) for i in range(2)]
        wcs = [k.sb(es, "wc%d" % i, [128, 8], F32) for i in range(2)]
        bon, bonb = k.sb(es, "bon", [128, 4, 12], F32)
        XTv = d["XT"].rearrange("(c p) t -> p c t", p=128)
        KBWv = d["KBWd"].rearrange("h s c b n -> s h c b n")
        Vcv = d["Vcd"].rearrange("h s c n -> s h c n")
        BONv = d["BONd"].rearrange("(a p) h -> p a h", p=128)
        w2a2, w2a2b = W["w2a2"]
        g2a, g2ab = W["g2a"]
        g2b, g2bb = W["g2b"]
        bones, bonesb = W["bones"]
        sel, selb = W["sel"]
        identF = g["identF"]

        def proj(pt, ptb, rows, c0, c1, xTh, xThb):
            for c in range(8):
                k.mm(pt[0:rows, :], F1[:, c, c0:c1], xTh[:, c, 1:513], c == 0, False, [F1b, xThb], [ptb])
            for c in range(8):
                k.mm(pt[0:rows, :], F2[:, c, c0:c1], xTh[:, c, 0:512], False, c == 7, [F2b, xThb], [ptb])

        for s in range(T // 512):
            tk0 = s * 512
            xTh, xThb = xThs[s % 2]
            if s == 0:
                k.memset("pool", xTh[:, :, 0:1], 0.0, [xThb])
                k.dma(xTh[:, :, 1:513], XTv[:, :, 0:512], xThb, writes=[xThb], partial=True)
            if s + 1 < T // 512:
                nx, nxb = xThs[(s + 1) % 2]
                k.dma(nx[:, :, 0:513], XTv[:, :, tk0 + 511:tk0 + 1024], nxb, writes=[nxb])
            pt, ptb = k.nps()
            proj(pt, ptb, 128, 1536, 1664, xTh, xThb)
            k.act(lr[0:64, :], pt[0:64, :], AF.Tanh, [ptb], [lrb])
            k.cp("act", lr[64:128, :], pt[64:128, :], [ptb], [lrb], partial=True)
            pt, ptb = k.nps()
            proj(pt, ptb, 128, 1664, 1792, xTh, xThb)
            k.act(SG[:], pt[:], AF.Sigmoid, [ptb], [SGb])
            pt, ptb = k.nps()
            proj(pt, ptb, 32, 1792, 1824, xTh, xThb)
            k.act(SG2[0:32, :], pt[0:32, :], AF.Sigmoid, [ptb], [SG2b])
            def front_b():
                WQ_, WQb_ = W["QM"]
                for j in range(2):
                    pt, ptb = k.nps()
                    for c in range(8):
                        k.mm(pt[:], WQ_[:, c, j * 128:(j + 1) * 128], xTh[:, c, 1:513], c == 0, c == 7, [WQb_, xThb], [ptb])
                    o, ob = obs[j]
                    k.cp("act", o[:], pt[:], [ptb], [ob])
                    yield
                    k.dma(d["QM"][j * 128:(j + 1) * 128, tk0:tk0 + 512], o[:], ob, reads=[ob])
                    yield
                for t4 in range(4):
                    tk = slice(tk0 + t4 * 128, tk0 + (t4 + 1) * 128)
                    for (c0, c1) in ((0, 512), (512, 768)):
                        pt, ptb = k.nps()
                        n = c1 - c0
                        for c in range(8):
                            k.mm(pt[:, 0:n], xTh[:, c, 1 + t4 * 128:1 + (t4 + 1) * 128], V1[:, c, c0:c1], c == 0, False, [V1b, xThb], [ptb])
                        for c in range(8):
                            k.mm(pt[:, 0:n], xTh[:, c, t4 * 128:(t4 + 1) * 128], V2[:, c, c0:c1], False, c == 7, [V2b, xThb], [ptb])
                        k.cp("act", vtok[:, c0:c1], pt[:, 0:n], [ptb], [vtokb], partial=(c0 > 0))
                        yield
                    k.dma(d["Vtd"][tk, :], vtok[:], vtokb, reads=[vtokb])
                    yield
                    for half in range(2):
                        cg = (s * 4 + t4) * 2 + half
                        k.dma(Vcv[:, :, cg, :], vtok[half * 64:(half + 1) * 64, :].rearrange("p (h n) -> p h n", h=12), vtokb, reads=[vtokb])
                        yield
                    for (c0, c1) in ((0, 512), (512, 768)):
                        pt, ptb = k.nps()
                        n = c1 - c0
                        k.mm(pt[:, 0:n], SG[:, t4 * 128:(t4 + 1) * 128], g2a[:, 0, c0:c1], True, False, [SGb, g2ab], [ptb])
                        k.mm(pt[:, 0:n], SG2[0:32, t4 * 128:(t4 + 1) * 128], g2b[0:32, 0, c0:c1], False, True, [SG2b, g2bb], [ptb])
                        k.cp("dve", gtok[:, c0:c1], pt[:, 0:n], [ptb], [gtokb], partial=(c0 > 0))
                        yield
                    k.dma(d["Gtd"][tk, :], gtok[:], gtokb, reads=[gtokb])
                    yield
            pbon_i, (pbon, pbonb) = k.reserve()
            def pair(p):
                ARt, ARtb = ARts[p % 2]
                KBt, KBtb = KBts[p % 2]
                wc, wcb = wcs[p % 2]
                Fw = Fws[p % 2]
                kk2, kk2b = kk2s[p % 2]
                rkp, rkpb = rkps[p % 2]
                f = lambda n, Fw=Fw: Fw[n][0]
                fb = lambda n, Fw=Fw: Fw[n][1]
                pcol = lambda i: prm[:, p, i:i + 1]
                psr, psrb = k.nps()
                proj(psr, psrb, 128, p * 128, (p + 1) * 128, xTh, xThb)
                yield
                k.cp("act", f("r_s")[:], psr[:], [psrb], [fb("r_s")])
                yield
                psk, pskb = k.nps()
                proj(psk, pskb, 128, 768 + p * 128, 768 + (p + 1) * 128, xTh, xThb)
                yield
                k.cp("dve", f("k_s")[:], psk[:], [pskb], [fb("k_s")])
                yield
                k.act(kk2[:], psk[:], AF.Square, [pskb, prmb], [kk2b], scale=pcol(2))
                yield
                psw, pswb = k.nps()
                k.mm(psw[:], w2a2[0:64, 0, p * 128:(p + 1) * 128], lr[0:64, :], True, True, [w2a2b, lrb], [pswb])
                yield
                k.act(f("lw")[:], psw[:], AF.Sigmoid, [pswb, prmb], [fb("lw")], bias=pcol(0))
                yield
                k.ts("dve", f("lw")[:], f("lw")[:], -float(np.exp(-0.5)), None, ALU.mult, None, [fb("lw")], [fb("lw")])
                yield
                psa, psab = k.nps()
                k.mm(psa[:], w2a2[64:128, 0, p * 128:(p + 1) * 128], lr[64:128, :], True, True, [w2a2b, lrb], [psab])
                yield
                k.act(f("a_s")[:], psa[:], AF.Sigmoid, [psab, prmb], [fb("a_s")], bias=pcol(1))
                yield
                k.ts("dve", f("kk")[:], f("k_s")[:], pcol(2), None, ALU.mult, None, [fb("k_s"), prmb], [fb("kk")])
                yield
                pss, pssb = k.nps()
                k.mm(pss[:], bones[:, 0, :], kk2[:], True, True, [bonesb, kk2b], [pssb])
                yield
                k.ts("dve", f("nrm")[:], pss[:], 1e-24, None, ALU.add, None, [pssb], [fb("nrm")])
                yield
                k.act(f("nrm")[:], f("nrm")[:], AF.Ln, [fb("nrm")], [fb("nrm")])
                yield
                k.act(f("nrm")[:], f("nrm")[:], AF.Exp, [fb("nrm")], [fb("nrm")], scale=-0.5)
                yield
                k.tt("pool", f("kkn")[:], f("kk")[:], f("nrm")[:], ALU.mult, [fb("kk"), fb("nrm")], [fb("kkn")])
                yield
                k.ts("dve", f("tmp")[:], f("a_s")[:], pcol(3), pcol(5), ALU.mult, ALU.add, [fb("a_s"), prmb], [fb("tmp")])
                yield
                k.tt("pool", f("kp")[:], f("k_s")[:], f("tmp")[:], ALU.mult, [fb("k_s"), fb("tmp")], [fb("kp")])
                yield
                k.tt("pool", f("bsc")[:], f("kkn")[:], f("a_s")[:], ALU.mult, [fb("kkn"), fb("a_s")], [fb("bsc")])
                yield
                k.S.op("dve", lambda e, Fw=Fw: e.tensor_tensor_scan(out=Fw["cw"][0][:], data0=rst[:], data1=Fw["lw"][0][:], initial=0.0,
                                                                     op0=ALU.mult, op1=ALU.add), [rstb, fb("lw")], [fb("cw")])
                yield
                k.tt("pool", f("cwx")[:], f("cw")[:], f("lw")[:], ALU.subtract, [fb("cw"), fb("lw")], [fb("cwx")])
                yield
                cw3 = f("cw")[:].rearrange("p (c t) -> p c t", t=64)
                k.tt("dve", f("dlt")[:].rearrange("p (c t) -> p c t", t=64), cw3[:, :, 63:64].to_broadcast([128, 8, 64]), cw3,
                     ALU.subtract, [fb("cw")], [fb("dlt")])
                yield
                k.act(f("E1")[:], f("cw")[:], AF.Exp, [fb("cw")], [fb("E1")])
                yield
                k.act(f("E2")[:], f("cw")[:], AF.Exp, [fb("cw")], [fb("E2")], scale=-1.0)
                yield
                k.act(f("E3")[:], f("cwx")[:], AF.Exp, [fb("cwx")], [fb("E3")])
                yield
                k.act(f("E4")[:], f("dlt")[:], AF.Exp, [fb("dlt")], [fb("E4")])
                yield
                k.act(wc[:].rearrange("p (c o) -> p c o", o=1), cw3[:, :, 63:64], AF.Exp, [fb("cw")], [wcb])
                yield
                v4 = lambda t_, i: t_[:, :, i, :]
                k.tt("pool", v4(ARt, 0), f("r_s")[:].rearrange("p (c t) -> p c t", t=64), f("E1")[:].rearrange("p (c t) -> p c t", t=64),
                     ALU.mult, [fb("r_s"), fb("E1")], [ARtb])
                yield
                k.stt(v4(ARt, 1), f("kkn")[:].rearrange("p (c t) -> p c t", t=64), -1.0, f("E3")[:].rearrange("p (c t) -> p c t", t=64),
                      ALU.mult, ALU.mult, [fb("kkn"), fb("E3")], [ARtb], partial=True)
                yield
                r3 = lambda t_, lo: t_[lo:lo + 64, :].rearrange("p (c t) -> p c t", t=64)
                for lo in (0, 64):
                    ki, bi = (0, 1) if lo == 0 else (1, 0)
                    k.tt("pool", KBt[lo:lo + 64, :, ki, :], r3(f("kp"), lo), r3(f("E2"), lo), ALU.mult, [fb("kp"), fb("E2")], [KBtb],
                         partial=(lo == 64))
                    yield
                    k.tt("dve", KBt[lo:lo + 64, :, bi, :], r3(f("bsc"), lo), r3(f("E2"), lo), ALU.mult, [fb("bsc"), fb("E2")], [KBtb],
                         partial=True)
                    yield
                k.tt("pool", f("kW")[:], f("kp")[:], f("E4")[:], ALU.mult, [fb("kp"), fb("E4")], [fb("kW")])
                yield
                k.tt("dve", f("bW")[:], f("bsc")[:], f("E4")[:], ALU.mult, [fb("bsc"), fb("E4")], [fb("bW")])
                yield
                k.stt(rkp[:], f("r_s")[:], pcol(4), f("kp")[:], ALU.mult, ALU.mult, [fb("r_s"), fb("kp"), prmb], [rkpb])
                yield
                k.dma(d["ARd"][p][:, s * 8:(s + 1) * 8, :], ARt[:].rearrange("p c a t -> p c (a t)"), ARtb, reads=[ARtb])
                yield
                k.dma(d["KBd"][p][:, s * 8:(s + 1) * 8, :], KBt[:].rearrange("p c a t -> p c (a t)"), KBtb, reads=[KBtb])
                yield
                k.dma(d["WCd"][p][:, s * 8:(s + 1) * 8], wc[:], wcb, reads=[wcb])
                yield
                for t4 in range(4):
                    k.mm(pbon[:, t4 * 12 + 2 * p:t4 * 12 + 2 * p + 2], rkp[:, t4 * 128:(t4 + 1) * 128], sel[:, 0, :], True, True,
                         [rkpb, selb], [pbonb])
                    yield
                    kbw, kbwb = kbws[(p % 2) * 2 + t4 % 2]
                    pT, pTb = k.nps()
                    k.tp(pT[:, 0:128], f("kW")[:, t4 * 128:(t4 + 1) * 128], identF[0][:], [fb("kW"), identF[1]], [pTb], first=True)
                    yield
                    k.tp(pT[:, 128:256], f("bW")[:, t4 * 128:(t4 + 1) * 128], identF[0][:], [fb("bW"), identF[1]], [pTb])
                    yield
                    k.cp(k.alt(), kbw[:].rearrange("p h b n -> p b h n"), pT[:, 0:256].rearrange("p (b h n) -> p b h n", b=2, h=2),
                         [pTb], [kbwb])
                    yield
                    for half in range(2):
                        cg = (s * 4 + t4) * 2 + half
                        k.dma(KBWv[:, 2 * p:2 * p + 2, cg, :, :], kbw[half * 64:(half + 1) * 64, :, :, :], kbwb, reads=[kbwb])
                        yield
            interleave([front_b(), interleave_gen((pair(p) for p in range(6)), 2)], 2)
            k.cp("act", bon[:].rearrange("p a h -> p (a h)"), pbon[:, 0:48], [pbonb], [bonb])
            k.release(pbon_i)
            k.dma(BONv[:, s * 4:(s + 1) * 4, :], bon[:], bonb, reads=[bonb])
    S.flush()


def phaseE(k, d, g):
    S = k.S
    with contextlib.ExitStack() as es:
        stg = Stage(k, es, n=2, cols=512)
        MK1, MK1b = load_w(k, es, stg, "MK1b", d["MK1"], 128, 512)
        ML, MLb = load_w(k, es, stg, "MLb", d["ML"], 128, 512)
        IDN, IDNb = load_w(k, es, stg, "IDNb", d["IDN"], 128, 512)
        STs = [k.sb(es, "ST%d" % i, [128, 3, 64], F32) for i in range(2)]
        SBs = [k.sb(es, "SB%d" % i, [128, 3, 64], BF16) for i in range(2)]
        for i in range(2):
            k.memset("pool", STs[i][0][:], 0.0, [STs[i][1]])
            k.memset("pool", SBs[i][0][:], 0.0, [SBs[i][1]])
        sets = []
        for i in range(2):
            sets.append({
                "AR": [k.sb(es, "AR%d_%d" % (i, p), [128, 8, 128], BF16) for p in range(6)],
                "KB": [k.sb(es, "KB%d_%d" % (i, p), [128, 8, 128], BF16) for p in range(6)],
                "KW": [k.sb(es, "KW%d_%d" % (i, h), [128, 8, 64], BF16) for h in range(12)],
                "VU": [k.sb(es, "VU%d_%d" % (i, gI), [128, 3, 8, 2, 64], BF16) for gI in range(2)],
                "WC": k.sb(es, "WC%d" % i, [128, 6, 8], F32),
            })
        M1s = [[k.sb(es, "M1_%d_%d" % (i, h), [128, 8, 128], BF16) for h in range(12)] for i in range(2)]
        PFs = [[k.sb(es, "PF_%d_%d" % (i, p), [128, 8, 128], BF16) for p in range(6)] for i in range(2)]
        XD = [k.sb(es, "XD%d" % i, [128, 8, 128], BF16) for i in range(3)]
        XTD = [k.sb(es, "XTD%d" % i, [128, 8, 128], BF16) for i in range(3)]
        PD = [k.sb(es, "PD%d" % i, [128, 8, 128], BF16) for i in range(3)]
        Zs = [k.sb(es, "Zs%d" % i, [128, 192], BF16) for i in range(2)]
        Yb = [k.sb(es, "Yb%d" % i, [64, 768], F32) for i in range(2)]
        cnt = {"x": 0}
        KBWv = d["KBWd"]

        def loads(b):
            st = sets[b % 2]
            cs = slice(b * 8, (b + 1) * 8)
            for p in range(6):
                k.dma(st["AR"][p][0][:], d["ARd"][p][:, cs, :], st["AR"][p][1], writes=[st["AR"][p][1]])
                k.dma(st["KB"][p][0][:], d["KBd"][p][:, cs, :], st["KB"][p][1], writes=[st["KB"][p][1]])
            k.dma(st["WC"][0][:], d["WCd"].rearrange("p q c -> q p c")[:, :, cs], st["WC"][1], writes=[st["WC"][1]])
            for h in range(12):
                e = h % 2
                kp_, bp_ = e * 64, 64 - e * 64
                kw, kwb = st["KW"][h]
                k.dma(kw[kp_:kp_ + 64, :, :], KBWv[h][:, cs, 0, :], kwb, writes=[kwb])
                k.dma(kw[bp_:bp_ + 64, :, :], KBWv[h][:, cs, 1, :], kwb, writes=[kwb], partial=True)
                vu, vub = st["VU"][h // 6]
                k.dma(vu[kp_:kp_ + 64, (h % 6) // 2, :, e, :], d["Vcd"][h][:, cs, :], vub, writes=[vub], partial=True)

        def setup(b):
            for p in range(6):
                yield from setup_pair(b, p)

        def setup_pair(b, p):
            st = sets[b % 2]
            AR, ARb = st["AR"][p]
            KB, KBb = st["KB"][p]
            for e in range(2):
                h, hp = 2 * p + e, e * 64
                m1, m1b = M1s[b % 2][h]
                for half in range(2):
                    ps, psb = k.nps()
                    for j4 in range(4):
                        j = half * 4 + j4
                        k.mm(ps[:, j4 * 128:(j4 + 1) * 128], KB[hp:hp + 64, j, :], AR[hp:hp + 64, j, :], True, True, [KBb, ARb], [psb])
                    k.tt("dve", m1[:, half * 4:(half + 1) * 4, :].rearrange("p c n -> p (c n)"), ps[:], MK1[:, 0, :], ALU.mult,
                         [psb, MK1b], [m1b], partial=(half == 1))
                    yield
            ci = cnt["x"]
            cnt["x"] += 1
            X, Xb = XD[ci % 3]
            XT, XTb = XTD[ci % 3]
            P, Pb = PD[ci % 3]
            k.memset("pool", X[:], 0.0, [Xb])
            k.memset("pool", XT[:], 0.0, [XTb])
            k.memset("pool", P[:], 0.0, [Pb])
            yield
            v3 = lambda a: a.rearrange("p (c n) -> p c n", n=64)
            for e in range(2):
                h, hp, bp_ = 2 * p + e, e * 64, 64 - e * 64
                m1, m1b = M1s[b % 2][h]
                bc = slice(64, 128) if e == 0 else slice(0, 64)
                ps, psb = k.nps()
                for j in range(8):
                    k.mm(ps[bp_:bp_ + 64, j * 64:(j + 1) * 64], AR[hp:hp + 64, j, 64:128], KB[hp:hp + 64, j, bc], True, True, [ARb, KBb], [psb])
                k.tt("dve", X[bp_:bp_ + 64, :, bp_:bp_ + 64], v3(ps[bp_:bp_ + 64, :]), v3(ML[bp_:bp_ + 64, 0, :]), ALU.mult,
                     [psb, MLb], [Xb], partial=True)
                k.cp("pool", XT[bp_:bp_ + 64, :, bp_:bp_ + 64], m1[bp_:bp_ + 64, :, 64:128], [m1b], [XTb], partial=True)
                k.tt("pool", P[bp_:bp_ + 64, :, bp_:bp_ + 64], m1[bp_:bp_ + 64, :, 64:128], v3(IDN[bp_:bp_ + 64, 0, :]), ALU.add,
                     [m1b, IDNb], [Pb], partial=True)
                yield
            fl = lambda T_, half: T_[:, half * 4:(half + 1) * 4, :].rearrange("p c n -> p (c n)")
            for lvl in range(1, 6):
                ci = cnt["x"]
                cnt["x"] += 1
                Xn, Xnb = XD[ci % 3]
                for half in range(2):
                    ps, psb = k.nps()
                    for j4 in range(4):
                        j = half * 4 + j4
                        k.mm(ps[:, j4 * 128:(j4 + 1) * 128], XT[:, j, :], X[:, j, :], True, True, [XTb, Xb], [psb])
                    k.cp("act", fl(Xn, half), ps[:], [psb], [Xnb], partial=(half == 1))
                    yield
                if lvl < 5:
                    XTn, XTnb = XTD[ci % 3]
                    for half in range(2):
                        ps, psb = k.nps()
                        for j4 in range(4):
                            j = half * 4 + j4
                            k.mm(ps[:, j4 * 128:(j4 + 1) * 128], X[:, j, :], XT[:, j, :], True, True, [XTb, Xb], [psb])
                        k.cp("act" if half == 0 else "dve", fl(XTn, half), ps[:], [psb], [XTnb], partial=(half == 1))
                        yield
                Pn, Pnb = PD[ci % 3] if lvl < 5 else PFs[b % 2][p]
                for half in range(2):
                    ps, psb = k.nps()
                    for j4 in range(4):
                        j = half * 4 + j4
                        k.mm(ps[:, j4 * 128:(j4 + 1) * 128], Xn[:, j, :], P[:, j, :], True, True, [Xnb, Pb], [psb])
                    k.tt("dve", fl(Pn, half), ps[:], fl(P, half), ALU.add, [psb, Pb], [Pnb], partial=(half == 1))
                    yield
                X, Xb = Xn, Xnb
                if lvl < 5:
                    XT, XTb = XTn, XTnb
                P, Pb = Pn, Pnb

        def pull(gen, n):
            if gen is None:
                return
            for _ in range(n):
                try:
                    next(gen)
                except StopIteration:
                    return

        def sequential(b, gen):
            st = sets[b % 2]
            M1, PF = M1s[b % 2], PFs[b % 2]
            WC, WCb = st["WC"]
            for j in range(8):
                cg = b * 8 + j
                for gI in range(2):
                    vu, vub = st["VU"][gI]
                    SB, SBb = SBs[gI]
                    ps, psb = k.nps()
                    for m in range(6):
                        h = gI * 6 + m
                        p, e, pl = h // 2, h % 2, m // 2
                        hp, kp_, bp_ = e * 64, e * 64, 64 - e * 64
                        AR, ARb = st["AR"][p]
                        m1, m1b = M1[h]
                        k.mm(ps[bp_:bp_ + 64, pl * 64:(pl + 1) * 64], m1[kp_:kp_ + 64, j, 64:128], vu[kp_:kp_ + 64, pl, j, e, :],
                             True, False, [m1b, vub], [psb])
                        k.mm(ps[bp_:bp_ + 64, pl * 64:(pl + 1) * 64], AR[hp:hp + 64, j, 64:128], SB[hp:hp + 64, pl, :],
                             False, True, [ARb, SBb], [psb])
                    zs, zsb = Zs[gI]
                    k.cp("act", zs[:], ps[:, 0:192], [psb], [zsb])
                pull(gen, 10)
                for gI in range(2):
                    vu, vub = st["VU"][gI]
                    zs, zsb = Zs[gI]
                    ps, psb = k.nps()
                    for m in range(6):
                        h = gI * 6 + m
                        e, pl = h % 2, m // 2
                        bp_ = 64 - e * 64
                        pf, pfb = PF[h // 2]
                        k.mm(ps[bp_:bp_ + 64, pl * 64:(pl + 1) * 64], pf[bp_:bp_ + 64, j, bp_:bp_ + 64], zs[bp_:bp_ + 64, pl * 64:(pl + 1) * 64],
                             True, True, [pfb, zsb], [psb])
                    for e in range(2):
                        bp_ = 64 - e * 64
                        k.cp("act", vu[bp_:bp_ + 64, :, j, e, :], ps[bp_:bp_ + 64, 0:192].rearrange("p (a n) -> p a n", n=64), [psb], [vub],
                             partial=True)
                pull(gen, 10)
                for gI in range(2):
                    vu, vub = st["VU"][gI]
                    SB, SBb = SBs[gI]
                    ST, STb_ = STs[gI]
                    pss, pssb = k.nps()
                    for m in range(6):
                        h = gI * 6 + m
                        p, e, pl = h // 2, h % 2, m // 2
                        hp = e * 64
                        kw, kwb = st["KW"][h]
                        k.mm(pss[hp:hp + 64, pl * 64:(pl + 1) * 64], kw[:, j, :], vu[:, pl, j, e, :], True, True, [kwb, vub], [pssb])
                    psy, psyb = k.nps()
                    for m in range(6):
                        h = gI * 6 + m
                        p, e, pl = h // 2, h % 2, m // 2
                        hp = e * 64
                        AR, ARb = st["AR"][p]
                        m1, m1b = M1[h]
                        k.mm(psy[0:64, m * 64:(m + 1) * 64], m1[:, j, 0:64], vu[:, pl, j, e, :], True, False, [m1b, vub], [psyb])
                        k.mm(psy[0:64, m * 64:(m + 1) * 64], AR[hp:hp + 64, j, 0:64], SB[hp:hp + 64, pl, :], False, True, [ARb, SBb], [psyb])
                    for pl in range(3):
                        p = gI * 3 + pl
                        k.stt(ST[:, pl, :], ST[:, pl, :], WC[:, p, j:j + 1], pss[:, pl * 64:(pl + 1) * 64], ALU.mult, ALU.add,
                              [STb_, WCb, pssb], [STb_])
                    k.cp("pool", SB[:], ST[:], [STb_], [SBb])
                    yb, ybb = Yb[j % 2]
                    k.cp("act", yb[0:64, gI * 384:(gI + 1) * 384], psy[0:64, 0:384], [psyb], [ybb], partial=(gI == 1))
                yb, ybb = Yb[j % 2]
                k.dma(d["Ysd"][cg * 64:(cg + 1) * 64, :], yb[0:64, :], ybb, reads=[ybb])
                pull(gen, 10)

        nb = NCH // 8
        loads(0)
        g0 = setup(0)
        pull(g0, 10 ** 6)
        for b in range(nb):
            gen = None
            if b + 1 < nb:
                loads(b + 1)
                gen = setup(b + 1)
            sequential(b, gen)
            pull(gen, 10 ** 6)
    S.flush()


def phaseF(k, d, g):
    S = k.S
    identF = g["identF"]
    NB = 4
    with contextlib.ExitStack() as es:
        GG = bcast_load(k, es, "gnG", d["gn_g"], 768)
        GB = bcast_load(k, es, "gnB", d["gn_b"], 768)
        NL = NB + 1
        ys = [k.sb(es, "y%d" % i, [128, 768], F32) for i in range(NL)]
        vs = [k.sb(es, "v%d" % i, [128, 768], BF16) for i in range(NL)]
        gs = [k.sb(es, "g%d" % i, [128, 768], F32) for i in range(NL)]
        bs = [k.sb(es, "b%d" % i, [128, 12], F32) for i in range(NL)]
        sqs = [k.sb(es, "sq%d" % i, [128, 768], F32) for i in range(NB)]
        t2s = [k.sb(es, "t2%d" % i, [128, 768], F32) for i in range(NB)]
        sts = [k.sb(es, "st%d" % i, [128, 4, 12], F32) for i in range(NB)]
        oTs = [k.sb(es, "oT%d" % i, [128, 6, 128], BF16) for i in range(NB)]
        OTv = d["OT"].rearrange("(c p) t -> p c t", p=128)
        v3 = lambda a: a.rearrange("p (h n) -> p h n", n=64)
        bc = lambda a: a.rearrange("p (h o) -> p h o", o=1).to_broadcast([128, 12, 64])

        def floads(t):
            tk = slice(t * 128, (t + 1) * 128)
            k.dma(ys[t % NL][0][:], d["Ysd"][tk, :], ys[t % NL][1], writes=[ys[t % NL][1]])
            k.dma(vs[t % NL][0][:], d["Vtd"][tk, :], vs[t % NL][1], writes=[vs[t % NL][1]])
            k.dma(gs[t % NL][0][:], d["Gtd"][tk, :], gs[t % NL][1], writes=[gs[t % NL][1]])
            k.dma(bs[t % NL][0][:], d["BONd"][tk, :], bs[t % NL][1], writes=[bs[t % NL][1]])

        def tile(t):
            tk = slice(t * 128, (t + 1) * 128)
            y, yb = ys[t % NL]
            v, vb = vs[t % NL]
            gt, gb = gs[t % NL]
            bn, bnb = bs[t % NL]
            sq, sqb = sqs[t % NB]
            t2, t2b = t2s[t % NB]
            stt_, sttb = sts[t % NB]
            oT, oTb = oTs[t % NB]
            if t + 1 < NT:
                floads(t + 1)
            yield
            k.S.op("dve", lambda e: e.tensor_reduce(out=stt_[:, 0, :], in_=v3(y[:]), axis=AX.X, op=ALU.add), [yb], [sttb])
            k.act(sq[:], y[:], AF.Square, [yb], [sqb])
            yield
            k.S.op("dve", lambda e: e.tensor_reduce(out=stt_[:, 1, :], in_=v3(sq[:]), axis=AX.X, op=ALU.add), [sqb], [sttb], partial=True)
            yield
            k.ts("dve", stt_[:, 0, :], stt_[:, 0, :], 1.0 / 64.0, None, ALU.mult, None, [sttb], [sttb])
            yield
            k.tt("dve", stt_[:, 2, :], stt_[:, 0, :], stt_[:, 0, :], ALU.mult, [sttb], [sttb])
            yield
            k.stt(stt_[:, 1, :], stt_[:, 1, :], 1.0 / 64.0, stt_[:, 2, :], ALU.mult, ALU.subtract, [sttb], [sttb])
            yield
            k.ts("dve", stt_[:, 1, :], stt_[:, 1, :], float(GN_EPS), None, ALU.add, None, [sttb], [sttb])
            yield
            k.act(stt_[:, 1, :], stt_[:, 1, :], AF.Sqrt, [sttb], [sttb])
            yield
            k.recip(stt_[:, 3, :], stt_[:, 1, :], [sttb], [sttb])
            k.tt("pool", v3(sq[:]), v3(v[:]), bc(bn[:]), ALU.mult, [vb, bnb], [sqb])
            yield
            k.stt(stt_[:, 2, :], stt_[:, 0, :], -1.0, stt_[:, 3, :], ALU.mult, ALU.mult, [sttb], [sttb])
            yield
            for h in range(12):
                k.act(t2[:, h * 64:(h + 1) * 64], y[:, h * 64:(h + 1) * 64], AF.Identity, [yb, sttb], [t2b],
                      bias=stt_[:, 2, h:h + 1], scale=stt_[:, 3, h:h + 1], partial=(h > 0))
            yield
            k.tt("dve", t2[:], t2[:], GG[0][:], ALU.mult, [t2b, GG[1]], [t2b])
            yield
            k.tt("pool", t2[:], t2[:], GB[0][:], ALU.add, [t2b, GB[1]], [t2b])
            yield
            k.tt("dve", t2[:], t2[:], sq[:], ALU.add, [t2b, sqb], [t2b])
            yield
            k.tt("pool", t2[:], t2[:], gt[:], ALU.mult, [t2b, gb], [t2b])
            yield
            for half in range(2):
                pt, ptb = k.nps()
                for j in range(3):
                    c = half * 3 + j
                    k.tp(pt[:, j * 128:(j + 1) * 128], t2[:, c * 128:(c + 1) * 128], identF[0][:], [t2b, identF[1]], [ptb], first=(j == 0))
                k.cp("act", oT[:, half * 3:half * 3 + 3, :], pt[:, 0:384].rearrange("p (c t) -> p c t", c=3), [ptb], [oTb],
                     partial=(half == 1))
                yield
            k.dma(OTv[:, 0:6, tk], oT[:], oTb, reads=[oTb])

        bufs = {
            "pt": [k.sb(es, "mpt%d" % i, [128, 512], BF16) for i in range(6)],
            "rec": [k.sb(es, "mrec%d" % i, [64, 512], F32) for i in range(2)],
            "osb": [k.sb(es, "mosb%d" % i, [64, 512], BF16) for i in range(2)],
        }
        qms = [k.sb(es, "mqm%d" % i, [128, T], BF16) for i in range(2)]
        mk, mkb = g["mem_kT"]
        mv, mvb = g["mem_va"]

        def memheads():
            st = {"s": 0, "p": 0, "o": 0, "r": 0, "bg": None}
            for pair in range(2):
                qm, qmb = qms[pair]
                k.dma(qm[:], d["QM"][pair * 128:(pair + 1) * 128, :], qmb, writes=[qmb])
            yield
            for pair in range(2):
                qm, qmb = qms[pair]
                for j2 in range(2):
                    j = pair * 2 + j2
                    b0 = j2 * 64
                    attn_head(k, g, bufs,
                              lambda kt, b0=b0, pair=pair: mk[b0:b0 + 64, pair, kt * 128:(kt + 1) * 128], mkb,
                              lambda c0, c1, b0=b0, qm=qm: qm[b0:b0 + 64, c0:c1], qmb,
                              lambda kt, j=j: mv[:, kt, j, :], mvb,
                              False, 64.0 ** -0.5, d["OT"][768 + j * 64:768 + (j + 1) * 64, :], st)
                    for _ in range(6):
                        yield

        floads(0)
        interleave([memheads(), interleave_gen((tile(t) for t in range(NT)), NB)], 2)
    S.flush()


def rwkv_layer(k, d, g, stage):
    phaseD(k, d, g)
    if stage == "pD":
        return
    phaseE(k, d, g)
    if stage == "pE":
        return
    phaseF(k, d, g)
    if stage == "pF":
        return
    phaseC(k, d, g, 1, d["X1"], d["out"], False)
```
